# Optimizing a Trainium2 kernel written in Bass

```python
import math
import jax, jax.numpy as jnp
from jax import lax
import numpy as np

D_MODEL = 1024
BATCH = 4
SEQ = 8192
DEPTH = 4

GRID_W = 64
NA_HEADS = 8
NA_HEAD_DIM = 64
NA_WIN_ROWS = 8
NA_WIN_COLS = 16
NA_W = NA_HEADS * NA_HEAD_DIM
RET_HEADS = 4
RET_QK_DIM = 128
RET_V_DIM = 256
RET_CHUNK = 128
RET_QK_W = RET_HEADS * RET_QK_DIM
RET_V_W = RET_HEADS * RET_V_DIM
ROPE_BASE = 10000.0
IN_SIZES = (NA_W, NA_W, NA_W, RET_QK_W, RET_QK_W, RET_V_W, RET_V_W, D_MODEL, D_MODEL)
IN_SPLITS = tuple(int(s) for s in np.cumsum(IN_SIZES)[:-1])
D_IN = int(sum(IN_SIZES))
D_FF = 2816
CONV_W = 3
PLE_DIM = 256
LN_EPS = 1e-5
GN_EPS = 1e-6
DEEPNORM_ALPHA = (2.0 * DEPTH) ** 0.25
DEEPNORM_BETA = (8.0 * DEPTH) ** -0.25

kernel_name = "hybrid_na2d_retention_deepnorm_encoder"


def layer_norm(x, g, b):
    xf = x.astype(jnp.float32)
    mu = jnp.mean(xf, axis=-1, keepdims=True)
    var = jnp.mean(jnp.square(xf - mu), axis=-1, keepdims=True)
    y = (xf - mu) * lax.rsqrt(var + LN_EPS) * g.astype(jnp.float32) + b.astype(jnp.float32)
    return y.astype(x.dtype)


def rotary(x, pos):
    half = x.shape[-1] // 2
    inv = ROPE_BASE ** (-jnp.arange(half, dtype=jnp.float32) / half)
    ang = pos[:, None] * inv[None, :]
    cos, sin = jnp.cos(ang), jnp.sin(ang)
    x1, x2 = x[..., :half].astype(jnp.float32), x[..., half:].astype(jnp.float32)
    return jnp.concatenate([x1 * cos - x2 * sin, x2 * cos + x1 * sin], axis=-1).astype(x.dtype)


def neighbourhood_attention(q, k, v, rpb):
    B, T, H, dh = q.shape
    rows = T // GRID_W
    wr = min(NA_WIN_ROWS, rows)
    scale = dh ** -0.5
    q_g = q.reshape(B, rows, GRID_W, H, dh).transpose(1, 0, 3, 2, 4)
    k_g = k.reshape(B, rows, GRID_W, H, dh).transpose(0, 3, 1, 2, 4)
    v_g = v.reshape(B, rows, GRID_W, H, dh).transpose(0, 3, 1, 2, 4)
    cols = np.arange(GRID_W)
    col_start = np.clip(cols - NA_WIN_COLS // 2, 0, GRID_W - NA_WIN_COLS)
    col_idx = col_start[:, None] + np.arange(NA_WIN_COLS)[None, :]
    col_off = col_idx - cols[:, None] + (NA_WIN_COLS - 1)

    def row_step(args):
        r, q_row = args
        rs = jnp.clip(r - wr // 2, 0, rows - wr)
        k_rows = lax.dynamic_slice_in_dim(k_g, rs, wr, axis=2)
        v_rows = lax.dynamic_slice_in_dim(v_g, rs, wr, axis=2)
        k_win = jnp.take(k_rows, col_idx, axis=3)
        v_win = jnp.take(v_rows, col_idx, axis=3)
        s = jnp.einsum('bhcd,bhrckd->bhcrk', q_row, k_win).astype(jnp.float32) * scale
        row_off = rs + jnp.arange(wr) - r + (NA_WIN_ROWS - 1)
        bias = jnp.take(rpb[:, row_off], col_off, axis=2)
        s = s + bias.transpose(0, 2, 1, 3)[None].astype(jnp.float32)
        pr = jax.nn.softmax(s.reshape(B, H, GRID_W, wr * NA_WIN_COLS), axis=-1)
        pr = pr.reshape(B, H, GRID_W, wr, NA_WIN_COLS).astype(v.dtype)
        return jnp.einsum('bhcrk,bhrckd->bhcd', pr, v_win)

    out = lax.map(row_step, (jnp.arange(rows), q_g))
    return out.transpose(1, 0, 3, 2, 4).reshape(B, T, H * dh)


def retention_dir(q, k, v, gamma, include_diag):
    B, H, T, dk = q.shape
    dv = v.shape[-1]
    C = RET_CHUNK
    n = T // C
    qc = q.reshape(B, H, n, C, dk)
    kc = k.reshape(B, H, n, C, dk)
    vc = v.reshape(B, H, n, C, dv)
    log_g = jnp.log(gamma)
    i = np.arange(C)
    diff = i[:, None] - i[None, :]
    mask = (diff >= 0) if include_diag else (diff > 0)
    dpos = np.maximum(diff, 0).astype(np.float32)
    decay_in = jnp.where(mask[None], jnp.exp(log_g[:, None, None] * dpos[None]), 0.0)
    s = jnp.einsum('bhnid,bhnjd->bhnij', qc, kc) * decay_in[None, :, None]
    inner = jnp.einsum('bhnij,bhnje->bhnie', s, vc)
    zeta = jnp.exp(log_g[:, None] * (C - 1 - i).astype(np.float32))
    xi = jnp.exp(log_g[:, None] * (i + 1).astype(np.float32))
    chunk_decay = jnp.exp(log_g * C)
    kv = jnp.einsum('bhnjd,bhnje->nbhde', kc * zeta[None, :, None, :, None], vc)

    def body(S, kv_c):
        return S * chunk_decay[None, :, None, None] + kv_c, S

    S0 = jnp.zeros((B, H, dk, dv), kv.dtype)
    _, S_prev = lax.scan(body, S0, kv)
    cross = jnp.einsum('bhnid,nbhde->bhnie', qc, S_prev) * xi[None, :, None, :, None]
    return (inner + cross).reshape(B, H, T, dv)


def bidirectional_retention(q, k, v, g, decay_f, decay_b):
    B, T, H, dk = q.shape
    pos = jnp.arange(T, dtype=jnp.float32)
    qh = rotary(q.transpose(0, 2, 1, 3), pos)
    kh = rotary(k.transpose(0, 2, 1, 3), pos) * (dk ** -0.5)
    vh = v.reshape(B, T, H, RET_V_DIM).transpose(0, 2, 1, 3)
    gam_f = jax.nn.sigmoid(decay_f.astype(jnp.float32))
    gam_b = jax.nn.sigmoid(decay_b.astype(jnp.float32))
    fwd = retention_dir(qh, kh, vh, gam_f, True)
    bwd = jnp.flip(retention_dir(jnp.flip(qh, 2), jnp.flip(kh, 2), jnp.flip(vh, 2), gam_b, False), 2)
    o = (fwd + bwd).astype(jnp.float32)
    mu = jnp.mean(o, axis=-1, keepdims=True)
    var = jnp.mean(jnp.square(o - mu), axis=-1, keepdims=True)
    o = ((o - mu) * lax.rsqrt(var + GN_EPS)).astype(v.dtype)
    o = o.transpose(0, 2, 1, 3).reshape(B, T, H * RET_V_DIM)
    return jax.nn.silu(g) * o


def conv_glu_ffn(x, w_up, conv_w, conv_b, w_down):
    h = x @ w_up
    hp = jnp.pad(h, ((0, 0), (1, 1), (0, 0)))
    h = hp[:, :-2] * conv_w[0] + hp[:, 1:-1] * conv_w[1] + hp[:, 2:] * conv_w[2] + conv_b
    a, b = jnp.split(h, 2, axis=-1)
    return (jax.nn.gelu(a) * b) @ w_down


def setup_inputs(seed: int = 0) -> dict:
    key = jax.random.key(seed)
    ks = jax.random.split(key, 20)
    f32 = jnp.float32
    nrm = lambda k, shape: jax.random.normal(k, shape, f32)
    beta = DEEPNORM_BETA
    col_scale = np.ones((D_IN,), np.float32)
    col_scale[2 * NA_W:3 * NA_W] = beta
    v0 = 3 * NA_W + 2 * RET_QK_W
    col_scale[v0:v0 + RET_V_W] = beta
    gam = 1.0 - 2.0 ** (-5.0 - np.arange(RET_HEADS))
    base_logit = jnp.asarray(np.log(gam / (1.0 - gam)), f32)
    return {
        "x": nrm(ks[0], (BATCH, SEQ, D_MODEL)),
        "p": nrm(ks[1], (DEPTH, BATCH, SEQ, PLE_DIM)),
        "w_in": nrm(ks[2], (DEPTH, D_MODEL, D_IN)) * (D_MODEL ** -0.5) * jnp.asarray(col_scale),
        "na_rpb": 0.02 * nrm(ks[3], (DEPTH, NA_HEADS, 2 * NA_WIN_ROWS - 1, 2 * NA_WIN_COLS - 1)),
        "ret_decay_f": base_logit + 0.01 * nrm(ks[4], (DEPTH, RET_HEADS)),
        "ret_decay_b": base_logit + 0.01 * nrm(ks[5], (DEPTH, RET_HEADS)),
        "w_branch_a": nrm(ks[6], (DEPTH, NA_W, D_MODEL)) * (NA_W ** -0.5) * beta,
        "w_branch_b": nrm(ks[7], (DEPTH, RET_V_W, D_MODEL)) * (RET_V_W ** -0.5) * beta,
        "w_out": nrm(ks[8], (DEPTH, D_MODEL, D_MODEL)) * (D_MODEL ** -0.5) * beta,
        "ln1_g": 1.0 + 0.02 * nrm(ks[9], (DEPTH, D_MODEL)),
        "ln1_b": 0.02 * nrm(ks[10], (DEPTH, D_MODEL)),
        "w_up": nrm(ks[11], (DEPTH, D_MODEL, 2 * D_FF)) * (D_MODEL ** -0.5) * beta,
        "conv_w": nrm(ks[12], (DEPTH, CONV_W, 2 * D_FF)) * (CONV_W ** -0.5),
        "conv_b": 0.02 * nrm(ks[13], (DEPTH, 2 * D_FF)),
        "w_down": nrm(ks[14], (DEPTH, D_FF, D_MODEL)) * (D_FF ** -0.5) * beta,
        "w_ple_gate": nrm(ks[15], (DEPTH, D_MODEL, D_MODEL)) * (D_MODEL ** -0.5),
        "w_ple_proj": nrm(ks[16], (DEPTH, PLE_DIM, D_MODEL)) * (PLE_DIM ** -0.5) * beta,
        "ln2_g": 1.0 + 0.02 * nrm(ks[17], (DEPTH, D_MODEL)),
        "ln2_b": 0.02 * nrm(ks[18], (DEPTH, D_MODEL)),
    }


def reference(x, p, w_in, na_rpb, ret_decay_f, ret_decay_b, w_branch_a, w_branch_b, w_out,
              ln1_g, ln1_b, w_up, conv_w, conv_b, w_down, w_ple_gate, w_ple_proj, ln2_g, ln2_b):
    B, T, _ = x.shape
    for i in range(DEPTH):
        proj = x @ w_in[i]
        q_na, k_na, v_na, q_r, k_r, v_r, g_r, gate_a, gate_b = jnp.split(proj, IN_SPLITS, axis=-1)
        ya = neighbourhood_attention(q_na.reshape(B, T, NA_HEADS, NA_HEAD_DIM),
                                     k_na.reshape(B, T, NA_HEADS, NA_HEAD_DIM),
                                     v_na.reshape(B, T, NA_HEADS, NA_HEAD_DIM), na_rpb[i])
        yb = bidirectional_retention(q_r.reshape(B, T, RET_HEADS, RET_QK_DIM),
                                     k_r.reshape(B, T, RET_HEADS, RET_QK_DIM),
                                     v_r, g_r, ret_decay_f[i], ret_decay_b[i])
        merged = jax.nn.sigmoid(gate_a) * (ya @ w_branch_a[i]) + jax.nn.sigmoid(gate_b) * (yb @ w_branch_b[i])
        y = merged @ w_out[i]
        x = layer_norm(DEEPNORM_ALPHA * x + y, ln1_g[i], ln1_b[i])
        ffn = conv_glu_ffn(x, w_up[i], conv_w[i], conv_b[i], w_down[i])
        ple = jax.nn.sigmoid(x @ w_ple_gate[i]) * (p[i] @ w_ple_proj[i])
        x = layer_norm(DEEPNORM_ALPHA * x + ffn + ple, ln2_g[i], ln2_b[i])
    return x
```

```python
import contextlib
import numpy as np
import ml_dtypes
import concourse.bass as bass
import concourse.mybir as mybir
from concourse.bass_utils import run_bass_kernel_spmd

F32 = mybir.dt.float32
BF16 = mybir.dt.bfloat16
AF = mybir.ActivationFunctionType
ALU = mybir.AluOpType

D = 1024
NT = 4096
NCH = 32
DEPTH = 4
DIN = 6656
DFF = 2816
PLE = 256
ALPHA = (2.0 * DEPTH) ** 0.25
LN_EPS = 1e-5
GN_EPS = 1e-6
NEG = -30000.0
PAIRS = [[0, 1], [2, 3], [4, 5], [6, 7]]


def na_superset(r):
    if r <= 3:
        return list(range(r - 4, 8))
    if r >= 61:
        return list(range(56, r + 4))
    return list(range(r - 4, r + 4))


def na_valid(parity, r, kr):
    R = r + 64 * parity
    rs = min(max(R - 4, 0), 120)
    KR = kr + 64 * parity
    return rs <= KR <= rs + 7


NA_BND = (0, 1, 2, 3, 61, 62, 63)
NA_QR = {kr: [r for r in range(64) if kr in na_superset(r)] for kr in range(-4, 67)}
NA_SLOT = {}
for _kr in range(-4, 67):
    for _r in NA_QR[_kr]:
        if _r in NA_BND:
            NA_SLOT[(_kr, _r)] = len(NA_SLOT)
NA_NSLOT = len(NA_SLOT)


class Sem:
    def __init__(self, kb, name):
        self.h = kb.es.enter_context(kb.nc.semaphore(name))
        self.n = 0


class KB:
    def __init__(self):
        self.nc = bass.Bass("TRN2", target_bir_lowering=False)
        self.es = contextlib.ExitStack()
        self.cur = self.es
        self.sems = {}
        nc = self.nc
        self.eng = {"pe": nc.tensor, "act": nc.scalar, "dve": nc.vector, "pool": nc.gpsimd, "sp": nc.sync}
        self.prog = {e: Sem(self, "prog_" + e) for e in self.eng}
        self.waited = {}
        self.nsem = len(self.eng)
        self.dram = {}

    def uniq(self, name):
        self.ucnt = getattr(self, "ucnt", 0) + 1
        return "%s_u%d" % (name, self.ucnt)

    def sb(self, name, shape, dt=F32):
        return self.cur.enter_context(self.nc.sbuf_tensor(self.uniq(name), list(shape), dt))

    def ps(self, name, shape, dt=F32):
        return self.cur.enter_context(self.nc.psum_tensor(self.uniq(name), list(shape), dt))

    def sem(self, name):
        if name not in self.sems:
            self.nsem += 1
            self.sems[name] = Sem(self, name)
        return self.sems[name]

    @contextlib.contextmanager
    def scope(self):
        old = self.cur
        with contextlib.ExitStack() as st:
            self.cur = st
            yield
        self.cur = old

    def din(self, name, shape, dt=F32):
        t = self.nc.dram_tensor(name, list(shape), dt, kind="ExternalInput")
        self.dram[name] = t
        return t

    def dout(self, name, shape, dt=F32):
        t = self.nc.dram_tensor(name, list(shape), dt, kind="ExternalOutput")
        self.dram[name] = t
        return t

    def dscr(self, name, shape, dt=BF16):
        t = self.nc.dram_tensor(name, list(shape), dt)
        self.dram[name] = t
        return t

    def mark(self, e, ins):
        s = self.prog[e]
        ins.then_inc(s.h, 1)
        s.n += 1
        return (s, s.n)

    def wait(self, e, tok, force=False):
        if tok is None:
            return
        s, v = tok
        if v <= 0:
            return
        if s is self.prog[e] and not force:
            return
        key = (e, id(s))
        if self.waited.get(key, 0) >= v:
            return
        self.waited[key] = v
        self.eng[e].wait_ge(s.h, v)

    def dma(self, e, out, in_, sem, **kw):
        ins = self.eng[e].dma_start(out=out, in_=in_, **kw)
        ins.then_inc(sem.h, 16)
        sem.n += 16
        return (sem, sem.n)


class Ring:
    def __init__(self, bufs):
        self.bufs = bufs
        self.n = len(bufs)
        self.k = 0
        self.free = [[] for _ in bufs]

    def next(self):
        i = self.k % self.n
        self.k += 1
        fr = self.free[i]
        self.free[i] = []
        return i, self.bufs[i], fr


def build_program(n_layers=DEPTH, debug=None, pairs=PAIRS):
    kb = KB()
    nc = kb.nc
    debug = debug or {}
    stop = debug.get("stop")

    x_in = kb.din("x", [NT, D])
    p_in = kb.din("p", [DEPTH, NT, PLE])
    w_in = kb.din("w_in", [DEPTH, D, DIN])
    rope = kb.din("rope", [4, 128, NT])
    ident_d = kb.din("ident", [128, 128])
    w_ba = kb.din("w_branch_a", [DEPTH, 512, D])
    w_bb = kb.din("w_branch_b", [DEPTH, 1024, D])
    w_out = kb.din("w_out", [DEPTH, D, D])
    ln1_g = kb.din("ln1_g", [DEPTH, D])
    ln1_b = kb.din("ln1_b", [DEPTH, D])
    ln2_g = kb.din("ln2_g", [DEPTH, D])
    ln2_b = kb.din("ln2_b", [DEPTH, D])
    w_up = kb.din("w_up", [DEPTH, D, 2 * DFF])
    convp = kb.din("convp", [DEPTH, 128, 44, 4])
    w_down = kb.din("w_down", [DEPTH, DFF, D])
    w_pg = kb.din("w_ple_gate", [DEPTH, D, D])
    w_pp = kb.din("w_ple_proj", [DEPTH, PLE, D])
    dec_f = kb.din("ret_decay_f", [DEPTH, 4])
    dec_b = kb.din("ret_decay_b", [DEPTH, 4])
    rconst = kb.din("rconst", [128, 4 * 128 + 2 + 32])
    par_d = kb.din("par", [128, 2])
    na_bi = kb.din("na_bi", [DEPTH, 4, 128, 15 * 64])
    na_bb = kb.din("na_bb", [DEPTH, 4, 128, NA_NSLOT * 64])
    out_d = kb.dout("out", [NT, D])

    XT_D = kb.dscr("XT_D", [8, 128, NT])
    XA = kb.dscr("XA", [NT, D], F32)
    QNA_T = kb.dscr("QNA_T", [4, 128, NT])
    KNA_T = kb.dscr("KNA_T", [4, 128, NT])
    VNA = kb.dscr("VNA", [NT, 512])
    QR_T = kb.dscr("QR_T", [4, 128, NT])
    KR_T = kb.dscr("KR_T", [4, 128, NT])
    VR = kb.dscr("VR", [NT, 1024])
    SG = kb.dscr("SG", [NT, 1024])
    GA_T = kb.dscr("GA_T", [8, 128, NT])
    GB_T = kb.dscr("GB_T", [8, 128, NT])
    NAB = kb.dscr("NAB", [4, 131072])
    NAG = kb.dscr("NAG", [8, 131072])
    YA_T = kb.dscr("YA_T", [4, 128, NT])
    YB_T = kb.dscr("YB_T", [8, 128, NT])
    X1 = kb.dscr("X1", [NT, D], F32)
    X1T_D = kb.dscr("X1T_D", [8, 128, NT])
    U_T = kb.dscr("U_T", [22, 128, NT])
    CXB = kb.dscr("CXB", [2, 1024])
    CXG = kb.dscr("CXG", [4, 1024])
    RSB = kb.dscr("RSB", [2, 131072], F32)
    RSG = kb.dscr("RSG", [4, 131072], F32)

    ident_f = kb.sb("ident_f", [128, 128], F32)
    ident_b = kb.sb("ident_b", [128, 128], BF16)
    psb = [kb.ps("psb%d" % i, [128, 512], F32) for i in range(6)]
    ptb = [kb.ps("ptb%d" % i, [128, 1024], BF16) for i in range(2)]

    s_misc = kb.sem("misc")
    t = kb.dma("sp", ident_f[:, :], ident_d.ap()[:, :], s_misc)
    kb.wait("dve", t)
    tok_ident = kb.mark("dve", nc.vector.tensor_copy(out=ident_b[:, :], in_=ident_f[:, :]))

    def barrier(toks):
        for e in kb.eng:
            for tk in toks:
                kb.wait(e, tk)

    pend = []

    def transpose_chunk(src_bf, src_tok, dst_ap, pbank, pbank_free, evac_e):
        kb.wait("pe", src_tok)
        for tk in pbank_free:
            kb.wait("pe", tk)
        kb.wait("pe", tok_ident)
        pv = pbank
        ins = None
        for kc in range(8):
            ins = nc.tensor.transpose(pv[:, kc * 128:(kc + 1) * 128], src_bf[:, kc * 128:(kc + 1) * 128], ident_b[:, :])
        pt = kb.mark("pe", ins)
        kb.wait(evac_e, pt)
        src = pv[:, 0:1024].rearrange("p (k t) -> p k t", k=8)
        if evac_e == "act":
            ins = nc.scalar.copy(out=dst_ap, in_=src)
        else:
            ins = nc.vector.tensor_copy(out=dst_ap, in_=src)
        et = kb.mark(evac_e, ins)
        return pt, et

    def phase_p0():
        xb_ring = Ring([kb.sb("p0_xb%d" % i, [128, D], BF16) for i in range(2)])
        xb_sem = [kb.sem("p0_xbs%d" % i) for i in range(2)]
        xt_ring = Ring([kb.sb("p0_xt%d" % i, [128, 8, 128], BF16) for i in range(2)])
        xt_sem = [kb.sem("p0_xts%d" % i) for i in range(2)]
        pfree = [[], []]
        toks = []
        s_xa = kb.sem("p0_xa")
        for q in range(4):
            toks.append(kb.dma("sp", XA.ap()[q * 1024:(q + 1) * 1024, :], x_in.ap()[q * 1024:(q + 1) * 1024, :], s_xa))
        toks = toks[-1:]
        for c in range(NCH):
            i, xb, fr = xb_ring.next()
            for tk in fr:
                kb.wait("pool", tk)
            lt = kb.dma("pool", xb[:, :], x_in.ap()[c * 128:(c + 1) * 128, :], xb_sem[i])
            j, xt, fr2 = xt_ring.next()
            e = "act" if c % 2 == 0 else "dve"
            for tk in fr2:
                kb.wait(e, tk)
            pt, et = transpose_chunk(xb, lt, xt[:, :, :], ptb[c % 2], pfree[c % 2], e)
            pfree[c % 2] = [et]
            xb_ring.free[i] = [pt]
            kb.wait("sp", et)
            st = kb.dma("sp", XT_D.ap()[:, :, c * 128:(c + 1) * 128].rearrange("k p t -> p k t"), xt[:, :, :], xt_sem[j])
            xt_ring.free[j] = [st]
            toks.append(st)
        return toks

    shared = {}

    def na_exchange(after):
        s_cc = kb.sem("cc")
        s_ex = kb.sem("na_ex")
        for tk in after:
            kb.wait("sp", tk)
        kb.dma("sp", NAB.ap()[0, :].rearrange("(a p t) -> a p t", a=4, p=128), KNA_T.ap()[:, :, 0:256], s_ex)
        kb.dma("sp", NAB.ap()[1, :].rearrange("(a p t) -> a p t", a=4, p=128), KNA_T.ap()[:, :, NT - 256:NT], s_ex)
        kb.dma("sp", NAB.ap()[2, :].rearrange("(t c) -> t c", c=512), VNA.ap()[0:256, :], s_ex)
        t = kb.dma("sp", NAB.ap()[3, :].rearrange("(t c) -> t c", c=512), VNA.ap()[NT - 256:NT, :], s_ex)
        kb.wait("pool", t)
        ins = nc.gpsimd.collective_compute("AllGather", ALU.bypass, replica_groups=pairs,
                                           ins=[NAB.ap().opt()], outs=[NAG.ap().opt()])
        ins.then_inc(s_cc.h)
        s_cc.n += 1
        return (s_cc, s_cc.n)

    def phase_p1a(l):
        XT = kb.sb("p1a_XT", [128, 8, NT], BF16)
        s_xt = kb.sem("p1a_xt")
        lt = None
        for kc in range(8):
            lt = kb.dma("sp", XT[:, kc, :], XT_D.ap()[kc, :, :], s_xt)
        xt_tok = lt
        wg_ring = Ring([kb.sb("p1a_wg%d" % i, [128, 8, 512], BF16) for i in range(2)])
        wg_sem = [kb.sem("p1a_wgs%d" % i) for i in range(2)]
        rp_ring = Ring([kb.sb("p1a_rp%d" % i, [128, 2, 512], F32) for i in range(2)])
        rp_sem = [kb.sem("p1a_rps%d" % i) for i in range(2)]
        st_ring = Ring([kb.sb("p1a_st%d" % i, [128, 512], BF16) for i in range(4)])
        st_sem = [kb.sem("p1a_sts%d" % i) for i in range(4)]
        tmp_ring = Ring([kb.sb("p1a_tmp%d" % i, [128, 2, 512], F32) for i in range(2)])
        ps_ring = Ring(psb[0:4])
        out_toks = []
        ev_alt = [0]

        def load_w(g):
            i, wg, fr = wg_ring.next()
            for tk in fr:
                kb.wait("pool", tk)
            src = w_in.ap()[l, :, g * 512:(g + 1) * 512].rearrange("(k p) c -> p k c", p=128)
            return wg, kb.dma("pool", wg[:, :, :], src, wg_sem[i]), i

        groups = list(range(13))
        nxt = load_w(groups[0])
        for gi, g in enumerate(groups):
            wg, wtok, wi = nxt
            if gi + 1 < len(groups):
                nxt = load_w(groups[gi + 1])
            fm = g in (0, 1, 3, 4, 9, 10, 11, 12)
            last_pe = None
            if g == 5 and stop != "p1a":
                shared["na_cc"] = na_exchange([tk for fr in st_ring.free for tk in fr])
            for tile in range(32):
                if fm:
                    if g in (3, 4):
                        tt, fb = tile // 4, tile % 4
                    else:
                        fb, tt = tile // 8, tile % 8
                if g in (3, 4) and fb == 0:
                    def ld_rope(tt_):
                        ri, rp, fr = rp_ring.next()
                        for tk in fr:
                            kb.wait("sp", tk)
                        base = 0 if g == 3 else 2
                        rtok = kb.dma("sp", rp[:, :, :], rope.ap()[base:base + 2, :, tt_ * 512:(tt_ + 1) * 512].rearrange("a p t -> p a t"), rp_sem[ri])
                        return (rp, rtok, ri)
                    if tt == 0:
                        rp_nxt = ld_rope(0)
                    rp_cur = rp_nxt
                    if tt + 1 < 8:
                        rp_nxt = ld_rope(tt + 1)
                pi, pb, pfr = ps_ring.next()
                for tk in pfr:
                    kb.wait("pe", tk)
                kb.wait("pe", wtok)
                kb.wait("pe", xt_tok)
                ins = None
                for kc in range(8):
                    if fm:
                        ins = nc.tensor.matmul(pb[:, :], lhsT=wg[:, kc, fb * 128:(fb + 1) * 128], rhs=XT[:, kc, tt * 512:(tt + 1) * 512],
                                               start=(kc == 0), stop=(kc == 7))
                    else:
                        ins = nc.tensor.matmul(pb[:, :], lhsT=XT[:, kc, tile * 128:(tile + 1) * 128], rhs=wg[:, kc, :],
                                               start=(kc == 0), stop=(kc == 7))
                ptok = kb.mark("pe", ins)
                last_pe = ptok
                si, stg, sfr = st_ring.next()
                if g in (3, 4):
                    rp, rtok, ri = rp_cur
                    ti, tmp, tfr = tmp_ring.next()
                    kb.wait("dve", ptok)
                    kb.wait("dve", rtok)
                    for tk in tfr:
                        kb.wait("dve", tk)
                    nc.vector.tensor_tensor(out=tmp[:, 0, :], in0=pb[:, :], in1=rp[:, 0, :], op=ALU.mult)
                    nc.vector.tensor_tensor(out=tmp[0:64, 1, :], in0=pb[64:128, :], in1=rp[64:128, 1, :], op=ALU.mult)
                    ins = nc.vector.tensor_tensor(out=tmp[64:128, 1, :], in0=pb[0:64, :], in1=rp[0:64, 1, :], op=ALU.mult)
                    dtok = kb.mark("dve", ins)
                    ps_ring.free[pi] = [dtok]
                    if fb == 3:
                        rp_ring.free[ri] = [dtok]
                    kb.wait("pool", dtok)
                    for tk in sfr:
                        kb.wait("pool", tk)
                    ins = nc.gpsimd.tensor_tensor(out=stg[:, :], in0=tmp[:, 0, :], in1=tmp[:, 1, :], op=ALU.add)
                    etok = kb.mark("pool", ins)
                    tmp_ring.free[ti] = [etok]
                else:
                    if g in (7, 8, 9, 10, 11, 12, 0):
                        e = "act"
                    elif g in (5, 6):
                        e = "act" if tile % 2 == 0 else "dve"
                    else:
                        e = "dve"
                    kb.wait(e, ptok)
                    for tk in sfr:
                        kb.wait(e, tk)
                    if e == "act":
                        if g == 0:
                            ins = nc.scalar.mul(out=stg[:, :], in_=pb[:, :], mul=0.125)
                        elif g in (7, 8):
                            ins = nc.scalar.activation(out=stg[:, :], in_=pb[:, :], func=AF.Silu)
                        elif g in (9, 10, 11, 12):
                            ins = nc.scalar.activation(out=stg[:, :], in_=pb[:, :], func=AF.Sigmoid)
                        else:
                            ins = nc.scalar.copy(out=stg[:, :], in_=pb[:, :])
                    else:
                        ins = nc.vector.tensor_copy(out=stg[:, :], in_=pb[:, :])
                    etok = kb.mark(e, ins)
                    ps_ring.free[pi] = [etok]
                if g == 0:
                    dst = QNA_T.ap()[fb, :, tt * 512:(tt + 1) * 512]
                elif g == 1:
                    dst = KNA_T.ap()[fb, :, tt * 512:(tt + 1) * 512]
                elif g == 2:
                    dst = VNA.ap()[tile * 128:(tile + 1) * 128, :]
                elif g == 3:
                    dst = QR_T.ap()[fb, :, tt * 512:(tt + 1) * 512]
                elif g == 4:
                    dst = KR_T.ap()[fb, :, tt * 512:(tt + 1) * 512]
                elif g in (5, 6):
                    dst = VR.ap()[tile * 128:(tile + 1) * 128, (g - 5) * 512:(g - 4) * 512]
                elif g in (7, 8):
                    dst = SG.ap()[tile * 128:(tile + 1) * 128, (g - 7) * 512:(g - 6) * 512]
                elif g in (9, 10):
                    dst = GA_T.ap()[(g - 9) * 4 + fb, :, tt * 512:(tt + 1) * 512]
                else:
                    dst = GB_T.ap()[(g - 11) * 4 + fb, :, tt * 512:(tt + 1) * 512]
                kb.wait("sp", etok)
                stok = kb.dma("sp", dst, stg[:, :], st_sem[si])
                st_ring.free[si] = [stok]
            wg_ring.free[wi] = [last_pe]
        for si in range(4):
            out_toks += st_ring.free[si]
        return out_toks


    def phase_na(l):
        if "na_cc" in shared:
            cc_tok = shared.pop("na_cc")
        else:
            cc_tok = na_exchange([])
        kb.wait("sp", cc_tok)
        if stop == "na_ex":
            return [cc_tok]

        NK = 71
        bi = kb.sb("na_bi", [128, 4, 15 * 64], F32)
        s_bi = kb.sem("na_bis")
        for hp in range(4):
            bi_tok = kb.dma("sp", bi[:, hp, :], na_bi.ap()[l, hp, :, :], s_bi)
        bb_ring = [kb.sb("na_bb%d" % i, [128, NA_NSLOT * 64], F32) for i in range(2)]
        kt_ring = [kb.sb("na_kt%d" % i, [128, 72, 128], BF16) for i in range(2)]
        vt_ring = [kb.sb("na_vt%d" % i, [128, 72, 130], BF16) for i in range(2)]
        qt_ring = [kb.sb("na_qt%d" % i, [128, NT], BF16) for i in range(2)]
        ld_sem = [kb.sem("na_lds%d" % i) for i in range(2)]
        ones_tok = None
        for i in range(2):
            nc.gpsimd.memset(kt_ring[i][:, :, :], 0.0)
            zt = kb.mark("pool", nc.gpsimd.memset(vt_ring[i][:, :, :], 0.0))
            kb.wait("pool", zt, force=True)
            nc.gpsimd.memset(vt_ring[i][0:64, :, 64:65], 1.0)
            ones_tok = kb.mark("pool", nc.gpsimd.memset(vt_ring[i][64:128, :, 129:130], 1.0))
        z_ring = Ring([kb.sb("na_z%d" % i, [128, 768], F32) for i in range(2)])
        pt_ring = [kb.sb("na_pt%d" % i, [128, 768], BF16) for i in range(16)]
        pt_free = [[] for _ in range(16)]
        rs_ring = Ring([kb.sb("na_rs%d" % i, [64, 6], F32) for i in range(2)])
        ya_ring = Ring([kb.sb("na_ya%d" % i, [64, 6, 64], BF16) for i in range(2)])
        stg_ring = Ring([kb.sb("na_stg%d" % i, [128, 192], BF16) for i in range(3)])
        stg_sem = [kb.sem("na_stgs%d" % i) for i in range(3)]
        s_tiles = [kb.ps("na_s%d" % i, [128, 1024], F32) for i in range(2)] if False else None
        s_ring = Ring([(psb[0], psb[1]), (psb[2], psb[3])])
        acc_ring = Ring([psb[4], psb[5]])
        tp_ring = Ring(ptb)

        def load_hp(hp, slot, free_toks):
            for tk in free_toks:
                kb.wait("sp", tk)
            kt, vt, qt, bb, sm = kt_ring[slot], vt_ring[slot], qt_ring[slot], bb_ring[slot], ld_sem[slot]
            kb.wait("sp", ones_tok)
            kb.dma("sp", qt[:, :], QNA_T.ap()[hp, :, :], sm)
            kb.dma("sp", bb[:, :], na_bb.ap()[l, hp, :, :], sm)
            ktop = NAG.ap()[1, :].rearrange("(a p t) -> a p t", a=4, p=128)[hp]
            kbot = NAG.ap()[4, :].rearrange("(a p t) -> a p t", a=4, p=128)[hp]
            vtop = NAG.ap()[3, :].rearrange("(t c) -> t c", c=512)
            vbot = NAG.ap()[6, :].rearrange("(t c) -> t c", c=512)
            for h in range(2):
                ps_ = slice(h * 64, (h + 1) * 64)
                kc_ = slice(h * 64, (h + 1) * 64)
                ksrc = KNA_T.ap()[hp, ps_, :].rearrange("p (r c) -> p r c", c=64)
                for r16 in range(4):
                    kb.dma("sp", kt[ps_, 4 + r16 * 16:20 + r16 * 16, kc_], ksrc[:, r16 * 16:(r16 + 1) * 16, :], sm)
                kb.dma("sp", kt[ps_, 0:4, kc_], ktop[ps_, :].rearrange("p (r c) -> p r c", c=64), sm)
                kb.dma("sp", kt[ps_, 68:72, kc_], kbot[ps_, :].rearrange("p (r c) -> p r c", c=64), sm)
                c0 = hp * 128 + h * 64
                vc_ = slice(h * 65, h * 65 + 64)
                vsrc = VNA.ap()[:, c0:c0 + 64].rearrange("(r c) d -> c r d", c=64)
                for r16 in range(4):
                    kb.dma("sp", vt[ps_, 4 + r16 * 16:20 + r16 * 16, vc_], vsrc[:, r16 * 16:(r16 + 1) * 16, :], sm)
                kb.dma("sp", vt[ps_, 0:4, vc_], vtop[:, c0:c0 + 64].rearrange("(r c) d -> c r d", c=64), sm)
                tk = kb.dma("sp", vt[ps_, 68:72, vc_], vbot[:, c0:c0 + 64].rearrange("(r c) d -> c r d", c=64), sm)
            return tk

        out_toks = []
        hp_free = [[], []]
        ld_tok = [None, None]
        ld_tok[0] = load_hp(0, 0, [])
        for hp in range(4):
            slot = hp % 2
            if hp + 1 < 4:
                ld_tok[1 - slot] = load_hp(hp + 1, 1 - slot, hp_free[1 - slot])
            kt, vt, qt, bb = kt_ring[slot], vt_ring[slot], qt_ring[slot], bb_ring[slot]
            ltok = ld_tok[slot]
            exp_tok = {}
            last_pv_pe = None
            last_dve_read = None

            def qk(kidx):
                kr = kidx - 4
                rows = NA_QR[kr]
                qlo, qhi = rows[0], rows[-1]
                nq = (qhi - qlo + 1) * 64
                si, (b0, b1), sfr = s_ring.next()
                for tk in sfr:
                    kb.wait("pe", tk)
                kb.wait("pe", ltok)
                ins = None
                for seg, bank in ((0, b0), (1, b1)):
                    c0 = seg * 512
                    if c0 >= nq:
                        continue
                    w = min(512, nq - c0)
                    ins = nc.tensor.matmul(bank[:, 0:w], lhsT=kt[:, kidx, :],
                                           rhs=qt[:, qlo * 64 + c0:qlo * 64 + c0 + w], start=True, stop=True)
                ptok = kb.mark("pe", ins)
                zi, z, zfr = z_ring.next()
                kb.wait("dve", ptok)
                kb.wait("dve", ltok)
                kb.wait("dve", bi_tok)
                for tk in zfr:
                    kb.wait("dve", tk)
                parts = []
                for r in rows:
                    kind = "b" if r in NA_BND else "i"
                    if parts and parts[-1][0] == kind:
                        parts[-1][2] = r
                    else:
                        parts.append([kind, r, r])
                ins = None
                for kind, r0, r1 in parts:
                    a0, a1 = (r0 - qlo) * 64, (r1 - qlo + 1) * 64
                    pieces = []
                    if a0 < 512 < a1:
                        pieces = [(a0, 512), (512, a1)]
                    else:
                        pieces = [(a0, a1)]
                    for (c0, c1) in pieces:
                        bank = b0 if c0 < 512 else b1
                        off = 0 if c0 < 512 else 512
                        rr = qlo + c0 // 64
                        if kind == "i":
                            t0 = (7 + rr - kr) * 64
                            tab = bi[:, hp, t0:t0 + (c1 - c0)]
                        else:
                            t0 = NA_SLOT[(kr, rr)] * 64
                            tab = bb[:, t0:t0 + (c1 - c0)]
                        ins = nc.vector.tensor_tensor(out=z[:, c0:c1], in0=bank[:, c0 - off:c1 - off], in1=tab, op=ALU.add)
                dtok = kb.mark("dve", ins)
                s_ring.free[si] = [dtok]
                pslot = kidx % 16
                kb.wait("act", dtok)
                for tk in pt_free[pslot]:
                    kb.wait("act", tk)
                pt_free[pslot] = []
                ins = nc.scalar.activation(out=pt_ring[pslot][:, 0:nq], in_=z[:, 0:nq], func=AF.Exp)
                etok = kb.mark("act", ins)
                z_ring.free[zi] = [etok]
                exp_tok[kidx] = etok
                return dtok

            def pv_group(rows3):
                ai, acc, afr = acc_ring.next()
                for tk in afr:
                    kb.wait("pe", tk)
                ins = None
                for j, r in enumerate(rows3):
                    ks = na_superset(r)
                    kb.wait("pe", exp_tok[ks[-1] + 4])
                    for n_, kr in enumerate(ks):
                        kidx = kr + 4
                        qlo = NA_QR[kr][0]
                        c0 = (r - qlo) * 64
                        ins = nc.tensor.matmul(acc[0:64, j * 130:(j + 1) * 130],
                                               lhsT=pt_ring[kidx % 16][:, c0:c0 + 64], rhs=vt[:, kidx, :],
                                               start=(n_ == 0 and j == 0), stop=(n_ == len(ks) - 1), skip_group_check=True)
                ptok = kb.mark("pe", ins)
                for r in rows3:
                    for kr in na_superset(r):
                        pt_free[(kr + 4) % 16] = [ptok]
                ng = len(rows3)

                def fin():
                    return pv_fin(rows3, ai, acc, ptok, ng)
                return ptok, fin

            def pv_fin(rows3, ai, acc, ptok, ng):
                ri, rs, rfr = rs_ring.next()
                yi, ya, yfr = ya_ring.next()
                kb.wait("dve", ptok)
                for tk in rfr + yfr:
                    kb.wait("dve", tk)
                accv = acc[0:64, 0:ng * 130].rearrange("p (g e) -> p g e", e=65)
                rtk = kb.mark("dve", nc.vector.reciprocal(out=rs[:, 0:2 * ng], in_=accv[:, :, 64]))
                kb.wait("dve", rtk, force=True)
                ins = nc.vector.tensor_tensor(out=ya[:, 0:2 * ng, :], in0=accv[:, :, 0:64],
                                              in1=rs[:, 0:2 * ng].rearrange("p (g o) -> p g o", o=1).broadcast_to([64, 2 * ng, 64]), op=ALU.mult)
                ntok = kb.mark("dve", ins)
                acc_ring.free[ai] = [ntok]
                ti, tp, tfr = tp_ring.next()
                kb.wait("pe", ntok)
                for tk in tfr:
                    kb.wait("pe", tk)
                yav = ya[:, :, :].rearrange("p g d -> p (g d)")
                for j in range(ng):
                    ins = nc.tensor.transpose(tp[:, j * 64:(j + 1) * 64], yav[0:64, j * 128:(j + 1) * 128], ident_b[0:64, 0:64])
                ttok = kb.mark("pe", ins)
                rs_ring.free[ri] = [ntok]
                ya_ring.free[yi] = [ttok]
                gi, stg, gfr = stg_ring.next()
                kb.wait("act", ttok)
                for tk in gfr:
                    kb.wait("act", tk)
                ins = nc.scalar.copy(out=stg[:, 0:ng * 64], in_=tp[:, 0:ng * 64])
                ctok = kb.mark("act", ins)
                tp_ring.free[ti] = [ctok]
                kb.wait("sp", ctok)
                r0 = rows3[0]
                stok = kb.dma("sp", YA_T.ap()[hp, :, r0 * 64:(r0 + ng) * 64], stg[:, 0:ng * 64], stg_sem[gi])
                stg_ring.free[gi] = [stok]

            groups = [list(range(r0, min(r0 + 3, 64))) for r0 in range(0, 64, 3)]
            gnext = 0
            LAG = 2
            pending = []
            for kidx in range(NK + LAG):
                if kidx < NK:
                    last_dve_read = qk(kidx)
                for f in pending:
                    f()
                pending = []
                done_kr = kidx - LAG - 4
                while gnext < len(groups) and max(na_superset(groups[gnext][-1])) <= done_kr:
                    for f in pending:
                        f()
                    pending = []
                    last_pv_pe, f = pv_group(groups[gnext])
                    pending.append(f)
                    gnext += 1
            for f in pending:
                f()
            assert gnext == len(groups)
            hp_free[slot] = [last_pv_pe, last_dve_read]
        for gi in range(3):
            out_toks += stg_ring.free[gi]
        return out_toks


    def phase_ret(l):
        s_cc = kb.sem("cc")
        s_ld = kb.sem("ret_c")
        dve, act, pool, pe = nc.vector, nc.scalar, nc.gpsimd, nc.tensor

        def bc(ap2, n, g=4):
            return ap2.rearrange("p (g o) -> p g o", o=1).broadcast_to([128, g, n])

        rc = kb.sb("ret_rc", [128, 4 * 128 + 2 + 32], F32)
        dfb = kb.sb("ret_dfb", [128, 8], F32)
        par = kb.sb("ret_par", [128, 2], F32)
        kb.dma("sp", rc[:, :], rconst.ap()[:, :], s_ld)
        kb.dma("sp", dfb[:, 0:4], dec_f.ap()[l, :].partition_broadcast(128), s_ld)
        kb.dma("sp", dfb[:, 4:8], dec_b.ap()[l, :].partition_broadcast(128), s_ld)
        t0 = kb.dma("sp", par[:, :], par_d.ap()[:, :], s_ld)
        A1, A2, IDX1, IDX2 = rc[:, 0:128], rc[:, 128:256], rc[:, 256:384], rc[:, 384:512]
        PIDX, NIDX = rc[:, 512:514], rc[:, 514:546]
        lg = kb.sb("ret_lg", [128, 8], F32)
        sg_ = kb.sb("ret_sg", [128, 8], F32)
        kb.wait("act", t0)
        tk = kb.mark("act", act.activation(out=sg_[:, :], in_=dfb[:, :], func=AF.Sigmoid))
        kb.wait("act", tk, force=True)
        tk = kb.mark("act", act.activation(out=lg[:, :], in_=sg_[:, :], func=AF.Ln))
        kb.wait("act", tk, force=True)
        kb.wait("dve", tk)
        DT = kb.sb("ret_DT", [128, 4, 128], F32)
        XF = kb.sb("ret_XF", [128, 4, 128], F32)
        XB = kb.sb("ret_XB", [128, 4, 128], F32)
        ZF = kb.sb("ret_ZF", [128, 4], F32)
        ZB = kb.sb("ret_ZB", [128, 4], F32)
        GC = kb.sb("ret_GC", [128, 8], F32)
        CDF = kb.sb("ret_CDF", [128, 4, 32], F32)
        CDB = kb.sb("ret_CDB", [128, 4, 32], F32)
        ZFn = kb.sb("ret_ZFn", [128, 4, 32], F32)
        tmpa = kb.sb("ret_tmpa", [128, 4, 128], F32)
        tmpz = kb.sb("ret_tmpz", [128, 8], F32)
        for h in range(4):
            dve.tensor_scalar(out=tmpa[:, h, :], in0=A1, scalar1=lg[:, h:h + 1], scalar2=None, op0=ALU.mult)
        tk = kb.mark("dve", dve.tensor_copy(out=tmpz[:, 0:1], in_=lg[:, 0:1]))
        kb.wait("dve", tk, force=True)
        for h in range(4):
            dve.scalar_tensor_tensor(out=tmpa[:, h, :], in0=A2, scalar=lg[:, 4 + h:5 + h], in1=tmpa[:, h, :], op0=ALU.mult, op1=ALU.add)
            dve.tensor_scalar(out=tmpz[:, h:h + 1], in0=PIDX[:, 0:1], scalar1=lg[:, h:h + 1], scalar2=None, op0=ALU.mult)
            dve.tensor_scalar(out=tmpz[:, 4 + h:5 + h], in0=PIDX[:, 1:2], scalar1=lg[:, 4 + h:5 + h], scalar2=None, op0=ALU.mult)
            dve.tensor_scalar(out=CDF[:, h, :], in0=NIDX, scalar1=lg[:, h:h + 1], scalar2=None, op0=ALU.mult)
            dve.tensor_scalar(out=ZFn[:, h, :], in0=NIDX, scalar1=PIDX[:, 0:1], scalar2=lg[:, h:h + 1], op0=ALU.add, op1=ALU.mult)
            tk = kb.mark("dve", dve.tensor_scalar(out=CDB[:, h, :], in0=NIDX, scalar1=lg[:, 4 + h:5 + h], scalar2=None, op0=ALU.mult))
        kb.wait("act", tk)
        act.activation(out=DT[:, :, :], in_=tmpa[:, :, :], func=AF.Exp)
        act.activation(out=ZF[:, :], in_=tmpz[:, 0:4], func=AF.Exp)
        act.activation(out=ZB[:, :], in_=tmpz[:, 4:8], func=AF.Exp)
        act.activation(out=CDF[:, :, :], in_=CDF[:, :, :], func=AF.Exp)
        act.activation(out=CDB[:, :, :], in_=CDB[:, :, :], func=AF.Exp)
        act.activation(out=ZFn[:, :, :], in_=ZFn[:, :, :], func=AF.Exp)
        act.activation(out=GC[:, :], in_=lg[:, :], func=AF.Exp, scale=128.0)
        for h in range(4):
            act.activation(out=XF[:, h, :], in_=IDX1, func=AF.Exp, scale=lg[:, h:h + 1])
            tk = kb.mark("act", act.activation(out=XB[:, h, :], in_=IDX2, func=AF.Exp, scale=lg[:, 4 + h:5 + h]))
        tab_tok = tk

        SB_all = kb.sb("ret_SBall", [128, 32, 4, 256], BF16)
        S_run = kb.sb("ret_Srun", [128, 4, 256], F32)
        E_f = kb.sb("ret_Ef", [128, 4, 256], F32)
        Sf_bf = [kb.sb("ret_Sfbf%d" % i, [128, 4, 256], BF16) for i in range(2)]
        kt_ring = Ring([kb.sb("ret_kt%d" % i, [128, 4, 128], BF16) for i in range(3)])
        qt_ring = Ring([kb.sb("ret_qt%d" % i, [128, 4, 128], BF16) for i in range(3)])
        v_ring = Ring([kb.sb("ret_v%d" % i, [128, 4, 256], BF16) for i in range(3)])
        g_ring = Ring([kb.sb("ret_g%d" % i, [128, 4, 256], BF16) for i in range(3)])
        ld_sem = [kb.sem("ret_lds%d" % i) for i in range(3)]
        kzf_ring = Ring([kb.sb("ret_kzf%d" % i, [128, 4, 128], BF16) for i in range(2)])
        kzb_ring = Ring([kb.sb("ret_kzb%d" % i, [128, 4, 128], BF16) for i in range(2)])
        pt_ring = Ring([kb.sb("ret_pt%d" % i, [128, 4, 128], BF16) for i in range(2)])
        qf_ring = Ring([kb.sb("ret_qf%d" % i, [128, 4, 128], BF16) for i in range(2)])
        qb_ring = Ring([kb.sb("ret_qb%d" % i, [128, 4, 128], BF16) for i in range(2)])
        on_ring = Ring([kb.sb("ret_on%d" % i, [128, 4, 256], F32) for i in range(2)])
        yb_ring = Ring([kb.sb("ret_yb%d" % i, [128, 1024], BF16) for i in range(2)])
        ybt_ring = Ring([kb.sb("ret_ybt%d" % i, [128, 8, 128], BF16) for i in range(2)])
        ybt_sem = [kb.sem("ret_ybts%d" % i) for i in range(2)]
        st_ring = Ring([kb.sb("ret_st%d" % i, [128, 4, 6], F32) for i in range(2)])
        mv_ring = Ring([kb.sb("ret_mv%d" % i, [128, 4, 2], F32) for i in range(2)])
        r1_ring = Ring([kb.sb("ret_r1%d" % i, [128, 3, 4], F32) for i in range(2)])
        nm_ring = Ring([kb.sb("ret_nm%d" % i, [128, 4], F32) for i in range(2)])
        tmps = kb.sb("ret_tmps", [128, 4, 256], F32)
        ST_r = Ring([psb[0], psb[1]])
        OTa, OTb = psb[2], psb[3]
        KVa, KVb = psb[4], psb[5]
        KTR, YTR = ptb[0], ptb[1]

        def load_chunk(n, want_q):
            i = n % 2 if not want_q else n % 2
            _, kt, fk = kt_ring.next()
            _, v, fv = v_ring.next()
            for tk in fk + fv:
                kb.wait("sp", tk)
            sl = slice(n * 128, (n + 1) * 128)
            sm = ld_sem[(kt_ring.k - 1) % 3]
            kb.dma("sp", kt[:, :, :], KR_T.ap()[:, :, sl].rearrange("h p t -> p h t"), sm)
            tk = kb.dma("sp", v[:, :, :], VR.ap()[sl, :].rearrange("t (h e) -> t h e", h=4), sm)
            qt = g = None
            if want_q:
                _, qt, fq = qt_ring.next()
                _, g, fg = g_ring.next()
                for t_ in fq + fg:
                    kb.wait("sp", t_)
                kb.dma("sp", qt[:, :, :], QR_T.ap()[:, :, sl].rearrange("h p t -> p h t"), sm)
                tk = kb.dma("sp", g[:, :, :], SG.ap()[sl, :].rearrange("t (h e) -> t h e", h=4), sm)
            return dict(kt=kt, v=v, qt=qt, g=g, tok=tk, ki=(kt_ring.k - 1) % 3, vi=(v_ring.k - 1) % 3,
                        qi=(qt_ring.k - 1) % 3, gi=(g_ring.k - 1) % 3)

        def mm4(dst_a, dst_b, lhs_fn, rhs_fn, groups_extra=None):
            ins = None
            for h in range(4):
                bank = dst_a if h < 2 else dst_b
                terms = [(lhs_fn(h), rhs_fn(h))] + ([(a(h), b(h)) for a, b in groups_extra] if groups_extra else [])
                for ti, (lh, rh) in enumerate(terms):
                    ins = pe.matmul(bank[:, (h % 2) * 256:(h % 2) * 256 + 256], lhsT=lh, rhs=rh,
                                    start=(ti == 0 and h % 2 == 0), stop=(ti == len(terms) - 1), skip_group_check=True)
            return ins

        def bank4(a, b):
            return a[:, :].rearrange("p (g e) -> p g e", e=256), b[:, :].rearrange("p (g e) -> p g e", e=256)

        kb.wait("pool", tab_tok)
        pool.memset(S_run[:, :, :], 0.0)
        pool.memset(E_f[:, :, :], 0.0)
        z_tok = kb.mark("pool", pool.memset(SB_all[:, 31, :, :], 0.0))
        ktr_free = []
        kv_free = []
        upd_tok = z_tok
        pend_copy = None
        atok = None
        nxt = load_chunk(31, False)
        for n in range(31, -1, -1):
            cur = nxt
            if n > 0:
                nxt = load_chunk(n - 1, False)
            kt, v = cur["kt"], cur["v"]
            kb.wait("pe", cur["tok"])
            kb.wait("pe", tok_ident)
            for tk in ktr_free:
                kb.wait("pe", tk)
            for h in range(4):
                ins = pe.transpose(KTR[:, h * 128:(h + 1) * 128], kt[:, h, :], ident_b[:, :])
            ttok = kb.mark("pe", ins)
            kt_ring.free[cur["ki"]] = [ttok]
            _, kzf, f1 = kzf_ring.next()
            _, kzb, f2 = kzb_ring.next()
            kb.wait("act", ttok)
            kb.wait("act", tab_tok)
            for tk in f1 + f2:
                kb.wait("act", tk)
            for h in range(4):
                act.activation(out=kzf[:, h, :], in_=KTR[:, h * 128:(h + 1) * 128], func=AF.Identity, scale=ZF[:, h:h + 1])
                ins = act.activation(out=kzb[:, h, :], in_=KTR[:, h * 128:(h + 1) * 128], func=AF.Identity, scale=ZB[:, h:h + 1])
            ztok = kb.mark("act", ins)
            ktr_free = [ztok]
            if pend_copy is not None:
                kb.wait("act", upd_tok)
                atok = kb.mark("act", act.copy(out=SB_all[:, pend_copy, :, :], in_=S_run[:, :, :]))
                pend_copy = None
            kb.wait("pe", ztok)
            for tk in kv_free:
                kb.wait("pe", tk)
            mm4(psb[2], psb[3], lambda h: kzf[:, h, :], lambda h: v[:, h, :])
            ins = mm4(psb[4], psb[5], lambda h: kzb[:, h, :], lambda h: v[:, h, :])
            mtok = kb.mark("pe", ins)
            kzf_ring.free[(kzf_ring.k - 1) % 2] = [mtok]
            kzb_ring.free[(kzb_ring.k - 1) % 2] = [mtok]
            v_ring.free[cur["vi"]] = [mtok]
            kb.wait("dve", mtok)
            kb.wait("dve", upd_tok, force=True)
            kb.wait("dve", atok)
            fa, fb = bank4(psb[2], psb[3])
            ba, bb_ = bank4(psb[4], psb[5])
            for h in range(4):
                fsrc = (fa if h < 2 else fb)[:, h % 2, :]
                dve.scalar_tensor_tensor(out=E_f[:, h, :], in0=fsrc, scalar=CDF[:, h, n:n + 1], in1=E_f[:, h, :], op0=ALU.mult, op1=ALU.add)
            for h in range(4):
                bsrc = (ba if h < 2 else bb_)[:, h % 2, :]
                tk = kb.mark("dve", dve.scalar_tensor_tensor(out=S_run[:, h, :], in0=S_run[:, h, :], scalar=GC[:, 4 + h:5 + h], in1=bsrc, op0=ALU.mult, op1=ALU.add))
            upd_tok = tk
            kv_free = [tk]
            if n > 0:
                pend_copy = n - 1
        ef_tok = upd_tok
        s_ex = kb.sem("ret_ex")
        kb.wait("sp", ef_tok)
        kb.dma("sp", RSB.ap()[0, :].rearrange("(p f) -> p f", p=128), E_f[:, :, :].rearrange("p h e -> p (h e)"), s_ex)
        t = kb.dma("sp", RSB.ap()[1, :].rearrange("(p f) -> p f", p=128), S_run[:, :, :].rearrange("p h e -> p (h e)"), s_ex)
        kb.wait("pool", t)
        ins = pool.collective_compute("AllGather", ALU.bypass, replica_groups=pairs, ins=[RSB.ap().opt()], outs=[RSG.ap().opt()])
        ins.then_inc(s_cc.h)
        s_cc.n += 1
        cc_tok = (s_cc, s_cc.n)
        kb.wait("sp", cc_tok)
        Sb_in = kb.sb("ret_Sbin", [128, 4, 256], F32)
        kb.wait("sp", t)
        kb.dma("sp", S_run[:, :, :].rearrange("p h e -> p (h e)"), RSG.ap()[0, :].rearrange("(p f) -> p f", p=128), s_ex)
        t = kb.dma("sp", Sb_in[:, :, :].rearrange("p h e -> p (h e)"), RSG.ap()[3, :].rearrange("(p f) -> p f", p=128), s_ex)
        kb.wait("dve", t)
        dve.tensor_scalar(out=S_run[:, :, :], in0=S_run[:, :, :], scalar1=par[:, 0:1], scalar2=None, op0=ALU.mult)
        tk = kb.mark("dve", dve.tensor_scalar(out=Sb_in[:, :, :], in0=Sb_in[:, :, :], scalar1=par[:, 1:2], scalar2=None, op0=ALU.mult))
        kb.wait("dve", tk, force=True)
        kb.wait("act", tk)
        sf_tok = [kb.mark("act", act.copy(out=Sf_bf[0][:, :, :], in_=S_run[:, :, :])), None]
        sfc_tok = sf_tok[0]
        for n in range(32):
            for h in range(4):
                ins = dve.scalar_tensor_tensor(out=SB_all[:, n, h, :], in0=Sb_in[:, h, :], scalar=CDB[:, h, n:n + 1],
                                               in1=SB_all[:, n, h, :], op0=ALU.mult, op1=ALU.add)
        fix_tok = kb.mark("dve", ins)

        ot_free = [ef_tok]
        kv_free = [upd_tok]
        ytr_free = []
        upd_tok = fix_tok
        nm_ring2 = Ring([kb.sb("ret_nmr%d" % i, [128, 4], F32) for i in range(2)])
        pend = None

        def gn_norm(pd):
            nonlocal ot_free
            mv, r1, nm, oa, ob = pd["mv"], pd["r1"], pd["nm"], pd["oa"], pd["ob"]
            _, on, f5 = on_ring.next()
            kb.wait("act", pd["rtok"])
            for tk in f5:
                kb.wait("act", tk)
            for h in range(4):
                src = (oa if h < 2 else ob)[:, h % 2, :]
                ins = act.activation(out=on[:, h, :], in_=src, func=AF.Identity, bias=nm[:, h:h + 1], scale=r1[:, 2, h:h + 1])
            ntok = kb.mark("act", ins)
            ot_free = [ntok]
            st_ring.free[pd["sti"]] = [ntok]
            mv_ring.free[pd["mvi"]] = [ntok]
            r1_ring.free[pd["r1i"]] = [ntok]
            nm_ring2.free[pd["nmi"]] = [ntok]
            pd["on"] = on
            pd["oni"] = (on_ring.k - 1) % 2
            pd["ntok"] = ntok

        def gn_out(pd):
            nonlocal ytr_free
            n_, g, gi, on, ntok = pd["n"], pd["g"], pd["gi"], pd["on"], pd["ntok"]
            yi, yb, fy = yb_ring.next()
            kb.wait("pool", ntok)
            for tk in fy:
                kb.wait("pool", tk)
            ytok = kb.mark("pool", pool.tensor_tensor(out=yb[:, :].rearrange("p (h e) -> p h e", h=4), in0=on[:, :, :], in1=g[:, :, :], op=ALU.mult))
            on_ring.free[pd["oni"]] = [ytok]
            g_ring.free[gi] = [ytok]
            bi_, ybt, fb_ = ybt_ring.next()
            for tk in fb_:
                kb.wait("act", tk)
            pt_, et_ = transpose_chunk(yb, ytok, ybt[:, :, :], YTR, ytr_free, "act")
            ytr_free = [et_]
            yb_ring.free[yi] = [pt_]
            kb.wait("sp", et_)
            stt_ = kb.dma("sp", YB_T.ap()[:, :, n_ * 128:(n_ + 1) * 128].rearrange("k p t -> p k t"), ybt[:, :, :], ybt_sem[bi_])
            ybt_ring.free[bi_] = [stt_]

        nxt = load_chunk(0, True)
        for n in range(32):
            cur = nxt
            if n < 31:
                nxt = load_chunk(n + 1, True)
            kt, v, qt, g = cur["kt"], cur["v"], cur["qt"], cur["g"]
            sfb = Sf_bf[n % 2]
            si, STb, sfr = ST_r.next()
            kb.wait("pe", cur["tok"])
            for tk in sfr + ktr_free:
                kb.wait("pe", tk)
            for h in range(4):
                pe.matmul(STb[:, h * 128:(h + 1) * 128], lhsT=kt[:, h, :], rhs=qt[:, h, :], start=(h == 0), stop=True, skip_group_check=True)
            for h in range(4):
                ins = pe.transpose(KTR[:, h * 128:(h + 1) * 128], kt[:, h, :], ident_b[:, :])
            stok = kb.mark("pe", ins)
            kt_ring.free[cur["ki"]] = [stok]
            _, pt, f1 = pt_ring.next()
            kb.wait("dve", stok)
            for tk in f1:
                kb.wait("dve", tk)
            ptok = kb.mark("dve", dve.tensor_tensor(out=pt[:, :, :], in0=STb[:, :].rearrange("p (h t) -> p h t", h=4), in1=DT[:, :, :], op=ALU.mult))
            ST_r.free[si] = [ptok]
            _, kzf, f2 = kzf_ring.next()
            kb.wait("act", stok)
            for tk in f2:
                kb.wait("act", tk)
            for h in range(4):
                ins = act.activation(out=kzf[:, h, :], in_=KTR[:, h * 128:(h + 1) * 128], func=AF.Identity, scale=ZF[:, h:h + 1])
            ktok2 = kb.mark("act", ins)
            ktr_free = [ktok2]
            _, qf, f1 = qf_ring.next()
            _, qb, f2 = qb_ring.next()
            kb.wait("pool", cur["tok"])
            kb.wait("pool", tab_tok)
            for tk in f1 + f2:
                kb.wait("pool", tk)
            pool.tensor_tensor(out=qf[:, :, :], in0=qt[:, :, :], in1=XF[:, :, :], op=ALU.mult)
            qtok = kb.mark("pool", pool.tensor_tensor(out=qb[:, :, :], in0=qt[:, :, :], in1=XB[:, :, :], op=ALU.mult))
            qt_ring.free[cur["qi"]] = [qtok, stok]
            kb.wait("pe", ktok2)
            for tk in kv_free:
                kb.wait("pe", tk)
            ins = mm4(KVa, KVb, lambda h: kzf[:, h, :], lambda h: v[:, h, :])
            ktok = kb.mark("pe", ins)
            kzf_ring.free[(kzf_ring.k - 1) % 2] = [ktok]
            if pend is not None:
                gn_norm(pend)
            kb.wait("pe", ptok)
            kb.wait("pe", qtok)
            kb.wait("pe", sf_tok[n % 2])
            kb.wait("pe", fix_tok)
            for tk in ot_free:
                kb.wait("pe", tk)
            ins = mm4(OTa, OTb, lambda h: pt[:, h, :], lambda h: v[:, h, :],
                      [(lambda h: qf[:, h, :], lambda h: sfb[:, h, :]), (lambda h: qb[:, h, :], lambda h: SB_all[:, n, h, :])])
            otok = kb.mark("pe", ins)
            pt_ring.free[(pt_ring.k - 1) % 2] = [otok]
            qf_ring.free[(qf_ring.k - 1) % 2] = [otok]
            qb_ring.free[(qb_ring.k - 1) % 2] = [otok]
            v_ring.free[cur["vi"]] = [otok]
            kb.wait("dve", ktok)
            kb.wait("dve", upd_tok, force=True)
            kb.wait("dve", sfc_tok)
            ka, kb_ = bank4(KVa, KVb)
            for h in range(4):
                src = (ka if h < 2 else kb_)[:, h % 2, :]
                ins = dve.scalar_tensor_tensor(out=S_run[:, h, :], in0=S_run[:, h, :], scalar=GC[:, h:h + 1], in1=src, op0=ALU.mult, op1=ALU.add)
            upd_tok = kb.mark("dve", ins)
            kv_free = [upd_tok]
            kb.wait("act", upd_tok)
            kb.wait("act", otok if n > 0 else None)
            sfc_tok = kb.mark("act", act.copy(out=Sf_bf[(n + 1) % 2][:, :, :], in_=S_run[:, :, :]))
            sf_tok[(n + 1) % 2] = sfc_tok
            if pend is not None:
                gn_out(pend)
            sti, stt, f1 = st_ring.next()
            mvi, mv, f2 = mv_ring.next()
            r1i, r1, f3 = r1_ring.next()
            nmi, nm, f4 = nm_ring2.next()
            kb.wait("dve", otok)
            for tk in f1 + f2 + f3 + f4:
                kb.wait("dve", tk)
            oa, ob = bank4(OTa, OTb)
            for h in range(4):
                src = (oa if h < 2 else ob)[:, h % 2, :]
                tk = kb.mark("dve", dve.bn_stats(out=stt[:, h, :], in_=src))
            kb.wait("dve", tk, force=True)
            for h in range(4):
                tk = kb.mark("dve", dve.bn_aggr(out=mv[:, h, :], in_=stt[:, h:h + 1, :]))
            kb.wait("dve", tk, force=True)
            tk = kb.mark("dve", dve.tensor_scalar(out=r1[:, 0, :], in0=mv[:, :, 1], scalar1=GN_EPS, scalar2=None, op0=ALU.add))
            kb.wait("act", tk)
            tk = kb.mark("act", act.activation(out=r1[:, 1, :], in_=r1[:, 0, :], func=AF.Sqrt))
            kb.wait("dve", tk)
            tk = kb.mark("dve", dve.reciprocal(out=r1[:, 2, :], in_=r1[:, 1, :]))
            kb.wait("dve", tk, force=True)
            rtok = kb.mark("dve", dve.scalar_tensor_tensor(out=nm[:, :], in0=mv[:, :, 0], scalar=-1.0, in1=r1[:, 2, :], op0=ALU.mult, op1=ALU.mult))
            pend = dict(n=n, mv=mv, r1=r1, nm=nm, g=g, gi=cur["gi"], oa=oa, ob=ob, rtok=rtok, sti=sti, mvi=mvi, r1i=r1i, nmi=nmi)
        gn_norm(pend)
        gn_out(pend)
        return ybt_ring.free[0] + ybt_ring.free[1]

    class LN:
        def __init__(self, tag, g_d, b_d, l, banks):
            self.banks = banks
            self.G = kb.sb(tag + "_G", [128, D], F32)
            self.B = kb.sb(tag + "_B", [128, D], F32)
            sm = kb.sem(tag + "_gb")
            kb.dma("sp", self.G[:, :], g_d.ap()[l, :].partition_broadcast(128), sm)
            self.gb_tok = kb.dma("sp", self.B[:, :], b_d.ap()[l, :].partition_broadcast(128), sm)
            self.stt = Ring([kb.sb(tag + "_stt%d" % i, [128, 2, 6], F32) for i in range(3)])
            self.mv = Ring([kb.sb(tag + "_mv%d" % i, [128, 2], F32) for i in range(3)])
            self.r = Ring([kb.sb(tag + "_r%d" % i, [128, 4], F32) for i in range(3)])
            self.xn = Ring([kb.sb(tag + "_xn%d" % i, [128, D], F32) for i in range(2)])
            self.xo = Ring([kb.sb(tag + "_xo%d" % i, [128, D], F32) for i in range(2)])
            self.xo_sem = [kb.sem(tag + "_xos%d" % i) for i in range(2)]
            self.xb = Ring([kb.sb(tag + "_xb%d" % i, [128, D], BF16) for i in range(2)])
            self.xt = Ring([kb.sb(tag + "_xt%d" % i, [128, 8, 128], BF16) for i in range(2)])
            self.xt_sem = [kb.sem(tag + "_xts%d" % i) for i in range(2)]
            self.tp_free = [[] for _ in banks]
            self.k = 0

        def stats(self, pre, pre_tok):
            dve, act = nc.vector, nc.scalar
            si, stt, f1 = self.stt.next()
            mi, mv, f2 = self.mv.next()
            ri, r, f3 = self.r.next()
            kb.wait("dve", pre_tok, force=True)
            for tk in f1 + f2 + f3:
                kb.wait("dve", tk)
            dve.bn_stats(out=stt[:, 0, :], in_=pre[:, 0:512])
            tk = kb.mark("dve", dve.bn_stats(out=stt[:, 1, :], in_=pre[:, 512:1024]))
            kb.wait("dve", tk, force=True)
            tk = kb.mark("dve", dve.bn_aggr(out=mv[:, :], in_=stt[:, :, :]))
            kb.wait("dve", tk, force=True)
            tk = kb.mark("dve", dve.tensor_scalar(out=r[:, 0:1], in0=mv[:, 1:2], scalar1=LN_EPS, scalar2=None, op0=ALU.add))
            kb.wait("act", tk)
            stok = kb.mark("act", act.activation(out=r[:, 1:2], in_=r[:, 0:1], func=AF.Sqrt))
            return dict(pre=pre, mv=mv, r=r, si=si, mi=mi, ri=ri, stok=stok)

        def finish_a(self, h, dst_x, want_xt):
            dve, act = nc.vector, nc.scalar
            pre, mv, r = h["pre"], h["mv"], h["r"]
            kb.wait("dve", h["stok"])
            tk = kb.mark("dve", dve.reciprocal(out=r[:, 2:3], in_=r[:, 1:2]))
            kb.wait("dve", tk, force=True)
            tk = kb.mark("dve", dve.tensor_scalar(out=r[:, 3:4], in0=mv[:, 0:1], scalar1=r[:, 2:3], scalar2=-1.0, op0=ALU.mult, op1=ALU.mult))
            _, xn, f4 = self.xn.next()
            kb.wait("act", tk)
            for t_ in f4:
                kb.wait("act", t_)
            ntok = kb.mark("act", act.activation(out=xn[:, :], in_=pre[:, :], func=AF.Identity, bias=r[:, 3:4], scale=r[:, 2:3]))
            self.stt.free[h["si"]] = [ntok]
            self.mv.free[h["mi"]] = [ntok]
            self.r.free[h["ri"]] = [ntok]
            oi, xo, f5 = self.xo.next()
            kb.wait("dve", ntok)
            kb.wait("dve", self.gb_tok)
            for t_ in f5:
                kb.wait("dve", t_)
            dve.tensor_tensor(out=xo[:, :], in0=xn[:, :], in1=self.G[:, :], op=ALU.mult)
            tk = kb.mark("dve", dve.tensor_tensor(out=xo[:, :], in0=xo[:, :], in1=self.B[:, :], op=ALU.add))
            self.xn.free[(self.xn.k - 1) % 2] = [tk]
            toks = []
            btok = xb = bi_ = None
            if want_xt:
                bi_, xb, f6 = self.xb.next()
                kb.wait("act", tk)
                for t_ in f6:
                    kb.wait("act", t_)
                btok = kb.mark("act", act.copy(out=xb[:, :], in_=xo[:, :]))
            kb.wait("sp", tk)
            st = kb.dma("sp", dst_x, xo[:, :], self.xo_sem[oi])
            self.xo.free[oi] = [st] + ([btok] if btok else [])
            toks.append(st)
            h.update(ntok=ntok, toks=toks, btok=btok, xb=xb, bi=bi_)
            return h

        def finish_b(self, h, dst_xt):
            toks = h["toks"]
            if dst_xt is not None:
                ti, xt, f7 = self.xt.next()
                e = "act"
                for t_ in f7:
                    kb.wait(e, t_)
                j = self.k % len(self.banks)
                self.k += 1
                pt_, et_ = transpose_chunk(h["xb"], h["btok"], xt[:, :, :], self.banks[j], self.tp_free[j], e)
                self.tp_free[j] = [et_]
                self.xb.free[h["bi"]] = [pt_]
                kb.wait("sp", et_)
                st2 = kb.dma("sp", dst_xt, xt[:, :, :], self.xt_sem[ti])
                self.xt.free[ti] = [st2]
                toks.append(st2)
            return h["ntok"], toks

        def finish(self, h, dst_x, dst_xt):
            self.finish_a(h, dst_x, dst_xt is not None)
            return self.finish_b(h, dst_xt)

    def load_w_cast(dst, src, sem):
        return kb.dma("pool", dst, src, sem)

    def p1d_weights(l):
        s_w = kb.sem("p1d_w")
        Wa = kb.sb("p1d_Wa", [128, 4, D], BF16)
        Wb = kb.sb("p1d_Wb", [128, 8, D], BF16)
        Wo = kb.sb("p1d_Wo", [128, 8, D], BF16)
        load_w_cast(Wa[:, :, :], w_ba.ap()[l].rearrange("(k p) c -> p k c", p=128), s_w)
        for hf in range(2):
            load_w_cast(Wb[:, hf * 4:(hf + 1) * 4, :], w_bb.ap()[l, hf * 512:(hf + 1) * 512, :].rearrange("(k p) c -> p k c", p=128), s_w)
            wtok = load_w_cast(Wo[:, hf * 4:(hf + 1) * 4, :], w_out.ap()[l, hf * 512:(hf + 1) * 512, :].rearrange("(k p) c -> p k c", p=128), s_w)
        return Wa, Wb, Wo, wtok

    def phase_p1d(l, w1d):
        dve, act, pool, pe = nc.vector, nc.scalar, nc.gpsimd, nc.tensor
        Wa, Wb, Wo, wtok = w1d
        ln = LN("ln1", ln1_g, ln1_b, l, [ptb[0], ptb[1]])
        in_ring = Ring([dict(ya=kb.sb("p1d_ya%d" % i, [128, 4, 512], BF16), yb=kb.sb("p1d_yb%d" % i, [128, 8, 512], BF16),
                             ga=kb.sb("p1d_ga%d" % i, [128, 8, 512], BF16), gb=kb.sb("p1d_gb%d" % i, [128, 8, 512], BF16)) for i in range(2)])
        in_sem = [kb.sem("p1d_ins%d" % i) for i in range(2)]
        mg_ring = Ring([kb.sb("p1d_mg%d" % i, [128, 8, 512], BF16) for i in range(2)])
        t1_ring = Ring([kb.sb("p1d_t1%d" % i, [128, 512], F32) for i in range(2)])
        t2_ring = Ring([kb.sb("p1d_t2%d" % i, [128, 512], F32) for i in range(2)])
        x_ring = Ring([kb.sb("p1d_x%d" % i, [128, D], F32) for i in range(2)])
        x_sem = [kb.sem("p1d_xs%d" % i) for i in range(2)]
        pre_ring = Ring([kb.sb("p1d_pre%d" % i, [128, D], F32) for i in range(3)])
        za_ring = Ring([psb[0], psb[1]])
        zb_ring = Ring([psb[2], psb[3]])
        y_free = []
        out_toks = {}

        def load_tile(tt):
            i, bufs, fr = in_ring.next()
            for tk in fr:
                kb.wait("sp", tk)
            sl = slice(tt * 512, (tt + 1) * 512)
            kb.dma("sp", bufs["ya"][:, :, :], YA_T.ap()[:, :, sl].rearrange("k p t -> p k t"), in_sem[i])
            kb.dma("sp", bufs["yb"][:, :, :], YB_T.ap()[:, :, sl].rearrange("k p t -> p k t"), in_sem[i])
            kb.dma("sp", bufs["ga"][:, :, :], GA_T.ap()[:, :, sl].rearrange("k p t -> p k t"), in_sem[i])
            tk = kb.dma("sp", bufs["gb"][:, :, :], GB_T.ap()[:, :, sl].rearrange("k p t -> p k t"), in_sem[i])
            return i, bufs, tk

        pending = None

        pending_b = None

        def ln_fin_a(hnd, c, pi):
            nonlocal pending_b
            ln.finish_a(hnd, X1.ap()[c * 128:(c + 1) * 128, :], True)
            pre_ring.free[pi] = [hnd["ntok"]]
            pending_b = (hnd, c)

        def ln_fin_b(hnd, c):
            ntok, toks = ln.finish_b(hnd, X1T_D.ap()[:, :, c * 128:(c + 1) * 128].rearrange("k p t -> p k t"))
            for t_ in toks:
                out_toks[id(t_[0])] = t_

        tiles = {}

        def tile_begin(tt):
            ii, bufs, ltok = tiles[tt]["load"]
            mi, mg, mfr = mg_ring.next()
            tiles[tt].update(ii=ii, bufs=bufs, ltok=ltok, mi=mi, mg=mg, mfr=mfr)

        def fb_step(tt, fb):
            T = tiles[tt]
            bufs, ltok, mg = T["bufs"], T["ltok"], T["mg"]
            _, za, fa = za_ring.next()
            _, zb, fb_ = zb_ring.next()
            kb.wait("pe", ltok)
            kb.wait("pe", wtok)
            for tk in fa + fb_:
                kb.wait("pe", tk)
            for kc in range(4):
                pe.matmul(za[:, :], lhsT=Wa[:, kc, fb * 128:(fb + 1) * 128], rhs=bufs["ya"][:, kc, :], start=(kc == 0), stop=(kc == 3))
            for kc in range(8):
                ins = pe.matmul(zb[:, :], lhsT=Wb[:, kc, fb * 128:(fb + 1) * 128], rhs=bufs["yb"][:, kc, :], start=(kc == 0), stop=(kc == 7))
            ptok = kb.mark("pe", ins)
            _, t1, f1 = t1_ring.next()
            _, t2, f2 = t2_ring.next()
            kb.wait("dve", ptok)
            kb.wait("dve", ltok)
            for tk in f1 + f2:
                kb.wait("dve", tk)
            if fb == 0:
                for tk in T["mfr"]:
                    kb.wait("dve", tk)
            dve.tensor_tensor(out=t1[:, :], in0=za[:, :], in1=bufs["ga"][:, fb, :], op=ALU.mult)
            dtok = kb.mark("dve", dve.tensor_tensor(out=t2[:, :], in0=zb[:, :], in1=bufs["gb"][:, fb, :], op=ALU.mult))
            za_ring.free[(za_ring.k - 1) % 2] = [dtok]
            zb_ring.free[(zb_ring.k - 1) % 2] = [dtok]
            T["mtok"] = kb.mark("dve", dve.tensor_tensor(out=mg[:, fb, :], in0=t1[:, :], in1=t2[:, :], op=ALU.add))
            if fb == 7:
                in_ring.free[T["ii"]] = [ptok, dtok]

        def sub_chunk(tt, sc):
            nonlocal y_free, pending, pending_b
            T = tiles[tt]
            mg = T["mg"]
            c = tt * 4 + sc
            xi, xt_, fx = x_ring.next()
            for tk in fx:
                kb.wait("sp", tk)
            xtok = kb.dma("sp", xt_[:, :], XA.ap()[c * 128:(c + 1) * 128, :], x_sem[xi])
            kb.wait("pe", T["mtok"])
            for tk in y_free:
                kb.wait("pe", tk)
            for half in range(2):
                bank = psb[4 + half]
                for fb in range(8):
                    ins = pe.matmul(bank[:, :], lhsT=mg[:, fb, sc * 128:(sc + 1) * 128], rhs=Wo[:, fb, half * 512:(half + 1) * 512],
                                    start=(fb == 0), stop=(fb == 7))
            ytok = kb.mark("pe", ins)
            if pending_b is not None:
                ln_fin_b(*pending_b)
                pending_b = None
            pi, pre, fp = pre_ring.next()
            kb.wait("dve", ytok)
            kb.wait("dve", xtok)
            for tk in fp:
                kb.wait("dve", tk)
            dve.scalar_tensor_tensor(out=pre[:, 0:512], in0=xt_[:, 0:512], scalar=ALPHA, in1=psb[4][:, :], op0=ALU.mult, op1=ALU.add)
            ptk = kb.mark("dve", dve.scalar_tensor_tensor(out=pre[:, 512:1024], in0=xt_[:, 512:1024], scalar=ALPHA, in1=psb[5][:, :], op0=ALU.mult, op1=ALU.add))
            y_free = [ptk]
            x_ring.free[xi] = [ptk]
            hnd = ln.stats(pre, ptk)
            pending = (hnd, c, pi)
            if sc == 3:
                mg_ring.free[T["mi"]] = [ytok]

        tiles[0] = dict(load=load_tile(0))
        for tt in range(9):
            if tt < 8:
                if tt + 1 < 8:
                    tiles[tt + 1] = dict(load=load_tile(tt + 1))
                tile_begin(tt)
            for fb in range(8):
                if tt < 8:
                    fb_step(tt, fb)
                if fb % 2 == 0 and pending is not None:
                    ln_fin_a(*pending)
                    pending = None
                if tt >= 1 and fb % 2 == 1:
                    sub_chunk(tt - 1, fb // 2)
        if pending is not None:
            ln_fin_a(*pending)
        if pending_b is not None:
            ln_fin_b(*pending_b)
        return list(out_toks.values())

    def phase_p2a(l):
        dve, act, pool, pe = nc.vector, nc.scalar, nc.gpsimd, nc.tensor
        s_cc = kb.sem("cc")
        s_ex = kb.sem("p2a_ex")
        par = kb.sb("p2a_par", [128, 2], F32)
        kb.dma("sp", par[:, :], par_d.ap()[:, :], s_ex)
        kb.dma("sp", CXB.ap()[0, :].rearrange("(k p o) -> k p o", p=128, o=1), X1T_D.ap()[:, :, 0:1], s_ex, allow_slow_non_contiguous=True)
        t = kb.dma("sp", CXB.ap()[1, :].rearrange("(k p o) -> k p o", p=128, o=1), X1T_D.ap()[:, :, NT - 1:NT], s_ex, allow_slow_non_contiguous=True)
        ex_t = t
        XT = kb.sb("p2a_XT", [128, 8, NT + 2], BF16)
        s_xt = kb.sem("p2a_xt")
        for kc in range(8):
            xtm_tok = kb.dma("sp", XT[:, kc, 1:NT + 1], X1T_D.ap()[kc, :, :], s_xt)
        cp = kb.sb("p2a_cp", [128, 44, 4], F32)
        cp_tok = kb.dma("sp", cp[:, :, :], convp.ap()[l], kb.sem("p2a_cp"))
        w_ring = Ring([kb.sb("p2a_w%d" % i, [128, 2, 8, 128], BF16) for i in range(2)])
        w_sem = [kb.sem("p2a_ws%d" % i) for i in range(2)]
        HW = 2048
        H_ring = Ring([kb.sb("p2a_H%d" % i, [128, 2, HW + 2], F32) for i in range(2)])
        C_ring = Ring([kb.sb("p2a_C%d" % i, [128, 2, HW], F32) for i in range(2)])
        U_ring = Ring([kb.sb("p2a_U%d" % i, [128, HW], BF16) for i in range(2)])
        U_sem = [kb.sem("p2a_us%d" % i) for i in range(2)]
        ps_ring = Ring(psb[0:4])
        hp_ring = Ring([psb[4], psb[5]])

        def load_w(c):
            i, w, fr = w_ring.next()
            for tk in fr:
                kb.wait("pool", tk)
            for part in range(2):
                c0 = part * DFF + c * 128
                tk = load_w_cast(w[:, part, :, :], w_up.ap()[l, :, c0:c0 + 128].rearrange("(k p) c -> p k c", p=128), w_sem[i])
            return i, w, tk

        nxt = load_w(0)
        kb.wait("pool", ex_t)
        ins = pool.collective_compute("AllGather", ALU.bypass, replica_groups=pairs, ins=[CXB.ap().opt()], outs=[CXG.ap().opt()])
        ins.then_inc(s_cc.h)
        s_cc.n += 1
        cc_tok = (s_cc, s_cc.n)
        kb.wait("sp", cc_tok)
        hal = kb.sb("p2a_hal", [128, 8, 2], BF16)
        s_hal = kb.sem("p2a_hal")
        kb.dma("sp", hal[:, :, 0:1], CXG.ap()[1, :].rearrange("(k p o) -> p k o", p=128, o=1), s_hal, allow_slow_non_contiguous=True)
        t = kb.dma("sp", hal[:, :, 1:2], CXG.ap()[2, :].rearrange("(k p o) -> p k o", p=128, o=1), s_hal, allow_slow_non_contiguous=True)
        kb.wait("dve", t)
        kb.wait("dve", xtm_tok)
        dve.tensor_scalar(out=XT[:, :, 0], in0=hal[:, :, 0], scalar1=par[:, 0:1], scalar2=None, op0=ALU.mult)
        xt_tok = kb.mark("dve", dve.tensor_scalar(out=XT[:, :, NT + 1], in0=hal[:, :, 1], scalar1=par[:, 1:2], scalar2=None, op0=ALU.mult))
        for c in range(22):
            wi, w, wtok = nxt
            if c + 1 < 22:
                nxt = load_w(c + 1)
            for hf in range(2):
                T0 = hf * HW
                hi, H, fh = H_ring.next()
                ev_toks = []
                for part in range(2):
                    _, hb, fhp = hp_ring.next()
                    kb.wait("pe", wtok)
                    kb.wait("pe", xt_tok)
                    for tk in fhp:
                        kb.wait("pe", tk)
                    for kc in range(8):
                        ins = pe.matmul(hb[:, 0:2], lhsT=w[:, part, kc, :], rhs=XT[:, kc, T0:T0 + HW + 2:HW + 1], start=(kc == 0), stop=(kc == 7))
                    htok = kb.mark("pe", ins)
                    kb.wait("act", htok)
                    if part == 0:
                        for tk in fh:
                            kb.wait("act", tk)
                    etk = kb.mark("act", act.copy(out=H[:, part, 0:HW + 2:HW + 1], in_=hb[:, 0:2]))
                    hp_ring.free[(hp_ring.k - 1) % 2] = [etk]
                    for t4 in range(4):
                        pi, pb, pfr = ps_ring.next()
                        for tk in pfr:
                            kb.wait("pe", tk)
                        for kc in range(8):
                            ins = pe.matmul(pb[:, :], lhsT=w[:, part, kc, :], rhs=XT[:, kc, 1 + T0 + t4 * 512:1 + T0 + (t4 + 1) * 512],
                                            start=(kc == 0), stop=(kc == 7))
                        mtok = kb.mark("pe", ins)
                        kb.wait("act", mtok)
                        etk = kb.mark("act", act.copy(out=H[:, part, 1 + t4 * 512:1 + (t4 + 1) * 512], in_=pb[:, :]))
                        ps_ring.free[pi] = [etk]
                    ev_toks.append(etk)
                last_mm = mtok
                ci, C, fc = C_ring.next()
                ia, ib = c, 22 + c
                kb.wait("act", cp_tok)
                for tk in fc:
                    kb.wait("act", tk)
                a1 = kb.mark("act", act.activation(out=C[:, 0, :], in_=H[:, 0, 1:HW + 1], func=AF.Identity, bias=cp[:, ia, 3:4], scale=cp[:, ia, 1:2]))
                kb.wait("pool", ev_toks[1])
                kb.wait("pool", cp_tok)
                for tk in fc:
                    kb.wait("pool", tk)
                p1 = kb.mark("pool", pool.tensor_scalar(out=C[:, 1, :], in0=H[:, 1, 1:HW + 1], scalar1=cp[:, ib, 1:2], scalar2=cp[:, ib, 3:4], op0=ALU.mult, op1=ALU.add))
                kb.wait("dve", a1)
                kb.wait("dve", cp_tok)
                dve.scalar_tensor_tensor(out=C[:, 0, :], in0=H[:, 0, 0:HW], scalar=cp[:, ia, 0:1], in1=C[:, 0, :], op0=ALU.mult, op1=ALU.add)
                d1 = kb.mark("dve", dve.scalar_tensor_tensor(out=C[:, 0, :], in0=H[:, 0, 2:HW + 2], scalar=cp[:, ia, 2:3], in1=C[:, 0, :], op0=ALU.mult, op1=ALU.add))
                kb.wait("dve", p1)
                dve.scalar_tensor_tensor(out=C[:, 1, :], in0=H[:, 1, 0:HW], scalar=cp[:, ib, 0:1], in1=C[:, 1, :], op0=ALU.mult, op1=ALU.add)
                d2 = kb.mark("dve", dve.scalar_tensor_tensor(out=C[:, 1, :], in0=H[:, 1, 2:HW + 2], scalar=cp[:, ib, 2:3], in1=C[:, 1, :], op0=ALU.mult, op1=ALU.add))
                H_ring.free[hi] = [d1, d2]
                kb.wait("act", d1)
                g1 = kb.mark("act", act.activation(out=C[:, 0, :], in_=C[:, 0, :], func=AF.Gelu))
                ui, U, fu = U_ring.next()
                kb.wait("pool", g1)
                kb.wait("pool", d2)
                for tk in fu:
                    kb.wait("pool", tk)
                utok = kb.mark("pool", pool.tensor_tensor(out=U[:, :], in0=C[:, 0, :], in1=C[:, 1, :], op=ALU.mult))
                C_ring.free[ci] = [utok]
                kb.wait("sp", utok)
                st = kb.dma("sp", U_T.ap()[c, :, T0:T0 + HW], U[:, :], U_sem[ui])
                U_ring.free[ui] = [st]
            w_ring.free[wi] = [last_mm]
        return U_ring.free[0] + U_ring.free[1]

    def phase_p2b(l, last):
        dve, act, pool, pe = nc.vector, nc.scalar, nc.gpsimd, nc.tensor
        s_w = kb.sem("p2b_w")
        Wd = kb.sb("p2b_Wd", [128, 22, D], BF16)
        Wg = kb.sb("p2b_Wg", [128, 8, D], BF16)
        Wp = kb.sb("p2b_Wp", [128, 2, D], BF16)
        wd_tok, wg_tok = {}, {}
        for qi, q in enumerate(range(0, 22, 4)):
            n_ = min(4, 22 - q)
            tk = load_w_cast(Wd[:, q:q + n_, :], w_down.ap()[l, q * 128:(q + n_) * 128, :].rearrange("(k p) c -> p k c", p=128), kb.sem("p2b_wd%d" % qi))
            for k in range(q, q + n_):
                wd_tok[k] = tk
        for hf in range(2):
            tk = load_w_cast(Wg[:, hf * 4:(hf + 1) * 4, :], w_pg.ap()[l, hf * 512:(hf + 1) * 512, :].rearrange("(k p) c -> p k c", p=128), kb.sem("p2b_wg%d" % hf))
            for k in range(hf * 4, hf * 4 + 4):
                wg_tok[k] = tk
        wtok = load_w_cast(Wp[:, :, :], w_pp.ap()[l].rearrange("(k p) c -> p k c", p=128), s_w)
        ln = LN("ln2", ln2_g, ln2_b, l, [ptb[1]])
        in_ring = Ring([dict(u=kb.sb("p2b_u%d" % i, [128, 22, 128], BF16), xt=kb.sb("p2b_xt%d" % i, [128, 8, 128], BF16),
                             x1=kb.sb("p2b_x1%d" % i, [128, D], F32), pb=kb.sb("p2b_pb%d" % i, [128, PLE], BF16)) for i in range(3)])
        in_sem = [kb.sem("p2b_ins%d" % i) for i in range(3)]
        pin_sem = [kb.sem("p2b_pins%d" % i) for i in range(3)]
        pT_ring = Ring([kb.sb("p2b_pT%d" % i, [128, 2, 128], BF16) for i in range(2)])
        sg_ring = Ring([kb.sb("p2b_sg%d" % i, [128, D], F32) for i in range(2)])
        pre_ring = Ring([kb.sb("p2b_pre%d" % i, [128, D], F32) for i in range(3)])
        acc_free = [[], []]
        tp_free = []
        out_toks = {}

        def load_chunk(c):
            i, bufs, fr = in_ring.next()
            for tk in fr:
                kb.wait("sp", tk)
                kb.wait("pool", tk)
            sl = slice(c * 128, (c + 1) * 128)
            kb.dma("sp", bufs["u"][:, :, :], U_T.ap()[:, :, sl].rearrange("k p t -> p k t"), in_sem[i])
            kb.dma("sp", bufs["xt"][:, :, :], X1T_D.ap()[:, :, sl].rearrange("k p t -> p k t"), in_sem[i])
            tk = kb.dma("sp", bufs["x1"][:, :], X1.ap()[sl, :], in_sem[i])
            ptk = kb.dma("pool", bufs["pb"][:, :], p_in.ap()[l, sl, :], pin_sem[i])
            return i, bufs, tk, ptk

        def stage_P(c, ch):
            nonlocal tp_free
            pti, pT, fpt = pT_ring.next()
            kb.wait("pe", ch["lptok"])
            kb.wait("pe", tok_ident)
            for tk in tp_free:
                kb.wait("pe", tk)
            TPB = ptb[0]
            for k2 in range(2):
                ins = pe.transpose(TPB[:, k2 * 128:(k2 + 1) * 128], ch["bufs"]["pb"][:, k2 * 128:(k2 + 1) * 128], ident_b[:, :])
            ttok = kb.mark("pe", ins)
            kb.wait("act", ttok)
            for tk in fpt:
                kb.wait("act", tk)
            ctok = kb.mark("act", act.copy(out=pT[:, :, :], in_=TPB[:, 0:256].rearrange("p (k t) -> p k t", k=2)))
            tp_free = [ctok]
            ch.update(pT=pT, pti=pti, ctok=ctok, ttok=ttok)

        def stage_M(c, ch):
            bufs, pT = ch["bufs"], ch["pT"]
            kb.wait("pe", ch["ltok"])
            kb.wait("pe", ch["ctok"])
            ch["mtok"] = []
            for half in range(2):
                cs = slice(half * 512, (half + 1) * 512)
                by, bg, bp = psb[3 * half], psb[3 * half + 1], psb[3 * half + 2]
                for tk in acc_free[half]:
                    kb.wait("pe", tk)
                for k in range(22):
                    kb.wait("pe", wd_tok[k])
                    pe.matmul(by[:, :], lhsT=bufs["u"][:, k, :], rhs=Wd[:, k, cs], start=(k == 0), stop=(k == 21))
                for k in range(8):
                    kb.wait("pe", wg_tok[k])
                    pe.matmul(bg[:, :], lhsT=bufs["xt"][:, k, :], rhs=Wg[:, k, cs], start=(k == 0), stop=(k == 7))
                kb.wait("pe", wtok)
                for k in range(2):
                    ins = pe.matmul(bp[:, :], lhsT=pT[:, k, :], rhs=Wp[:, k, cs], start=(k == 0), stop=(k == 1))
                ch["mtok"].append(kb.mark("pe", ins))
            pT_ring.free[ch["pti"]] = [ch["mtok"][1]]

        def stage_E(c, ch):
            bufs = ch["bufs"]
            si, sg, fs = sg_ring.next()
            pi, pre, fp = pre_ring.next()
            for half in range(2):
                cs = slice(half * 512, (half + 1) * 512)
                by, bg, bp = psb[3 * half], psb[3 * half + 1], psb[3 * half + 2]
                kb.wait("act", ch["mtok"][half])
                if half == 0:
                    for tk in fs:
                        kb.wait("act", tk)
                stok = kb.mark("act", act.activation(out=sg[:, cs], in_=bg[:, :], func=AF.Sigmoid))
                kb.wait("dve", stok)
                kb.wait("dve", ch["ltok"])
                if half == 0:
                    for tk in fp:
                        kb.wait("dve", tk)
                dve.tensor_tensor(out=sg[:, cs], in0=sg[:, cs], in1=bp[:, :], op=ALU.mult)
                atk = kb.mark("dve", dve.tensor_tensor(out=sg[:, cs], in0=sg[:, cs], in1=by[:, :], op=ALU.add))
                acc_free[half] = [atk]
                ptk = kb.mark("dve", dve.scalar_tensor_tensor(out=pre[:, cs], in0=bufs["x1"][:, cs], scalar=ALPHA, in1=sg[:, cs], op0=ALU.mult, op1=ALU.add))
            sg_ring.free[si] = [ptk]
            in_ring.free[ch["ii"]] = [ch["mtok"][1], ptk, ch["ttok"]]
            ch["hnd"] = ln.stats(pre, ptk)
            ch["pi"] = pi

        def stage_F(c, ch):
            sl = slice(c * 128, (c + 1) * 128)
            if last:
                ntok, toks = ln.finish(ch["hnd"], out_d.ap()[sl, :], None)
            else:
                ntok, toks = ln.finish(ch["hnd"], XA.ap()[sl, :], XT_D.ap()[:, :, sl].rearrange("k p t -> p k t"))
            pre_ring.free[ch["pi"]] = [ntok]
            for t_ in toks:
                out_toks[id(t_[0])] = t_

        def mk(c):
            ii, bufs, ltok, lptok = load_chunk(c)
            return dict(ii=ii, bufs=bufs, ltok=ltok, lptok=lptok)

        chs = {0: mk(0), 1: mk(1)}
        stage_P(0, chs[0])
        for c in range(NCH):
            if c + 1 < NCH:
                stage_P(c + 1, chs[c + 1])
            stage_M(c, chs[c])
            if c >= 1:
                stage_F(c - 1, chs[c - 1])
                del chs[c - 1]
            stage_E(c, chs[c])
            if c + 2 < NCH:
                chs[c + 2] = mk(c + 2)
        stage_F(NCH - 1, chs[NCH - 1])
        return list(out_toks.values())

    with kb.scope():
        toks = phase_p0()
        barrier(toks)
    for l in range(n_layers):
        with kb.scope():
            toks = phase_p1a(l)
            barrier(toks)
        if stop == "p1a":
            break
        with kb.scope():
            toks = phase_na(l)
            barrier(toks)
        if stop in ("na", "na_ex"):
            break
        with kb.scope():
            w1d = p1d_weights(l)
            with kb.scope():
                toks = phase_ret(l)
                barrier(toks)
            if stop == "ret":
                break
            with kb.scope():
                toks = phase_p1d(l, w1d)
                barrier(toks)
        if stop == "p1d":
            break
        with kb.scope():
            toks = phase_p2a(l)
            barrier(toks)
        if stop == "p2a":
            break
        with kb.scope():
            toks = phase_p2b(l, last=(l == DEPTH - 1))
            barrier(toks)
        if stop == "p2b" and l == debug.get("stop_layer", 0):
            break

    s_fin = kb.sem("fin")
    fin = []
    for name in debug.get("dump", []):
        src = kb.dram[name]
        shp = list(src.shape)
        o = kb.dout("dbg_" + name, shp, src.dtype)
        fin.append(kb.dma("sp", o.ap(), src.ap(), s_fin))
    if not debug:
        pass
    for tk in fin[-1:]:
        kb.wait("sp", tk, force=True)
    return kb


def rope_tables(core):
    half = core % 2
    pos = (np.arange(NT, dtype=np.float64) + half * NT)
    inv = 10000.0 ** (-np.arange(64, dtype=np.float64) / 64.0)
    ang = (pos[:, None].astype(np.float32) * inv[None, :].astype(np.float32)).astype(np.float64)
    cos = np.cos(ang).T
    sin = np.sin(ang).T
    s = 128.0 ** -0.5
    cosf = np.concatenate([cos, cos], 0)
    sinsw = np.concatenate([sin, -sin], 0)
    return np.stack([cosf, sinsw, cosf * s, sinsw * s]).astype(np.float32)


def na_tables(rpb_l, parity):
    kc = np.arange(64)[:, None]
    c = np.arange(64)[None, :]
    cs = np.clip(c - 8, 0, 48)
    inwin = (kc >= cs) & (kc < cs + 16)
    off = np.clip(kc - c + 15, 0, 30)
    bi = np.full((8, 15, 64, 64), NEG, np.float32)
    for ro in range(15):
        g = rpb_l[:, ro][:, off]
        bi[:, 14 - ro] = np.where(inwin[None], g, np.float32(NEG))
    bb = np.full((8, NA_NSLOT, 64, 64), NEG, np.float32)
    for (kr, r), sl in NA_SLOT.items():
        if na_valid(parity, r, kr):
            bb[:, sl] = bi[:, 14 - (kr - r + 7)]
    def pack(a):
        n = a.shape[1]
        a = a.reshape(4, 2, n, 64, 64).transpose(0, 1, 3, 2, 4)
        return np.ascontiguousarray(a.reshape(4, 128, n * 64))
    return pack(bi), pack(bb)


def core_inputs(inputs, c, names):
    b, h = c // 2, c % 2
    sl = slice(h * NT, (h + 1) * NT)
    m = {}
    for n in names:
        if n == "x":
            m[n] = np.ascontiguousarray(inputs["x"][b, sl])
        elif n == "p":
            m[n] = np.ascontiguousarray(inputs["p"][:, b, sl])
        elif n == "rope":
            m[n] = rope_tables(c)
        elif n == "ident":
            m[n] = np.eye(128, dtype=np.float32)
        elif n == "rconst":
            i = np.arange(128)
            a1 = np.maximum(i[None, :] - i[:, None], 0)
            a2 = np.maximum(i[:, None] - i[None, :], 0)
            idx1 = np.broadcast_to(i[None, :] + 1, (128, 128))
            idx2 = np.broadcast_to(128 - i[None, :], (128, 128))
            pidx = np.stack([127 - i, i], 1)
            nidx = np.broadcast_to(128 * (31 - np.arange(32))[None, :], (128, 32))
            m[n] = np.ascontiguousarray(np.concatenate([a1, a2, idx1, idx2, pidx, nidx], 1).astype(np.float32))
        elif n == "convp":
            cw = inputs["conv_w"].reshape(DEPTH, 3, 44, 128)
            cb = inputs["conv_b"].reshape(DEPTH, 1, 44, 128)
            m[n] = np.ascontiguousarray(np.concatenate([cw, cb], 1).transpose(0, 3, 2, 1))
        elif n == "par":
            m[n] = np.ascontiguousarray(np.broadcast_to(np.array([h, 1 - h], np.float32)[None, :], (128, 2)))
        elif n == "na_bi":
            tabs = [na_tables(inputs["na_rpb"][l], h) for l in range(DEPTH)]
            m["na_bi"] = np.stack([t[0] for t in tabs])
            m["na_bb"] = np.stack([t[1] for t in tabs])
        elif n == "na_bb":
            pass
        else:
            m[n] = np.ascontiguousarray(inputs[n])
    return m


INPUT_NAMES = ["x", "p", "w_in", "rope", "ident", "na_bi", "na_bb", "ret_decay_f", "ret_decay_b", "rconst", "par",
               "w_branch_a", "w_branch_b", "w_out", "ln1_g", "ln1_b", "ln2_g", "ln2_b", "w_up", "convp", "w_down",
               "w_ple_gate", "w_ple_proj"]


_PROG = {}


def kernel(**inputs):
    inputs = {k: np.asarray(v) for k, v in inputs.items()}
    if "kb" not in _PROG:
        _PROG["kb"] = build_program(n_layers=DEPTH)
    kb = _PROG["kb"]
    names = [n for n in INPUT_NAMES if n in kb.dram]
    in_maps = [core_inputs(inputs, c, names) for c in range(8)]
    res = run_bass_kernel_spmd(kb.nc, in_maps, core_ids=list(range(8)))
    out = np.empty((4, 2 * NT, D), np.float32)
    for c in range(8):
        out[c // 2, (c % 2) * NT:(c % 2 + 1) * NT] = np.asarray(res.results[c]["out"], dtype=np.float32)
    return out
```

```python
import contextlib
import numpy as np
import ml_dtypes
import concourse.bass as bass
import concourse.mybir as mybir
from concourse.bass_utils import run_bass_kernel_spmd

F32 = mybir.dt.float32
BF16 = mybir.dt.bfloat16
AF = mybir.ActivationFunctionType
ALU = mybir.AluOpType

D = 1024
NT = 4096
NCH = 32
DEPTH = 4
DIN = 6656
DFF = 2816
PLE = 256
ALPHA = (2.0 * DEPTH) ** 0.25
LN_EPS = 1e-5
GN_EPS = 1e-6
NEG = -30000.0
PAIRS = [[0, 1], [2, 3], [4, 5], [6, 7]]


def na_superset(r):
    if r <= 3:
        return list(range(r - 4, 8))
    if r >= 61:
        return list(range(56, r + 4))
    return list(range(r - 4, r + 4))


def na_valid(parity, r, kr):
    R = r + 64 * parity
    rs = min(max(R - 4, 0), 120)
    KR = kr + 64 * parity
    return rs <= KR <= rs + 7


NA_BND = (0, 1, 2, 3, 61, 62, 63)
NA_QR = {kr: [r for r in range(64) if kr in na_superset(r)] for kr in range(-4, 67)}
NA_SLOT = {}
for _kr in range(-4, 67):
    for _r in NA_QR[_kr]:
        if _r in NA_BND:
            NA_SLOT[(_kr, _r)] = len(NA_SLOT)
NA_NSLOT = len(NA_SLOT)


class Sem:
    def __init__(self, kb, name):
        self.h = kb.es.enter_context(kb.nc.semaphore(name))
        self.n = 0


class KB:
    def __init__(self):
        self.nc = bass.Bass("TRN2", target_bir_lowering=False)
        self.es = contextlib.ExitStack()
        self.cur = self.es
        self.sems = {}
        nc = self.nc
        self.eng = {"pe": nc.tensor, "act": nc.scalar, "dve": nc.vector, "pool": nc.gpsimd, "sp": nc.sync}
        self.prog = {e: Sem(self, "prog_" + e) for e in self.eng}
        self.waited = {}
        self.nsem = len(self.eng)
        self.dram = {}

    def uniq(self, name):
        self.ucnt = getattr(self, "ucnt", 0) + 1
        return "%s_u%d" % (name, self.ucnt)

    def sb(self, name, shape, dt=F32):
        return self.cur.enter_context(self.nc.sbuf_tensor(self.uniq(name), list(shape), dt))

    def ps(self, name, shape, dt=F32):
        return self.cur.enter_context(self.nc.psum_tensor(self.uniq(name), list(shape), dt))

    def sem(self, name):
        if name not in self.sems:
            self.nsem += 1
            self.sems[name] = Sem(self, name)
        return self.sems[name]

    @contextlib.contextmanager
    def scope(self):
        old = self.cur
        with contextlib.ExitStack() as st:
            self.cur = st
            yield
        self.cur = old

    def din(self, name, shape, dt=F32):
        t = self.nc.dram_tensor(name, list(shape), dt, kind="ExternalInput")
        self.dram[name] = t
        return t

    def dout(self, name, shape, dt=F32):
        t = self.nc.dram_tensor(name, list(shape), dt, kind="ExternalOutput")
        self.dram[name] = t
        return t

    def dscr(self, name, shape, dt=BF16):
        t = self.nc.dram_tensor(name, list(shape), dt)
        self.dram[name] = t
        return t

    def mark(self, e, ins):
        s = self.prog[e]
        ins.then_inc(s.h, 1)
        s.n += 1
        return (s, s.n)

    def wait(self, e, tok, force=False):
        if tok is None:
            return
        s, v = tok
        if v <= 0:
            return
        if s is self.prog[e] and not force:
            return
        key = (e, id(s))
        if self.waited.get(key, 0) >= v:
            return
        self.waited[key] = v
        self.eng[e].wait_ge(s.h, v)

    def dma(self, e, out, in_, sem, **kw):
        ins = self.eng[e].dma_start(out=out, in_=in_, **kw)
        ins.then_inc(sem.h, 16)
        sem.n += 16
        return (sem, sem.n)


class Ring:
    def __init__(self, bufs):
        self.bufs = bufs
        self.n = len(bufs)
        self.k = 0
        self.free = [[] for _ in bufs]

    def next(self):
        i = self.k % self.n
        self.k += 1
        fr = self.free[i]
        self.free[i] = []
        return i, self.bufs[i], fr


def build_program(n_layers=DEPTH, debug=None, pairs=PAIRS):
    kb = KB()
    nc = kb.nc
    debug = debug or {}
    stop = debug.get("stop")

    x_in = kb.din("x", [NT, D])
    p_in = kb.din("p", [DEPTH, NT, PLE])
    w_in = kb.din("w_in", [DEPTH, D, DIN])
    rope = kb.din("rope", [4, 128, NT])
    ident_d = kb.din("ident", [128, 128])
    w_ba = kb.din("w_branch_a", [DEPTH, 512, D])
    w_bb = kb.din("w_branch_b", [DEPTH, 1024, D])
    w_out = kb.din("w_out", [DEPTH, D, D])
    ln1_g = kb.din("ln1_g", [DEPTH, D])
    ln1_b = kb.din("ln1_b", [DEPTH, D])
    ln2_g = kb.din("ln2_g", [DEPTH, D])
    ln2_b = kb.din("ln2_b", [DEPTH, D])
    w_up = kb.din("w_up", [DEPTH, D, 2 * DFF])
    convp = kb.din("convp", [DEPTH, 128, 44, 4])
    w_down = kb.din("w_down", [DEPTH, DFF, D])
    w_pg = kb.din("w_ple_gate", [DEPTH, D, D])
    w_pp = kb.din("w_ple_proj", [DEPTH, PLE, D])
    dec_f = kb.din("ret_decay_f", [DEPTH, 4])
    dec_b = kb.din("ret_decay_b", [DEPTH, 4])
    rconst = kb.din("rconst", [128, 4 * 128 + 2 + 32])
    par_d = kb.din("par", [128, 2])
    na_bi = kb.din("na_bi", [DEPTH, 4, 128, 15 * 64])
    na_bb = kb.din("na_bb", [DEPTH, 4, 128, NA_NSLOT * 64])
    out_d = kb.dout("out", [NT, D])

    XT_D = kb.dscr("XT_D", [8, 128, NT])
    XA = kb.dscr("XA", [NT, D], F32)
    QNA_T = kb.dscr("QNA_T", [4, 128, NT])
    KNA_T = kb.dscr("KNA_T", [4, 128, NT])
    VNA = kb.dscr("VNA", [NT, 512])
    QR_T = kb.dscr("QR_T", [4, 128, NT])
    KR_T = kb.dscr("KR_T", [4, 128, NT])
    VR = kb.dscr("VR", [NT, 1024])
    SG = kb.dscr("SG", [NT, 1024])
    GA_T = kb.dscr("GA_T", [8, 128, NT])
    GB_T = kb.dscr("GB_T", [8, 128, NT])
    NAB = kb.dscr("NAB", [4, 131072])
    NAG = kb.dscr("NAG", [8, 131072])
    YA_T = kb.dscr("YA_T", [4, 128, NT])
    YB_T = kb.dscr("YB_T", [8, 128, NT])
    X1 = kb.dscr("X1", [NT, D], F32)
    X1T_D = kb.dscr("X1T_D", [8, 128, NT])
    U_T = kb.dscr("U_T", [22, 128, NT])
    CXB = kb.dscr("CXB", [2, 1024])
    CXG = kb.dscr("CXG", [4, 1024])
    RSB = kb.dscr("RSB", [2, 131072], F32)
    RSG = kb.dscr("RSG", [4, 131072], F32)

    ident_f = kb.sb("ident_f", [128, 128], F32)
    ident_b = kb.sb("ident_b", [128, 128], BF16)
    psb = [kb.ps("psb%d" % i, [128, 512], F32) for i in range(6)]
    ptb = [kb.ps("ptb%d" % i, [128, 1024], BF16) for i in range(2)]

    s_misc = kb.sem("misc")
    t = kb.dma("sp", ident_f[:, :], ident_d.ap()[:, :], s_misc)
    kb.wait("dve", t)
    tok_ident = kb.mark("dve", nc.vector.tensor_copy(out=ident_b[:, :], in_=ident_f[:, :]))

    def barrier(toks):
        for e in kb.eng:
            for tk in toks:
                kb.wait(e, tk)

    pend = []

    def transpose_chunk(src_bf, src_tok, dst_ap, pbank, pbank_free, evac_e):
        kb.wait("pe", src_tok)
        for tk in pbank_free:
            kb.wait("pe", tk)
        kb.wait("pe", tok_ident)
        pv = pbank
        ins = None
        for kc in range(8):
            ins = nc.tensor.transpose(pv[:, kc * 128:(kc + 1) * 128], src_bf[:, kc * 128:(kc + 1) * 128], ident_b[:, :])
        pt = kb.mark("pe", ins)
        kb.wait(evac_e, pt)
        src = pv[:, 0:1024].rearrange("p (k t) -> p k t", k=8)
        if evac_e == "act":
            ins = nc.scalar.copy(out=dst_ap, in_=src)
        else:
            ins = nc.vector.tensor_copy(out=dst_ap, in_=src)
        et = kb.mark(evac_e, ins)
        return pt, et

    def phase_p0():
        xb_ring = Ring([kb.sb("p0_xb%d" % i, [128, D], BF16) for i in range(2)])
        xb_sem = [kb.sem("p0_xbs%d" % i) for i in range(2)]
        xt_ring = Ring([kb.sb("p0_xt%d" % i, [128, 8, 128], BF16) for i in range(2)])
        xt_sem = [kb.sem("p0_xts%d" % i) for i in range(2)]
        pfree = [[], []]
        toks = []
        s_xa = kb.sem("p0_xa")
        for q in range(4):
            toks.append(kb.dma("sp", XA.ap()[q * 1024:(q + 1) * 1024, :], x_in.ap()[q * 1024:(q + 1) * 1024, :], s_xa))
        toks = toks[-1:]
        for c in range(NCH):
            i, xb, fr = xb_ring.next()
            for tk in fr:
                kb.wait("pool", tk)
            lt = kb.dma("pool", xb[:, :], x_in.ap()[c * 128:(c + 1) * 128, :], xb_sem[i])
            j, xt, fr2 = xt_ring.next()
            e = "act" if c % 2 == 0 else "dve"
            for tk in fr2:
                kb.wait(e, tk)
            pt, et = transpose_chunk(xb, lt, xt[:, :, :], ptb[c % 2], pfree[c % 2], e)
            pfree[c % 2] = [et]
            xb_ring.free[i] = [pt]
            kb.wait("sp", et)
            st = kb.dma("sp", XT_D.ap()[:, :, c * 128:(c + 1) * 128].rearrange("k p t -> p k t"), xt[:, :, :], xt_sem[j])
            xt_ring.free[j] = [st]
            toks.append(st)
        return toks

    shared = {}

    def na_exchange(after):
        s_cc = kb.sem("cc")
        s_ex = kb.sem("na_ex")
        for tk in after:
            kb.wait("sp", tk)
        kb.dma("sp", NAB.ap()[0, :].rearrange("(a p t) -> a p t", a=4, p=128), KNA_T.ap()[:, :, 0:256], s_ex)
        kb.dma("sp", NAB.ap()[1, :].rearrange("(a p t) -> a p t", a=4, p=128), KNA_T.ap()[:, :, NT - 256:NT], s_ex)
        kb.dma("sp", NAB.ap()[2, :].rearrange("(t c) -> t c", c=512), VNA.ap()[0:256, :], s_ex)
        t = kb.dma("sp", NAB.ap()[3, :].rearrange("(t c) -> t c", c=512), VNA.ap()[NT - 256:NT, :], s_ex)
        kb.wait("pool", t)
        ins = nc.gpsimd.collective_compute("AllGather", ALU.bypass, replica_groups=pairs,
                                           ins=[NAB.ap().opt()], outs=[NAG.ap().opt()])
        ins.then_inc(s_cc.h)
        s_cc.n += 1
        return (s_cc, s_cc.n)

    def phase_p1a(l):
        XT = kb.sb("p1a_XT", [128, 8, NT], BF16)
        s_xt = kb.sem("p1a_xt")
        lt = None
        for kc in range(8):
            lt = kb.dma("sp", XT[:, kc, :], XT_D.ap()[kc, :, :], s_xt)
        xt_tok = lt
        wg_ring = Ring([kb.sb("p1a_wg%d" % i, [128, 8, 512], BF16) for i in range(2)])
        wg_sem = [kb.sem("p1a_wgs%d" % i) for i in range(2)]
        rp_ring = Ring([kb.sb("p1a_rp%d" % i, [128, 2, 512], F32) for i in range(2)])
        rp_sem = [kb.sem("p1a_rps%d" % i) for i in range(2)]
        st_ring = Ring([kb.sb("p1a_st%d" % i, [128, 512], BF16) for i in range(4)])
        st_sem = [kb.sem("p1a_sts%d" % i) for i in range(4)]
        tmp_ring = Ring([kb.sb("p1a_tmp%d" % i, [128, 2, 512], F32) for i in range(2)])
        ps_ring = Ring(psb[0:4])
        out_toks = []
        ev_alt = [0]

        def load_w(g):
            i, wg, fr = wg_ring.next()
            for tk in fr:
                kb.wait("pool", tk)
            src = w_in.ap()[l, :, g * 512:(g + 1) * 512].rearrange("(k p) c -> p k c", p=128)
            return wg, kb.dma("pool", wg[:, :, :], src, wg_sem[i]), i

        groups = list(range(13))
        nxt = load_w(groups[0])
        for gi, g in enumerate(groups):
            wg, wtok, wi = nxt
            if gi + 1 < len(groups):
                nxt = load_w(groups[gi + 1])
            fm = g in (0, 1, 3, 4, 9, 10, 11, 12)
            last_pe = None
            if g == 5 and stop != "p1a":
                shared["na_cc"] = na_exchange([tk for fr in st_ring.free for tk in fr])
            for tile in range(32):
                if fm:
                    if g in (3, 4):
                        tt, fb = tile // 4, tile % 4
                    else:
                        fb, tt = tile // 8, tile % 8
                if g in (3, 4) and fb == 0:
                    def ld_rope(tt_):
                        ri, rp, fr = rp_ring.next()
                        for tk in fr:
                            kb.wait("sp", tk)
                        base = 0 if g == 3 else 2
                        rtok = kb.dma("sp", rp[:, :, :], rope.ap()[base:base + 2, :, tt_ * 512:(tt_ + 1) * 512].rearrange("a p t -> p a t"), rp_sem[ri])
                        return (rp, rtok, ri)
                    if tt == 0:
                        rp_nxt = ld_rope(0)
                    rp_cur = rp_nxt
                    if tt + 1 < 8:
                        rp_nxt = ld_rope(tt + 1)
                pi, pb, pfr = ps_ring.next()
                for tk in pfr:
                    kb.wait("pe", tk)
                kb.wait("pe", wtok)
                kb.wait("pe", xt_tok)
                ins = None
                for kc in range(8):
                    if fm:
                        ins = nc.tensor.matmul(pb[:, :], lhsT=wg[:, kc, fb * 128:(fb + 1) * 128], rhs=XT[:, kc, tt * 512:(tt + 1) * 512],
                                               start=(kc == 0), stop=(kc == 7))
                    else:
                        ins = nc.tensor.matmul(pb[:, :], lhsT=XT[:, kc, tile * 128:(tile + 1) * 128], rhs=wg[:, kc, :],
                                               start=(kc == 0), stop=(kc == 7))
                ptok = kb.mark("pe", ins)
                last_pe = ptok
                si, stg, sfr = st_ring.next()
                if g in (3, 4):
                    rp, rtok, ri = rp_cur
                    ti, tmp, tfr = tmp_ring.next()
                    kb.wait("dve", ptok)
                    kb.wait("dve", rtok)
                    for tk in tfr:
                        kb.wait("dve", tk)
                    nc.vector.tensor_tensor(out=tmp[:, 0, :], in0=pb[:, :], in1=rp[:, 0, :], op=ALU.mult)
                    nc.vector.tensor_tensor(out=tmp[0:64, 1, :], in0=pb[64:128, :], in1=rp[64:128, 1, :], op=ALU.mult)
                    ins = nc.vector.tensor_tensor(out=tmp[64:128, 1, :], in0=pb[0:64, :], in1=rp[0:64, 1, :], op=ALU.mult)
                    dtok = kb.mark("dve", ins)
                    ps_ring.free[pi] = [dtok]
                    if fb == 3:
                        rp_ring.free[ri] = [dtok]
                    kb.wait("pool", dtok)
                    for tk in sfr:
                        kb.wait("pool", tk)
                    ins = nc.gpsimd.tensor_tensor(out=stg[:, :], in0=tmp[:, 0, :], in1=tmp[:, 1, :], op=ALU.add)
                    etok = kb.mark("pool", ins)
                    tmp_ring.free[ti] = [etok]
                else:
                    if g in (7, 8, 9, 10, 11, 12, 0):
                        e = "act"
                    elif g in (5, 6):
                        e = "act" if tile % 2 == 0 else "dve"
                    else:
                        e = "dve"
                    kb.wait(e, ptok)
                    for tk in sfr:
                        kb.wait(e, tk)
                    if e == "act":
                        if g == 0:
                            ins = nc.scalar.mul(out=stg[:, :], in_=pb[:, :], mul=0.125)
                        elif g in (7, 8):
                            ins = nc.scalar.activation(out=stg[:, :], in_=pb[:, :], func=AF.Silu)
                        elif g in (9, 10, 11, 12):
                            ins = nc.scalar.activation(out=stg[:, :], in_=pb[:, :], func=AF.Sigmoid)
                        else:
                            ins = nc.scalar.copy(out=stg[:, :], in_=pb[:, :])
                    else:
                        ins = nc.vector.tensor_copy(out=stg[:, :], in_=pb[:, :])
                    etok = kb.mark(e, ins)
                    ps_ring.free[pi] = [etok]
                if g == 0:
                    dst = QNA_T.ap()[fb, :, tt * 512:(tt + 1) * 512]
                elif g == 1:
                    dst = KNA_T.ap()[fb, :, tt * 512:(tt + 1) * 512]
                elif g == 2:
                    dst = VNA.ap()[tile * 128:(tile + 1) * 128, :]
                elif g == 3:
                    dst = QR_T.ap()[fb, :, tt * 512:(tt + 1) * 512]
                elif g == 4:
                    dst = KR_T.ap()[fb, :, tt * 512:(tt + 1) * 512]
                elif g in (5, 6):
                    dst = VR.ap()[tile * 128:(tile + 1) * 128, (g - 5) * 512:(g - 4) * 512]
                elif g in (7, 8):
                    dst = SG.ap()[tile * 128:(tile + 1) * 128, (g - 7) * 512:(g - 6) * 512]
                elif g in (9, 10):
                    dst = GA_T.ap()[(g - 9) * 4 + fb, :, tt * 512:(tt + 1) * 512]
                else:
                    dst = GB_T.ap()[(g - 11) * 4 + fb, :, tt * 512:(tt + 1) * 512]
                kb.wait("sp", etok)
                stok = kb.dma("sp", dst, stg[:, :], st_sem[si])
                st_ring.free[si] = [stok]
            wg_ring.free[wi] = [last_pe]
        for si in range(4):
            out_toks += st_ring.free[si]
        return out_toks


    def phase_na(l):
        if "na_cc" in shared:
            cc_tok = shared.pop("na_cc")
        else:
            cc_tok = na_exchange([])
        kb.wait("sp", cc_tok)
        if stop == "na_ex":
            return [cc_tok]

        NK = 71
        bi = kb.sb("na_bi", [128, 4, 15 * 64], F32)
        s_bi = kb.sem("na_bis")
        for hp in range(4):
            bi_tok = kb.dma("sp", bi[:, hp, :], na_bi.ap()[l, hp, :, :], s_bi)
        bb_ring = [kb.sb("na_bb%d" % i, [128, NA_NSLOT * 64], F32) for i in range(2)]
        kt_ring = [kb.sb("na_kt%d" % i, [128, 72, 128], BF16) for i in range(2)]
        vt_ring = [kb.sb("na_vt%d" % i, [128, 72, 130], BF16) for i in range(2)]
        qt_ring = [kb.sb("na_qt%d" % i, [128, NT], BF16) for i in range(2)]
        ld_sem = [kb.sem("na_lds%d" % i) for i in range(2)]
        ones_tok = [None, None]
        for i in range(2):
            ktk = kb.mark("dve", nc.vector.memset(kt_ring[i][:, :, :], 0.0))
            zt = kb.mark("pool", nc.gpsimd.memset(vt_ring[i][:, :, :], 0.0))
            kb.wait("pool", zt, force=True)
            nc.gpsimd.memset(vt_ring[i][0:64, :, 64:65], 1.0)
            ones_tok[i] = (ktk, kb.mark("pool", nc.gpsimd.memset(vt_ring[i][64:128, :, 129:130], 1.0)))
        z_ring = Ring([kb.sb("na_z%d" % i, [128, 768], F32) for i in range(2)])
        pt_ring = [kb.sb("na_pt%d" % i, [128, 768], BF16) for i in range(16)]
        pt_free = [[] for _ in range(16)]
        rs_ring = Ring([kb.sb("na_rs%d" % i, [64, 6], F32) for i in range(2)])
        ya_ring = Ring([kb.sb("na_ya%d" % i, [64, 6, 64], BF16) for i in range(2)])
        stg_ring = Ring([kb.sb("na_stg%d" % i, [128, 192], BF16) for i in range(3)])
        stg_sem = [kb.sem("na_stgs%d" % i) for i in range(3)]
        s_tiles = [kb.ps("na_s%d" % i, [128, 1024], F32) for i in range(2)] if False else None
        s_ring = Ring([(psb[0], psb[1]), (psb[2], psb[3])])
        acc_ring = Ring([psb[4], psb[5]])
        tp_ring = Ring(ptb)

        def load_hp(hp, slot, free_toks):
            for tk in free_toks:
                kb.wait("sp", tk)
            kt, vt, qt, bb, sm = kt_ring[slot], vt_ring[slot], qt_ring[slot], bb_ring[slot], ld_sem[slot]
            kb.wait("sp", ones_tok[slot][0])
            kb.wait("sp", ones_tok[slot][1])
            kb.dma("sp", qt[:, :], QNA_T.ap()[hp, :, :], sm)
            kb.dma("sp", bb[:, :], na_bb.ap()[l, hp, :, :], sm)
            ktop = NAG.ap()[1, :].rearrange("(a p t) -> a p t", a=4, p=128)[hp]
            kbot = NAG.ap()[4, :].rearrange("(a p t) -> a p t", a=4, p=128)[hp]
            vtop = NAG.ap()[3, :].rearrange("(t c) -> t c", c=512)
            vbot = NAG.ap()[6, :].rearrange("(t c) -> t c", c=512)
            for h in range(2):
                ps_ = slice(h * 64, (h + 1) * 64)
                kc_ = slice(h * 64, (h + 1) * 64)
                ksrc = KNA_T.ap()[hp, ps_, :].rearrange("p (r c) -> p r c", c=64)
                for r16 in range(4):
                    kb.dma("sp", kt[ps_, 4 + r16 * 16:20 + r16 * 16, kc_], ksrc[:, r16 * 16:(r16 + 1) * 16, :], sm)
                kb.dma("sp", kt[ps_, 0:4, kc_], ktop[ps_, :].rearrange("p (r c) -> p r c", c=64), sm)
                kb.dma("sp", kt[ps_, 68:72, kc_], kbot[ps_, :].rearrange("p (r c) -> p r c", c=64), sm)
                c0 = hp * 128 + h * 64
                vc_ = slice(h * 65, h * 65 + 64)
                vsrc = VNA.ap()[:, c0:c0 + 64].rearrange("(r c) d -> c r d", c=64)
                for r16 in range(4):
                    kb.dma("sp", vt[ps_, 4 + r16 * 16:20 + r16 * 16, vc_], vsrc[:, r16 * 16:(r16 + 1) * 16, :], sm)
                kb.dma("sp", vt[ps_, 0:4, vc_], vtop[:, c0:c0 + 64].rearrange("(r c) d -> c r d", c=64), sm)
                tk = kb.dma("sp", vt[ps_, 68:72, vc_], vbot[:, c0:c0 + 64].rearrange("(r c) d -> c r d", c=64), sm)
            return tk

        out_toks = []
        hp_free = [[], []]
        ld_tok = [None, None]
        ld_tok[0] = load_hp(0, 0, [])
        for hp in range(4):
            slot = hp % 2
            if hp + 1 < 4:
                ld_tok[1 - slot] = load_hp(hp + 1, 1 - slot, hp_free[1 - slot])
            kt, vt, qt, bb = kt_ring[slot], vt_ring[slot], qt_ring[slot], bb_ring[slot]
            ltok = ld_tok[slot]
            exp_tok = {}
            last_pv_pe = None
            last_dve_read = None

            def qk(kidx):
                kr = kidx - 4
                rows = NA_QR[kr]
                qlo, qhi = rows[0], rows[-1]
                nq = (qhi - qlo + 1) * 64
                si, (b0, b1), sfr = s_ring.next()
                for tk in sfr:
                    kb.wait("pe", tk)
                kb.wait("pe", ltok)
                ins = None
                for seg, bank in ((0, b0), (1, b1)):
                    c0 = seg * 512
                    if c0 >= nq:
                        continue
                    w = min(512, nq - c0)
                    ins = nc.tensor.matmul(bank[:, 0:w], lhsT=kt[:, kidx, :],
                                           rhs=qt[:, qlo * 64 + c0:qlo * 64 + c0 + w], start=True, stop=True)
                ptok = kb.mark("pe", ins)
                zi, z, zfr = z_ring.next()
                kb.wait("dve", ptok)
                kb.wait("dve", ltok)
                kb.wait("dve", bi_tok)
                for tk in zfr:
                    kb.wait("dve", tk)
                parts = []
                for r in rows:
                    kind = "b" if r in NA_BND else "i"
                    if parts and parts[-1][0] == kind:
                        parts[-1][2] = r
                    else:
                        parts.append([kind, r, r])
                ins = None
                for kind, r0, r1 in parts:
                    a0, a1 = (r0 - qlo) * 64, (r1 - qlo + 1) * 64
                    pieces = []
                    if a0 < 512 < a1:
                        pieces = [(a0, 512), (512, a1)]
                    else:
                        pieces = [(a0, a1)]
                    for (c0, c1) in pieces:
                        bank = b0 if c0 < 512 else b1
                        off = 0 if c0 < 512 else 512
                        rr = qlo + c0 // 64
                        if kind == "i":
                            t0 = (7 + rr - kr) * 64
                            tab = bi[:, hp, t0:t0 + (c1 - c0)]
                        else:
                            t0 = NA_SLOT[(kr, rr)] * 64
                            tab = bb[:, t0:t0 + (c1 - c0)]
                        ins = nc.vector.tensor_tensor(out=z[:, c0:c1], in0=bank[:, c0 - off:c1 - off], in1=tab, op=ALU.add)
                dtok = kb.mark("dve", ins)
                s_ring.free[si] = [dtok]
                pslot = kidx % 16
                kb.wait("act", dtok)
                for tk in pt_free[pslot]:
                    kb.wait("act", tk)
                pt_free[pslot] = []
                ins = nc.scalar.activation(out=pt_ring[pslot][:, 0:nq], in_=z[:, 0:nq], func=AF.Exp)
                etok = kb.mark("act", ins)
                z_ring.free[zi] = [etok]
                exp_tok[kidx] = etok
                return dtok

            def pv_group(rows3):
                ai, acc, afr = acc_ring.next()
                for tk in afr:
                    kb.wait("pe", tk)
                ins = None
                for j, r in enumerate(rows3):
                    ks = na_superset(r)
                    kb.wait("pe", exp_tok[ks[-1] + 4])
                    for n_, kr in enumerate(ks):
                        kidx = kr + 4
                        qlo = NA_QR[kr][0]
                        c0 = (r - qlo) * 64
                        ins = nc.tensor.matmul(acc[0:64, j * 130:(j + 1) * 130],
                                               lhsT=pt_ring[kidx % 16][:, c0:c0 + 64], rhs=vt[:, kidx, :],
                                               start=(n_ == 0 and j == 0), stop=(n_ == len(ks) - 1), skip_group_check=True)
                ptok = kb.mark("pe", ins)
                for r in rows3:
                    for kr in na_superset(r):
                        pt_free[(kr + 4) % 16] = [ptok]
                ng = len(rows3)

                def fin():
                    return pv_fin(rows3, ai, acc, ptok, ng)
                return ptok, fin

            def pv_fin(rows3, ai, acc, ptok, ng):
                ri, rs, rfr = rs_ring.next()
                yi, ya, yfr = ya_ring.next()
                kb.wait("dve", ptok)
                for tk in rfr + yfr:
                    kb.wait("dve", tk)
                accv = acc[0:64, 0:ng * 130].rearrange("p (g e) -> p g e", e=65)
                rtk = kb.mark("dve", nc.vector.reciprocal(out=rs[:, 0:2 * ng], in_=accv[:, :, 64]))
                kb.wait("dve", rtk, force=True)
                ins = nc.vector.tensor_tensor(out=ya[:, 0:2 * ng, :], in0=accv[:, :, 0:64],
                                              in1=rs[:, 0:2 * ng].rearrange("p (g o) -> p g o", o=1).broadcast_to([64, 2 * ng, 64]), op=ALU.mult)
                ntok = kb.mark("dve", ins)
                acc_ring.free[ai] = [ntok]
                ti, tp, tfr = tp_ring.next()
                kb.wait("pe", ntok)
                for tk in tfr:
                    kb.wait("pe", tk)
                yav = ya[:, :, :].rearrange("p g d -> p (g d)")
                for j in range(ng):
                    ins = nc.tensor.transpose(tp[:, j * 64:(j + 1) * 64], yav[0:64, j * 128:(j + 1) * 128], ident_b[0:64, 0:64])
                ttok = kb.mark("pe", ins)
                rs_ring.free[ri] = [ntok]
                ya_ring.free[yi] = [ttok]
                gi, stg, gfr = stg_ring.next()
                kb.wait("act", ttok)
                for tk in gfr:
                    kb.wait("act", tk)
                ins = nc.scalar.copy(out=stg[:, 0:ng * 64], in_=tp[:, 0:ng * 64])
                ctok = kb.mark("act", ins)
                tp_ring.free[ti] = [ctok]
                kb.wait("sp", ctok)
                r0 = rows3[0]
                stok = kb.dma("sp", YA_T.ap()[hp, :, r0 * 64:(r0 + ng) * 64], stg[:, 0:ng * 64], stg_sem[gi])
                stg_ring.free[gi] = [stok]

            groups = [list(range(r0, min(r0 + 3, 64))) for r0 in range(0, 64, 3)]
            gnext = 0
            LAG = 2
            pending = []
            for kidx in range(NK + LAG):
                if kidx < NK:
                    last_dve_read = qk(kidx)
                for f in pending:
                    f()
                pending = []
                done_kr = kidx - LAG - 4
                while gnext < len(groups) and max(na_superset(groups[gnext][-1])) <= done_kr:
                    for f in pending:
                        f()
                    pending = []
                    last_pv_pe, f = pv_group(groups[gnext])
                    pending.append(f)
                    gnext += 1
            for f in pending:
                f()
            assert gnext == len(groups)
            hp_free[slot] = [last_pv_pe, last_dve_read]
        for gi in range(3):
            out_toks += stg_ring.free[gi]
        return out_toks


    def phase_ret(l):
        s_cc = kb.sem("cc")
        s_ld = kb.sem("ret_c")
        dve, act, pool, pe = nc.vector, nc.scalar, nc.gpsimd, nc.tensor

        def bc(ap2, n, g=4):
            return ap2.rearrange("p (g o) -> p g o", o=1).broadcast_to([128, g, n])

        rc = kb.sb("ret_rc", [128, 4 * 128 + 2 + 32], F32)
        dfb = kb.sb("ret_dfb", [128, 8], F32)
        par = kb.sb("ret_par", [128, 2], F32)
        kb.dma("sp", rc[:, :], rconst.ap()[:, :], s_ld)
        kb.dma("sp", dfb[:, 0:4], dec_f.ap()[l, :].partition_broadcast(128), s_ld)
        kb.dma("sp", dfb[:, 4:8], dec_b.ap()[l, :].partition_broadcast(128), s_ld)
        t0 = kb.dma("sp", par[:, :], par_d.ap()[:, :], s_ld)
        A1, A2, IDX1, IDX2 = rc[:, 0:128], rc[:, 128:256], rc[:, 256:384], rc[:, 384:512]
        PIDX, NIDX = rc[:, 512:514], rc[:, 514:546]
        lg = kb.sb("ret_lg", [128, 8], F32)
        sg_ = kb.sb("ret_sg", [128, 8], F32)
        kb.wait("act", t0)
        tk = kb.mark("act", act.activation(out=sg_[:, :], in_=dfb[:, :], func=AF.Sigmoid))
        kb.wait("act", tk, force=True)
        tk = kb.mark("act", act.activation(out=lg[:, :], in_=sg_[:, :], func=AF.Ln))
        kb.wait("act", tk, force=True)
        kb.wait("dve", tk)
        DT = kb.sb("ret_DT", [128, 4, 128], F32)
        XF = kb.sb("ret_XF", [128, 4, 128], F32)
        XB = kb.sb("ret_XB", [128, 4, 128], F32)
        ZF = kb.sb("ret_ZF", [128, 4], F32)
        ZB = kb.sb("ret_ZB", [128, 4], F32)
        GC = kb.sb("ret_GC", [128, 8], F32)
        CDF = kb.sb("ret_CDF", [128, 4, 32], F32)
        CDB = kb.sb("ret_CDB", [128, 4, 32], F32)
        ZFn = kb.sb("ret_ZFn", [128, 4, 32], F32)
        tmpa = kb.sb("ret_tmpa", [128, 4, 128], F32)
        tmpz = kb.sb("ret_tmpz", [128, 8], F32)
        for h in range(4):
            dve.tensor_scalar(out=tmpa[:, h, :], in0=A1, scalar1=lg[:, h:h + 1], scalar2=None, op0=ALU.mult)
        tk = kb.mark("dve", dve.tensor_copy(out=tmpz[:, 0:1], in_=lg[:, 0:1]))
        kb.wait("dve", tk, force=True)
        for h in range(4):
            dve.scalar_tensor_tensor(out=tmpa[:, h, :], in0=A2, scalar=lg[:, 4 + h:5 + h], in1=tmpa[:, h, :], op0=ALU.mult, op1=ALU.add)
            dve.tensor_scalar(out=tmpz[:, h:h + 1], in0=PIDX[:, 0:1], scalar1=lg[:, h:h + 1], scalar2=None, op0=ALU.mult)
            dve.tensor_scalar(out=tmpz[:, 4 + h:5 + h], in0=PIDX[:, 1:2], scalar1=lg[:, 4 + h:5 + h], scalar2=None, op0=ALU.mult)
            dve.tensor_scalar(out=CDF[:, h, :], in0=NIDX, scalar1=lg[:, h:h + 1], scalar2=None, op0=ALU.mult)
            dve.tensor_scalar(out=ZFn[:, h, :], in0=NIDX, scalar1=PIDX[:, 0:1], scalar2=lg[:, h:h + 1], op0=ALU.add, op1=ALU.mult)
            tk = kb.mark("dve", dve.tensor_scalar(out=CDB[:, h, :], in0=NIDX, scalar1=lg[:, 4 + h:5 + h], scalar2=None, op0=ALU.mult))
        kb.wait("act", tk)
        act.activation(out=DT[:, :, :], in_=tmpa[:, :, :], func=AF.Exp)
        act.activation(out=ZF[:, :], in_=tmpz[:, 0:4], func=AF.Exp)
        act.activation(out=ZB[:, :], in_=tmpz[:, 4:8], func=AF.Exp)
        act.activation(out=CDF[:, :, :], in_=CDF[:, :, :], func=AF.Exp)
        act.activation(out=CDB[:, :, :], in_=CDB[:, :, :], func=AF.Exp)
        act.activation(out=ZFn[:, :, :], in_=ZFn[:, :, :], func=AF.Exp)
        act.activation(out=GC[:, :], in_=lg[:, :], func=AF.Exp, scale=128.0)
        for h in range(4):
            act.activation(out=XF[:, h, :], in_=IDX1, func=AF.Exp, scale=lg[:, h:h + 1])
            tk = kb.mark("act", act.activation(out=XB[:, h, :], in_=IDX2, func=AF.Exp, scale=lg[:, 4 + h:5 + h]))
        tab_tok = tk

        SB_all = kb.sb("ret_SBall", [128, 32, 4, 256], BF16)
        S_run = kb.sb("ret_Srun", [128, 4, 256], F32)
        E_f = kb.sb("ret_Ef", [128, 4, 256], F32)
        Sf_bf = [kb.sb("ret_Sfbf%d" % i, [128, 4, 256], BF16) for i in range(2)]
        kt_ring = Ring([kb.sb("ret_kt%d" % i, [128, 4, 128], BF16) for i in range(3)])
        qt_ring = Ring([kb.sb("ret_qt%d" % i, [128, 4, 128], BF16) for i in range(3)])
        v_ring = Ring([kb.sb("ret_v%d" % i, [128, 4, 256], BF16) for i in range(3)])
        g_ring = Ring([kb.sb("ret_g%d" % i, [128, 4, 256], BF16) for i in range(3)])
        ld_sem = [kb.sem("ret_lds%d" % i) for i in range(3)]
        kzf_ring = Ring([kb.sb("ret_kzf%d" % i, [128, 4, 128], BF16) for i in range(2)])
        kzb_ring = Ring([kb.sb("ret_kzb%d" % i, [128, 4, 128], BF16) for i in range(2)])
        pt_ring = Ring([kb.sb("ret_pt%d" % i, [128, 4, 128], BF16) for i in range(2)])
        qf_ring = Ring([kb.sb("ret_qf%d" % i, [128, 4, 128], BF16) for i in range(2)])
        qb_ring = Ring([kb.sb("ret_qb%d" % i, [128, 4, 128], BF16) for i in range(2)])
        on_ring = Ring([kb.sb("ret_on%d" % i, [128, 4, 256], F32) for i in range(2)])
        yb_ring = Ring([kb.sb("ret_yb%d" % i, [128, 1024], BF16) for i in range(2)])
        ybt_ring = Ring([kb.sb("ret_ybt%d" % i, [128, 8, 128], BF16) for i in range(2)])
        ybt_sem = [kb.sem("ret_ybts%d" % i) for i in range(2)]
        st_ring = Ring([kb.sb("ret_st%d" % i, [128, 4, 6], F32) for i in range(2)])
        mv_ring = Ring([kb.sb("ret_mv%d" % i, [128, 4, 2], F32) for i in range(2)])
        r1_ring = Ring([kb.sb("ret_r1%d" % i, [128, 3, 4], F32) for i in range(2)])
        nm_ring = Ring([kb.sb("ret_nm%d" % i, [128, 4], F32) for i in range(2)])
        tmps = kb.sb("ret_tmps", [128, 4, 256], F32)
        ST_r = Ring([psb[0], psb[1]])
        OTa, OTb = psb[2], psb[3]
        KVa, KVb = psb[4], psb[5]
        KTR, YTR = ptb[0], ptb[1]

        def load_chunk(n, want_q):
            i = n % 2 if not want_q else n % 2
            _, kt, fk = kt_ring.next()
            _, v, fv = v_ring.next()
            for tk in fk + fv:
                kb.wait("sp", tk)
            sl = slice(n * 128, (n + 1) * 128)
            sm = ld_sem[(kt_ring.k - 1) % 3]
            kb.dma("sp", kt[:, :, :], KR_T.ap()[:, :, sl].rearrange("h p t -> p h t"), sm)
            tk = kb.dma("sp", v[:, :, :], VR.ap()[sl, :].rearrange("t (h e) -> t h e", h=4), sm)
            qt = g = None
            if want_q:
                _, qt, fq = qt_ring.next()
                _, g, fg = g_ring.next()
                for t_ in fq + fg:
                    kb.wait("sp", t_)
                kb.dma("sp", qt[:, :, :], QR_T.ap()[:, :, sl].rearrange("h p t -> p h t"), sm)
                tk = kb.dma("sp", g[:, :, :], SG.ap()[sl, :].rearrange("t (h e) -> t h e", h=4), sm)
            return dict(kt=kt, v=v, qt=qt, g=g, tok=tk, ki=(kt_ring.k - 1) % 3, vi=(v_ring.k - 1) % 3,
                        qi=(qt_ring.k - 1) % 3, gi=(g_ring.k - 1) % 3)

        def mm4(dst_a, dst_b, lhs_fn, rhs_fn, groups_extra=None):
            ins = None
            for h in range(4):
                bank = dst_a if h < 2 else dst_b
                terms = [(lhs_fn(h), rhs_fn(h))] + ([(a(h), b(h)) for a, b in groups_extra] if groups_extra else [])
                for ti, (lh, rh) in enumerate(terms):
                    ins = pe.matmul(bank[:, (h % 2) * 256:(h % 2) * 256 + 256], lhsT=lh, rhs=rh,
                                    start=(ti == 0 and h % 2 == 0), stop=(ti == len(terms) - 1), skip_group_check=True)
            return ins

        def bank4(a, b):
            return a[:, :].rearrange("p (g e) -> p g e", e=256), b[:, :].rearrange("p (g e) -> p g e", e=256)

        kb.wait("pool", tab_tok)
        pool.memset(S_run[:, :, :], 0.0)
        pool.memset(E_f[:, :, :], 0.0)
        z_tok = kb.mark("pool", pool.memset(SB_all[:, 31, :, :], 0.0))
        ktr_free = []
        kv_free = []
        upd_tok = z_tok
        pend_copy = None
        atok = None
        nxt = load_chunk(31, False)
        for n in range(31, -1, -1):
            cur = nxt
            if n > 0:
                nxt = load_chunk(n - 1, False)
            kt, v = cur["kt"], cur["v"]
            kb.wait("pe", cur["tok"])
            kb.wait("pe", tok_ident)
            for tk in ktr_free:
                kb.wait("pe", tk)
            for h in range(4):
                ins = pe.transpose(KTR[:, h * 128:(h + 1) * 128], kt[:, h, :], ident_b[:, :])
            ttok = kb.mark("pe", ins)
            kt_ring.free[cur["ki"]] = [ttok]
            _, kzf, f1 = kzf_ring.next()
            _, kzb, f2 = kzb_ring.next()
            kb.wait("act", ttok)
            kb.wait("act", tab_tok)
            for tk in f1 + f2:
                kb.wait("act", tk)
            for h in range(4):
                act.activation(out=kzf[:, h, :], in_=KTR[:, h * 128:(h + 1) * 128], func=AF.Identity, scale=ZF[:, h:h + 1])
                ins = act.activation(out=kzb[:, h, :], in_=KTR[:, h * 128:(h + 1) * 128], func=AF.Identity, scale=ZB[:, h:h + 1])
            ztok = kb.mark("act", ins)
            ktr_free = [ztok]
            if pend_copy is not None:
                kb.wait("act", upd_tok)
                atok = kb.mark("act", act.copy(out=SB_all[:, pend_copy, :, :], in_=S_run[:, :, :]))
                pend_copy = None
            kb.wait("pe", ztok)
            for tk in kv_free:
                kb.wait("pe", tk)
            mm4(psb[2], psb[3], lambda h: kzf[:, h, :], lambda h: v[:, h, :])
            ins = mm4(psb[4], psb[5], lambda h: kzb[:, h, :], lambda h: v[:, h, :])
            mtok = kb.mark("pe", ins)
            kzf_ring.free[(kzf_ring.k - 1) % 2] = [mtok]
            kzb_ring.free[(kzb_ring.k - 1) % 2] = [mtok]
            v_ring.free[cur["vi"]] = [mtok]
            kb.wait("dve", mtok)
            kb.wait("dve", upd_tok, force=True)
            kb.wait("dve", atok)
            fa, fb = bank4(psb[2], psb[3])
            ba, bb_ = bank4(psb[4], psb[5])
            for h in range(4):
                fsrc = (fa if h < 2 else fb)[:, h % 2, :]
                dve.scalar_tensor_tensor(out=E_f[:, h, :], in0=fsrc, scalar=CDF[:, h, n:n + 1], in1=E_f[:, h, :], op0=ALU.mult, op1=ALU.add)
            for h in range(4):
                bsrc = (ba if h < 2 else bb_)[:, h % 2, :]
                tk = kb.mark("dve", dve.scalar_tensor_tensor(out=S_run[:, h, :], in0=S_run[:, h, :], scalar=GC[:, 4 + h:5 + h], in1=bsrc, op0=ALU.mult, op1=ALU.add))
            upd_tok = tk
            kv_free = [tk]
            if n > 0:
                pend_copy = n - 1
        ef_tok = upd_tok
        s_ex = kb.sem("ret_ex")
        kb.wait("sp", ef_tok)
        kb.dma("sp", RSB.ap()[0, :].rearrange("(p f) -> p f", p=128), E_f[:, :, :].rearrange("p h e -> p (h e)"), s_ex)
        t = kb.dma("sp", RSB.ap()[1, :].rearrange("(p f) -> p f", p=128), S_run[:, :, :].rearrange("p h e -> p (h e)"), s_ex)
        kb.wait("pool", t)
        ins = pool.collective_compute("AllGather", ALU.bypass, replica_groups=pairs, ins=[RSB.ap().opt()], outs=[RSG.ap().opt()])
        ins.then_inc(s_cc.h)
        s_cc.n += 1
        cc_tok = (s_cc, s_cc.n)
        kb.wait("sp", cc_tok)
        Sb_in = kb.sb("ret_Sbin", [128, 4, 256], F32)
        kb.wait("sp", t)
        kb.dma("sp", S_run[:, :, :].rearrange("p h e -> p (h e)"), RSG.ap()[0, :].rearrange("(p f) -> p f", p=128), s_ex)
        t = kb.dma("sp", Sb_in[:, :, :].rearrange("p h e -> p (h e)"), RSG.ap()[3, :].rearrange("(p f) -> p f", p=128), s_ex)
        kb.wait("dve", t)
        dve.tensor_scalar(out=S_run[:, :, :], in0=S_run[:, :, :], scalar1=par[:, 0:1], scalar2=None, op0=ALU.mult)
        tk = kb.mark("dve", dve.tensor_scalar(out=Sb_in[:, :, :], in0=Sb_in[:, :, :], scalar1=par[:, 1:2], scalar2=None, op0=ALU.mult))
        kb.wait("dve", tk, force=True)
        kb.wait("act", tk)
        sf_tok = [kb.mark("act", act.copy(out=Sf_bf[0][:, :, :], in_=S_run[:, :, :])), None]
        sfc_tok = sf_tok[0]
        for n in range(32):
            for h in range(4):
                ins = dve.scalar_tensor_tensor(out=SB_all[:, n, h, :], in0=Sb_in[:, h, :], scalar=CDB[:, h, n:n + 1],
                                               in1=SB_all[:, n, h, :], op0=ALU.mult, op1=ALU.add)
        fix_tok = kb.mark("dve", ins)

        ot_free = [ef_tok]
        kv_free = [upd_tok]
        ytr_free = []
        upd_tok = fix_tok
        nm_ring2 = Ring([kb.sb("ret_nmr%d" % i, [128, 4], F32) for i in range(2)])
        pend = None

        def gn_norm(pd):
            nonlocal ot_free
            mv, r1, nm, oa, ob = pd["mv"], pd["r1"], pd["nm"], pd["oa"], pd["ob"]
            _, on, f5 = on_ring.next()
            kb.wait("act", pd["rtok"])
            for tk in f5:
                kb.wait("act", tk)
            for h in range(4):
                src = (oa if h < 2 else ob)[:, h % 2, :]
                ins = act.activation(out=on[:, h, :], in_=src, func=AF.Identity, bias=nm[:, h:h + 1], scale=r1[:, 2, h:h + 1])
            ntok = kb.mark("act", ins)
            ot_free = [ntok]
            st_ring.free[pd["sti"]] = [ntok]
            mv_ring.free[pd["mvi"]] = [ntok]
            r1_ring.free[pd["r1i"]] = [ntok]
            nm_ring2.free[pd["nmi"]] = [ntok]
            pd["on"] = on
            pd["oni"] = (on_ring.k - 1) % 2
            pd["ntok"] = ntok

        def gn_out(pd):
            nonlocal ytr_free
            n_, g, gi, on, ntok = pd["n"], pd["g"], pd["gi"], pd["on"], pd["ntok"]
            yi, yb, fy = yb_ring.next()
            kb.wait("pool", ntok)
            for tk in fy:
                kb.wait("pool", tk)
            ytok = kb.mark("pool", pool.tensor_tensor(out=yb[:, :].rearrange("p (h e) -> p h e", h=4), in0=on[:, :, :], in1=g[:, :, :], op=ALU.mult))
            on_ring.free[pd["oni"]] = [ytok]
            g_ring.free[gi] = [ytok]
            bi_, ybt, fb_ = ybt_ring.next()
            for tk in fb_:
                kb.wait("act", tk)
            pt_, et_ = transpose_chunk(yb, ytok, ybt[:, :, :], YTR, ytr_free, "act")
            ytr_free = [et_]
            yb_ring.free[yi] = [pt_]
            kb.wait("sp", et_)
            stt_ = kb.dma("sp", YB_T.ap()[:, :, n_ * 128:(n_ + 1) * 128].rearrange("k p t -> p k t"), ybt[:, :, :], ybt_sem[bi_])
            ybt_ring.free[bi_] = [stt_]

        nxt = load_chunk(0, True)
        for n in range(32):
            cur = nxt
            if n < 31:
                nxt = load_chunk(n + 1, True)
            kt, v, qt, g = cur["kt"], cur["v"], cur["qt"], cur["g"]
            sfb = Sf_bf[n % 2]
            si, STb, sfr = ST_r.next()
            kb.wait("pe", cur["tok"])
            for tk in sfr + ktr_free:
                kb.wait("pe", tk)
            for h in range(4):
                pe.matmul(STb[:, h * 128:(h + 1) * 128], lhsT=kt[:, h, :], rhs=qt[:, h, :], start=(h == 0), stop=True, skip_group_check=True)
            for h in range(4):
                ins = pe.transpose(KTR[:, h * 128:(h + 1) * 128], kt[:, h, :], ident_b[:, :])
            stok = kb.mark("pe", ins)
            kt_ring.free[cur["ki"]] = [stok]
            _, pt, f1 = pt_ring.next()
            kb.wait("dve", stok)
            for tk in f1:
                kb.wait("dve", tk)
            ptok = kb.mark("dve", dve.tensor_tensor(out=pt[:, :, :], in0=STb[:, :].rearrange("p (h t) -> p h t", h=4), in1=DT[:, :, :], op=ALU.mult))
            ST_r.free[si] = [ptok]
            _, kzf, f2 = kzf_ring.next()
            kb.wait("act", stok)
            for tk in f2:
                kb.wait("act", tk)
            for h in range(4):
                ins = act.activation(out=kzf[:, h, :], in_=KTR[:, h * 128:(h + 1) * 128], func=AF.Identity, scale=ZF[:, h:h + 1])
            ktok2 = kb.mark("act", ins)
            ktr_free = [ktok2]
            _, qf, f1 = qf_ring.next()
            _, qb, f2 = qb_ring.next()
            kb.wait("pool", cur["tok"])
            kb.wait("pool", tab_tok)
            for tk in f1 + f2:
                kb.wait("pool", tk)
            pool.tensor_tensor(out=qf[:, :, :], in0=qt[:, :, :], in1=XF[:, :, :], op=ALU.mult)
            qtok = kb.mark("pool", pool.tensor_tensor(out=qb[:, :, :], in0=qt[:, :, :], in1=XB[:, :, :], op=ALU.mult))
            qt_ring.free[cur["qi"]] = [qtok, stok]
            kb.wait("pe", ktok2)
            for tk in kv_free:
                kb.wait("pe", tk)
            ins = mm4(KVa, KVb, lambda h: kzf[:, h, :], lambda h: v[:, h, :])
            ktok = kb.mark("pe", ins)
            kzf_ring.free[(kzf_ring.k - 1) % 2] = [ktok]
            if pend is not None:
                gn_norm(pend)
            kb.wait("pe", ptok)
            kb.wait("pe", qtok)
            kb.wait("pe", sf_tok[n % 2])
            kb.wait("pe", fix_tok)
            for tk in ot_free:
                kb.wait("pe", tk)
            ins = mm4(OTa, OTb, lambda h: pt[:, h, :], lambda h: v[:, h, :],
                      [(lambda h: qf[:, h, :], lambda h: sfb[:, h, :]), (lambda h: qb[:, h, :], lambda h: SB_all[:, n, h, :])])
            otok = kb.mark("pe", ins)
            pt_ring.free[(pt_ring.k - 1) % 2] = [otok]
            qf_ring.free[(qf_ring.k - 1) % 2] = [otok]
            qb_ring.free[(qb_ring.k - 1) % 2] = [otok]
            v_ring.free[cur["vi"]] = [otok]
            kb.wait("dve", ktok)
            kb.wait("dve", upd_tok, force=True)
            kb.wait("dve", sfc_tok)
            ka, kb_ = bank4(KVa, KVb)
            for h in range(4):
                src = (ka if h < 2 else kb_)[:, h % 2, :]
                ins = dve.scalar_tensor_tensor(out=S_run[:, h, :], in0=S_run[:, h, :], scalar=GC[:, h:h + 1], in1=src, op0=ALU.mult, op1=ALU.add)
            upd_tok = kb.mark("dve", ins)
            kv_free = [upd_tok]
            kb.wait("act", upd_tok)
            kb.wait("act", otok if n > 0 else None)
            sfc_tok = kb.mark("act", act.copy(out=Sf_bf[(n + 1) % 2][:, :, :], in_=S_run[:, :, :]))
            sf_tok[(n + 1) % 2] = sfc_tok
            if pend is not None:
                gn_out(pend)
            sti, stt, f1 = st_ring.next()
            mvi, mv, f2 = mv_ring.next()
            r1i, r1, f3 = r1_ring.next()
            nmi, nm, f4 = nm_ring2.next()
            kb.wait("dve", otok)
            for tk in f1 + f2 + f3 + f4:
                kb.wait("dve", tk)
            oa, ob = bank4(OTa, OTb)
            for h in range(4):
                src = (oa if h < 2 else ob)[:, h % 2, :]
                tk = kb.mark("dve", dve.bn_stats(out=stt[:, h, :], in_=src))
            kb.wait("dve", tk, force=True)
            for h in range(4):
                tk = kb.mark("dve", dve.bn_aggr(out=mv[:, h, :], in_=stt[:, h:h + 1, :]))
            kb.wait("dve", tk, force=True)
            tk = kb.mark("dve", dve.tensor_scalar(out=r1[:, 0, :], in0=mv[:, :, 1], scalar1=GN_EPS, scalar2=None, op0=ALU.add))
            kb.wait("act", tk)
            tk = kb.mark("act", act.activation(out=r1[:, 1, :], in_=r1[:, 0, :], func=AF.Sqrt))
            kb.wait("dve", tk)
            tk = kb.mark("dve", dve.reciprocal(out=r1[:, 2, :], in_=r1[:, 1, :]))
            kb.wait("dve", tk, force=True)
            rtok = kb.mark("dve", dve.scalar_tensor_tensor(out=nm[:, :], in0=mv[:, :, 0], scalar=-1.0, in1=r1[:, 2, :], op0=ALU.mult, op1=ALU.mult))
            pend = dict(n=n, mv=mv, r1=r1, nm=nm, g=g, gi=cur["gi"], oa=oa, ob=ob, rtok=rtok, sti=sti, mvi=mvi, r1i=r1i, nmi=nmi)
        gn_norm(pend)
        gn_out(pend)
        return ybt_ring.free[0] + ybt_ring.free[1]

    class LN:
        def __init__(self, tag, g_d, b_d, l, banks):
            self.banks = banks
            self.G = kb.sb(tag + "_G", [128, D], F32)
            self.B = kb.sb(tag + "_B", [128, D], F32)
            sm = kb.sem(tag + "_gb")
            kb.dma("sp", self.G[:, :], g_d.ap()[l, :].partition_broadcast(128), sm)
            self.gb_tok = kb.dma("sp", self.B[:, :], b_d.ap()[l, :].partition_broadcast(128), sm)
            self.stt = Ring([kb.sb(tag + "_stt%d" % i, [128, 2, 6], F32) for i in range(3)])
            self.mv = Ring([kb.sb(tag + "_mv%d" % i, [128, 2], F32) for i in range(3)])
            self.r = Ring([kb.sb(tag + "_r%d" % i, [128, 4], F32) for i in range(3)])
            self.xn = Ring([kb.sb(tag + "_xn%d" % i, [128, D], F32) for i in range(2)])
            self.xo = Ring([kb.sb(tag + "_xo%d" % i, [128, D], F32) for i in range(2)])
            self.xo_sem = [kb.sem(tag + "_xos%d" % i) for i in range(2)]
            self.xb = Ring([kb.sb(tag + "_xb%d" % i, [128, D], BF16) for i in range(2)])
            self.xt = Ring([kb.sb(tag + "_xt%d" % i, [128, 8, 128], BF16) for i in range(2)])
            self.xt_sem = [kb.sem(tag + "_xts%d" % i) for i in range(2)]
            self.tp_free = [[] for _ in banks]
            self.k = 0

        def stats(self, pre, pre_tok):
            dve, act = nc.vector, nc.scalar
            si, stt, f1 = self.stt.next()
            mi, mv, f2 = self.mv.next()
            ri, r, f3 = self.r.next()
            kb.wait("dve", pre_tok, force=True)
            for tk in f1 + f2 + f3:
                kb.wait("dve", tk)
            dve.bn_stats(out=stt[:, 0, :], in_=pre[:, 0:512])
            tk = kb.mark("dve", dve.bn_stats(out=stt[:, 1, :], in_=pre[:, 512:1024]))
            kb.wait("dve", tk, force=True)
            tk = kb.mark("dve", dve.bn_aggr(out=mv[:, :], in_=stt[:, :, :]))
            kb.wait("dve", tk, force=True)
            tk = kb.mark("dve", dve.tensor_scalar(out=r[:, 0:1], in0=mv[:, 1:2], scalar1=LN_EPS, scalar2=None, op0=ALU.add))
            kb.wait("act", tk)
            stok = kb.mark("act", act.activation(out=r[:, 1:2], in_=r[:, 0:1], func=AF.Sqrt))
            return dict(pre=pre, mv=mv, r=r, si=si, mi=mi, ri=ri, stok=stok)

        def finish_a(self, h, dst_x, want_xt):
            dve, act = nc.vector, nc.scalar
            pre, mv, r = h["pre"], h["mv"], h["r"]
            kb.wait("dve", h["stok"])
            tk = kb.mark("dve", dve.reciprocal(out=r[:, 2:3], in_=r[:, 1:2]))
            kb.wait("dve", tk, force=True)
            tk = kb.mark("dve", dve.tensor_scalar(out=r[:, 3:4], in0=mv[:, 0:1], scalar1=r[:, 2:3], scalar2=-1.0, op0=ALU.mult, op1=ALU.mult))
            _, xn, f4 = self.xn.next()
            kb.wait("act", tk)
            for t_ in f4:
                kb.wait("act", t_)
            ntok = kb.mark("act", act.activation(out=xn[:, :], in_=pre[:, :], func=AF.Identity, bias=r[:, 3:4], scale=r[:, 2:3]))
            self.stt.free[h["si"]] = [ntok]
            self.mv.free[h["mi"]] = [ntok]
            self.r.free[h["ri"]] = [ntok]
            oi, xo, f5 = self.xo.next()
            kb.wait("dve", ntok)
            kb.wait("dve", self.gb_tok)
            for t_ in f5:
                kb.wait("dve", t_)
            dve.tensor_tensor(out=xo[:, :], in0=xn[:, :], in1=self.G[:, :], op=ALU.mult)
            tk = kb.mark("dve", dve.tensor_tensor(out=xo[:, :], in0=xo[:, :], in1=self.B[:, :], op=ALU.add))
            self.xn.free[(self.xn.k - 1) % 2] = [tk]
            toks = []
            btok = xb = bi_ = None
            if want_xt:
                bi_, xb, f6 = self.xb.next()
                kb.wait("act", tk)
                for t_ in f6:
                    kb.wait("act", t_)
                btok = kb.mark("act", act.copy(out=xb[:, :], in_=xo[:, :]))
            kb.wait("sp", tk)
            st = kb.dma("sp", dst_x, xo[:, :], self.xo_sem[oi])
            self.xo.free[oi] = [st] + ([btok] if btok else [])
            toks.append(st)
            h.update(ntok=ntok, toks=toks, btok=btok, xb=xb, bi=bi_)
            return h

        def finish_b(self, h, dst_xt):
            toks = h["toks"]
            if dst_xt is not None:
                ti, xt, f7 = self.xt.next()
                e = "act"
                for t_ in f7:
                    kb.wait(e, t_)
                j = self.k % len(self.banks)
                self.k += 1
                pt_, et_ = transpose_chunk(h["xb"], h["btok"], xt[:, :, :], self.banks[j], self.tp_free[j], e)
                self.tp_free[j] = [et_]
                self.xb.free[h["bi"]] = [pt_]
                kb.wait("sp", et_)
                st2 = kb.dma("sp", dst_xt, xt[:, :, :], self.xt_sem[ti])
                self.xt.free[ti] = [st2]
                toks.append(st2)
            return h["ntok"], toks

        def finish(self, h, dst_x, dst_xt):
            self.finish_a(h, dst_x, dst_xt is not None)
            return self.finish_b(h, dst_xt)

    def load_w_cast(dst, src, sem):
        return kb.dma("pool", dst, src, sem)

    def p1d_weights(l):
        s_w = kb.sem("p1d_w")
        Wa = kb.sb("p1d_Wa", [128, 4, D], BF16)
        Wb = kb.sb("p1d_Wb", [128, 8, D], BF16)
        Wo = kb.sb("p1d_Wo", [128, 8, D], BF16)
        load_w_cast(Wa[:, :, :], w_ba.ap()[l].rearrange("(k p) c -> p k c", p=128), s_w)
        for hf in range(2):
            load_w_cast(Wb[:, hf * 4:(hf + 1) * 4, :], w_bb.ap()[l, hf * 512:(hf + 1) * 512, :].rearrange("(k p) c -> p k c", p=128), s_w)
            wtok = load_w_cast(Wo[:, hf * 4:(hf + 1) * 4, :], w_out.ap()[l, hf * 512:(hf + 1) * 512, :].rearrange("(k p) c -> p k c", p=128), s_w)
        return Wa, Wb, Wo, wtok

    def phase_p1d(l, w1d):
        dve, act, pool, pe = nc.vector, nc.scalar, nc.gpsimd, nc.tensor
        Wa, Wb, Wo, wtok = w1d
        ln = LN("ln1", ln1_g, ln1_b, l, [ptb[0], ptb[1]])
        in_ring = Ring([dict(ya=kb.sb("p1d_ya%d" % i, [128, 4, 512], BF16), yb=kb.sb("p1d_yb%d" % i, [128, 8, 512], BF16),
                             ga=kb.sb("p1d_ga%d" % i, [128, 8, 512], BF16), gb=kb.sb("p1d_gb%d" % i, [128, 8, 512], BF16)) for i in range(2)])
        in_sem = [kb.sem("p1d_ins%d" % i) for i in range(2)]
        mg_ring = Ring([kb.sb("p1d_mg%d" % i, [128, 8, 512], BF16) for i in range(2)])
        t1_ring = Ring([kb.sb("p1d_t1%d" % i, [128, 512], F32) for i in range(2)])
        t2_ring = Ring([kb.sb("p1d_t2%d" % i, [128, 512], F32) for i in range(2)])
        x_ring = Ring([kb.sb("p1d_x%d" % i, [128, D], F32) for i in range(2)])
        x_sem = [kb.sem("p1d_xs%d" % i) for i in range(2)]
        pre_ring = Ring([kb.sb("p1d_pre%d" % i, [128, D], F32) for i in range(3)])
        za_ring = Ring([psb[0], psb[1]])
        zb_ring = Ring([psb[2], psb[3]])
        y_free = []
        out_toks = {}

        def load_tile(tt):
            i, bufs, fr = in_ring.next()
            for tk in fr:
                kb.wait("sp", tk)
            sl = slice(tt * 512, (tt + 1) * 512)
            kb.dma("sp", bufs["ya"][:, :, :], YA_T.ap()[:, :, sl].rearrange("k p t -> p k t"), in_sem[i])
            kb.dma("sp", bufs["yb"][:, :, :], YB_T.ap()[:, :, sl].rearrange("k p t -> p k t"), in_sem[i])
            kb.dma("sp", bufs["ga"][:, :, :], GA_T.ap()[:, :, sl].rearrange("k p t -> p k t"), in_sem[i])
            tk = kb.dma("sp", bufs["gb"][:, :, :], GB_T.ap()[:, :, sl].rearrange("k p t -> p k t"), in_sem[i])
            return i, bufs, tk

        pending = None

        pending_b = None

        def ln_fin_a(hnd, c, pi):
            nonlocal pending_b
            ln.finish_a(hnd, X1.ap()[c * 128:(c + 1) * 128, :], True)
            pre_ring.free[pi] = [hnd["ntok"]]
            pending_b = (hnd, c)

        def ln_fin_b(hnd, c):
            ntok, toks = ln.finish_b(hnd, X1T_D.ap()[:, :, c * 128:(c + 1) * 128].rearrange("k p t -> p k t"))
            for t_ in toks:
                out_toks[id(t_[0])] = t_

        tiles = {}

        def tile_begin(tt):
            ii, bufs, ltok = tiles[tt]["load"]
            mi, mg, mfr = mg_ring.next()
            tiles[tt].update(ii=ii, bufs=bufs, ltok=ltok, mi=mi, mg=mg, mfr=mfr)

        def fb_step(tt, fb):
            T = tiles[tt]
            bufs, ltok, mg = T["bufs"], T["ltok"], T["mg"]
            _, za, fa = za_ring.next()
            _, zb, fb_ = zb_ring.next()
            kb.wait("pe", ltok)
            kb.wait("pe", wtok)
            for tk in fa + fb_:
                kb.wait("pe", tk)
            for kc in range(4):
                pe.matmul(za[:, :], lhsT=Wa[:, kc, fb * 128:(fb + 1) * 128], rhs=bufs["ya"][:, kc, :], start=(kc == 0), stop=(kc == 3))
            for kc in range(8):
                ins = pe.matmul(zb[:, :], lhsT=Wb[:, kc, fb * 128:(fb + 1) * 128], rhs=bufs["yb"][:, kc, :], start=(kc == 0), stop=(kc == 7))
            ptok = kb.mark("pe", ins)
            _, t1, f1 = t1_ring.next()
            _, t2, f2 = t2_ring.next()
            kb.wait("dve", ptok)
            kb.wait("dve", ltok)
            for tk in f1 + f2:
                kb.wait("dve", tk)
            if fb == 0:
                for tk in T["mfr"]:
                    kb.wait("dve", tk)
            dve.tensor_tensor(out=t1[:, :], in0=za[:, :], in1=bufs["ga"][:, fb, :], op=ALU.mult)
            dtok = kb.mark("dve", dve.tensor_tensor(out=t2[:, :], in0=zb[:, :], in1=bufs["gb"][:, fb, :], op=ALU.mult))
            za_ring.free[(za_ring.k - 1) % 2] = [dtok]
            zb_ring.free[(zb_ring.k - 1) % 2] = [dtok]
            T["mtok"] = kb.mark("dve", dve.tensor_tensor(out=mg[:, fb, :], in0=t1[:, :], in1=t2[:, :], op=ALU.add))
            if fb == 7:
                in_ring.free[T["ii"]] = [ptok, dtok]

        def sub_chunk(tt, sc):
            nonlocal y_free, pending, pending_b
            T = tiles[tt]
            mg = T["mg"]
            c = tt * 4 + sc
            xi, xt_, fx = x_ring.next()
            for tk in fx:
                kb.wait("sp", tk)
            xtok = kb.dma("sp", xt_[:, :], XA.ap()[c * 128:(c + 1) * 128, :], x_sem[xi])
            kb.wait("pe", T["mtok"])
            for tk in y_free:
                kb.wait("pe", tk)
            for half in range(2):
                bank = psb[4 + half]
                for fb in range(8):
                    ins = pe.matmul(bank[:, :], lhsT=mg[:, fb, sc * 128:(sc + 1) * 128], rhs=Wo[:, fb, half * 512:(half + 1) * 512],
                                    start=(fb == 0), stop=(fb == 7))
            ytok = kb.mark("pe", ins)
            if pending_b is not None:
                ln_fin_b(*pending_b)
                pending_b = None
            pi, pre, fp = pre_ring.next()
            kb.wait("dve", ytok)
            kb.wait("dve", xtok)
            for tk in fp:
                kb.wait("dve", tk)
            dve.scalar_tensor_tensor(out=pre[:, 0:512], in0=xt_[:, 0:512], scalar=ALPHA, in1=psb[4][:, :], op0=ALU.mult, op1=ALU.add)
            ptk = kb.mark("dve", dve.scalar_tensor_tensor(out=pre[:, 512:1024], in0=xt_[:, 512:1024], scalar=ALPHA, in1=psb[5][:, :], op0=ALU.mult, op1=ALU.add))
            y_free = [ptk]
            x_ring.free[xi] = [ptk]
            hnd = ln.stats(pre, ptk)
            pending = (hnd, c, pi)
            if sc == 3:
                mg_ring.free[T["mi"]] = [ytok]

        tiles[0] = dict(load=load_tile(0))
        for tt in range(9):
            if tt < 8:
                if tt + 1 < 8:
                    tiles[tt + 1] = dict(load=load_tile(tt + 1))
                tile_begin(tt)
            for fb in range(8):
                if tt < 8:
                    fb_step(tt, fb)
                if fb % 2 == 0 and pending is not None:
                    ln_fin_a(*pending)
                    pending = None
                if tt >= 1 and fb % 2 == 1:
                    sub_chunk(tt - 1, fb // 2)
        if pending is not None:
            ln_fin_a(*pending)
        if pending_b is not None:
            ln_fin_b(*pending_b)
        return list(out_toks.values())

    def phase_p2a(l):
        dve, act, pool, pe = nc.vector, nc.scalar, nc.gpsimd, nc.tensor
        s_cc = kb.sem("cc")
        s_ex = kb.sem("p2a_ex")
        par = kb.sb("p2a_par", [128, 2], F32)
        kb.dma("sp", par[:, :], par_d.ap()[:, :], s_ex)
        kb.dma("sp", CXB.ap()[0, :].rearrange("(k p o) -> k p o", p=128, o=1), X1T_D.ap()[:, :, 0:1], s_ex, allow_slow_non_contiguous=True)
        t = kb.dma("sp", CXB.ap()[1, :].rearrange("(k p o) -> k p o", p=128, o=1), X1T_D.ap()[:, :, NT - 1:NT], s_ex, allow_slow_non_contiguous=True)
        ex_t = t
        XT = kb.sb("p2a_XT", [128, 8, NT + 2], BF16)
        s_xt = kb.sem("p2a_xt")
        for kc in range(8):
            xtm_tok = kb.dma("sp", XT[:, kc, 1:NT + 1], X1T_D.ap()[kc, :, :], s_xt)
        cp = kb.sb("p2a_cp", [128, 44, 4], F32)
        cp_tok = kb.dma("sp", cp[:, :, :], convp.ap()[l], kb.sem("p2a_cp"))
        w_ring = Ring([kb.sb("p2a_w%d" % i, [128, 2, 8, 128], BF16) for i in range(2)])
        w_sem = [kb.sem("p2a_ws%d" % i) for i in range(2)]
        HW = 2048
        H_ring = Ring([kb.sb("p2a_H%d" % i, [128, 2, HW + 2], F32) for i in range(2)])
        C_ring = Ring([kb.sb("p2a_C%d" % i, [128, 2, HW], F32) for i in range(2)])
        U_ring = Ring([kb.sb("p2a_U%d" % i, [128, HW], BF16) for i in range(2)])
        U_sem = [kb.sem("p2a_us%d" % i) for i in range(2)]
        ps_ring = Ring(psb[0:4])
        hp_ring = Ring([psb[4], psb[5]])

        def load_w(c):
            i, w, fr = w_ring.next()
            for tk in fr:
                kb.wait("pool", tk)
            for part in range(2):
                c0 = part * DFF + c * 128
                tk = load_w_cast(w[:, part, :, :], w_up.ap()[l, :, c0:c0 + 128].rearrange("(k p) c -> p k c", p=128), w_sem[i])
            return i, w, tk

        nxt = load_w(0)
        kb.wait("pool", ex_t)
        ins = pool.collective_compute("AllGather", ALU.bypass, replica_groups=pairs, ins=[CXB.ap().opt()], outs=[CXG.ap().opt()])
        ins.then_inc(s_cc.h)
        s_cc.n += 1
        cc_tok = (s_cc, s_cc.n)
        kb.wait("sp", cc_tok)
        hal = kb.sb("p2a_hal", [128, 8, 2], BF16)
        s_hal = kb.sem("p2a_hal")
        kb.dma("sp", hal[:, :, 0:1], CXG.ap()[1, :].rearrange("(k p o) -> p k o", p=128, o=1), s_hal, allow_slow_non_contiguous=True)
        t = kb.dma("sp", hal[:, :, 1:2], CXG.ap()[2, :].rearrange("(k p o) -> p k o", p=128, o=1), s_hal, allow_slow_non_contiguous=True)
        kb.wait("dve", t)
        kb.wait("dve", xtm_tok)
        dve.tensor_scalar(out=XT[:, :, 0], in0=hal[:, :, 0], scalar1=par[:, 0:1], scalar2=None, op0=ALU.mult)
        xt_tok = kb.mark("dve", dve.tensor_scalar(out=XT[:, :, NT + 1], in0=hal[:, :, 1], scalar1=par[:, 1:2], scalar2=None, op0=ALU.mult))
        for c in range(22):
            wi, w, wtok = nxt
            if c + 1 < 22:
                nxt = load_w(c + 1)
            for hf in range(2):
                T0 = hf * HW
                hi, H, fh = H_ring.next()
                ev_toks = []
                for part in range(2):
                    _, hb, fhp = hp_ring.next()
                    kb.wait("pe", wtok)
                    kb.wait("pe", xt_tok)
                    for tk in fhp:
                        kb.wait("pe", tk)
                    for kc in range(8):
                        ins = pe.matmul(hb[:, 0:2], lhsT=w[:, part, kc, :], rhs=XT[:, kc, T0:T0 + HW + 2:HW + 1], start=(kc == 0), stop=(kc == 7))
                    htok = kb.mark("pe", ins)
                    kb.wait("act", htok)
                    if part == 0:
                        for tk in fh:
                            kb.wait("act", tk)
                    etk = kb.mark("act", act.copy(out=H[:, part, 0:HW + 2:HW + 1], in_=hb[:, 0:2]))
                    hp_ring.free[(hp_ring.k - 1) % 2] = [etk]
                    for t4 in range(4):
                        pi, pb, pfr = ps_ring.next()
                        for tk in pfr:
                            kb.wait("pe", tk)
                        for kc in range(8):
                            ins = pe.matmul(pb[:, :], lhsT=w[:, part, kc, :], rhs=XT[:, kc, 1 + T0 + t4 * 512:1 + T0 + (t4 + 1) * 512],
                                            start=(kc == 0), stop=(kc == 7))
                        mtok = kb.mark("pe", ins)
                        kb.wait("act", mtok)
                        etk = kb.mark("act", act.copy(out=H[:, part, 1 + t4 * 512:1 + (t4 + 1) * 512], in_=pb[:, :]))
                        ps_ring.free[pi] = [etk]
                    ev_toks.append(etk)
                last_mm = mtok
                ci, C, fc = C_ring.next()
                ia, ib = c, 22 + c
                kb.wait("act", cp_tok)
                for tk in fc:
                    kb.wait("act", tk)
                a1 = kb.mark("act", act.activation(out=C[:, 0, :], in_=H[:, 0, 1:HW + 1], func=AF.Identity, bias=cp[:, ia, 3:4], scale=cp[:, ia, 1:2]))
                kb.wait("pool", ev_toks[1])
                kb.wait("pool", cp_tok)
                for tk in fc:
                    kb.wait("pool", tk)
                p1 = kb.mark("pool", pool.tensor_scalar(out=C[:, 1, :], in0=H[:, 1, 1:HW + 1], scalar1=cp[:, ib, 1:2], scalar2=cp[:, ib, 3:4], op0=ALU.mult, op1=ALU.add))
                kb.wait("dve", a1)
                kb.wait("dve", cp_tok)
                dve.scalar_tensor_tensor(out=C[:, 0, :], in0=H[:, 0, 0:HW], scalar=cp[:, ia, 0:1], in1=C[:, 0, :], op0=ALU.mult, op1=ALU.add)
                d1 = kb.mark("dve", dve.scalar_tensor_tensor(out=C[:, 0, :], in0=H[:, 0, 2:HW + 2], scalar=cp[:, ia, 2:3], in1=C[:, 0, :], op0=ALU.mult, op1=ALU.add))
                kb.wait("dve", p1)
                dve.scalar_tensor_tensor(out=C[:, 1, :], in0=H[:, 1, 0:HW], scalar=cp[:, ib, 0:1], in1=C[:, 1, :], op0=ALU.mult, op1=ALU.add)
                d2 = kb.mark("dve", dve.scalar_tensor_tensor(out=C[:, 1, :], in0=H[:, 1, 2:HW + 2], scalar=cp[:, ib, 2:3], in1=C[:, 1, :], op0=ALU.mult, op1=ALU.add))
                H_ring.free[hi] = [d1, d2]
                kb.wait("act", d1)
                g1 = kb.mark("act", act.activation(out=C[:, 0, :], in_=C[:, 0, :], func=AF.Gelu))
                ui, U, fu = U_ring.next()
                kb.wait("pool", g1)
                kb.wait("pool", d2)
                for tk in fu:
                    kb.wait("pool", tk)
                utok = kb.mark("pool", pool.tensor_tensor(out=U[:, :], in0=C[:, 0, :], in1=C[:, 1, :], op=ALU.mult))
                C_ring.free[ci] = [utok]
                kb.wait("sp", utok)
                st = kb.dma("sp", U_T.ap()[c, :, T0:T0 + HW], U[:, :], U_sem[ui])
                U_ring.free[ui] = [st]
            w_ring.free[wi] = [last_mm]
        return U_ring.free[0] + U_ring.free[1]

    def phase_p2b(l, last):
        dve, act, pool, pe = nc.vector, nc.scalar, nc.gpsimd, nc.tensor
        s_w = kb.sem("p2b_w")
        Wd = kb.sb("p2b_Wd", [128, 22, D], BF16)
        Wg = kb.sb("p2b_Wg", [128, 8, D], BF16)
        Wp = kb.sb("p2b_Wp", [128, 2, D], BF16)
        ln = LN("ln2", ln2_g, ln2_b, l, [ptb[1]])
        in_ring = Ring([dict(u=kb.sb("p2b_u%d" % i, [128, 22, 128], BF16), xt=kb.sb("p2b_xt%d" % i, [128, 8, 128], BF16),
                             x1=kb.sb("p2b_x1%d" % i, [128, D], F32), pb=kb.sb("p2b_pb%d" % i, [128, PLE], BF16)) for i in range(3)])
        in_sem = [kb.sem("p2b_ins%d" % i) for i in range(3)]
        pin_sem = [kb.sem("p2b_pins%d" % i) for i in range(3)]
        pT_ring = Ring([kb.sb("p2b_pT%d" % i, [128, 2, 128], BF16) for i in range(2)])
        sg_ring = Ring([kb.sb("p2b_sg%d" % i, [128, D], F32) for i in range(2)])
        pre_ring = Ring([kb.sb("p2b_pre%d" % i, [128, D], F32) for i in range(3)])
        acc_free = [[], []]
        tp_free = []
        out_toks = {}

        def load_chunk(c):
            i, bufs, fr = in_ring.next()
            for tk in fr:
                kb.wait("sp", tk)
                kb.wait("pool", tk)
            sl = slice(c * 128, (c + 1) * 128)
            kb.dma("sp", bufs["u"][:, :, :], U_T.ap()[:, :, sl].rearrange("k p t -> p k t"), in_sem[i])
            kb.dma("sp", bufs["xt"][:, :, :], X1T_D.ap()[:, :, sl].rearrange("k p t -> p k t"), in_sem[i])
            tk = kb.dma("sp", bufs["x1"][:, :], X1.ap()[sl, :], in_sem[i])
            ptk = kb.dma("pool", bufs["pb"][:, :], p_in.ap()[l, sl, :], pin_sem[i])
            return i, bufs, tk, ptk

        def stage_P(c, ch):
            nonlocal tp_free
            pti, pT, fpt = pT_ring.next()
            kb.wait("pe", ch["lptok"])
            kb.wait("pe", tok_ident)
            for tk in tp_free:
                kb.wait("pe", tk)
            TPB = ptb[0]
            for k2 in range(2):
                ins = pe.transpose(TPB[:, k2 * 128:(k2 + 1) * 128], ch["bufs"]["pb"][:, k2 * 128:(k2 + 1) * 128], ident_b[:, :])
            ttok = kb.mark("pe", ins)
            kb.wait("act", ttok)
            for tk in fpt:
                kb.wait("act", tk)
            ctok = kb.mark("act", act.copy(out=pT[:, :, :], in_=TPB[:, 0:256].rearrange("p (k t) -> p k t", k=2)))
            tp_free = [ctok]
            ch.update(pT=pT, pti=pti, ctok=ctok, ttok=ttok)

        def stage_M(c, ch):
            bufs, pT = ch["bufs"], ch["pT"]
            kb.wait("pe", ch["ltok"])
            kb.wait("pe", ch["ctok"])
            ch["mtok"] = []
            for half in range(2):
                cs = slice(half * 512, (half + 1) * 512)
                by, bg, bp = psb[3 * half], psb[3 * half + 1], psb[3 * half + 2]
                for tk in acc_free[half]:
                    kb.wait("pe", tk)
                for k in range(22):
                    kb.wait("pe", wd_tok[k])
                    pe.matmul(by[:, :], lhsT=bufs["u"][:, k, :], rhs=Wd[:, k, cs], start=(k == 0), stop=(k == 21))
                for k in range(8):
                    kb.wait("pe", wg_tok[k])
                    pe.matmul(bg[:, :], lhsT=bufs["xt"][:, k, :], rhs=Wg[:, k, cs], start=(k == 0), stop=(k == 7))
                kb.wait("pe", wtok)
                for k in range(2):
                    ins = pe.matmul(bp[:, :], lhsT=pT[:, k, :], rhs=Wp[:, k, cs], start=(k == 0), stop=(k == 1))
                ch["mtok"].append(kb.mark("pe", ins))
            pT_ring.free[ch["pti"]] = [ch["mtok"][1]]

        def stage_E(c, ch):
            bufs = ch["bufs"]
            si, sg, fs = sg_ring.next()
            pi, pre, fp = pre_ring.next()
            for half in range(2):
                cs = slice(half * 512, (half + 1) * 512)
                by, bg, bp = psb[3 * half], psb[3 * half + 1], psb[3 * half + 2]
                kb.wait("act", ch["mtok"][half])
                if half == 0:
                    for tk in fs:
                        kb.wait("act", tk)
                stok = kb.mark("act", act.activation(out=sg[:, cs], in_=bg[:, :], func=AF.Sigmoid))
                kb.wait("dve", stok)
                kb.wait("dve", ch["ltok"])
                if half == 0:
                    for tk in fp:
                        kb.wait("dve", tk)
                dve.tensor_tensor(out=sg[:, cs], in0=sg[:, cs], in1=bp[:, :], op=ALU.mult)
                atk = kb.mark("dve", dve.tensor_tensor(out=sg[:, cs], in0=sg[:, cs], in1=by[:, :], op=ALU.add))
                acc_free[half] = [atk]
                ptk = kb.mark("dve", dve.scalar_tensor_tensor(out=pre[:, cs], in0=bufs["x1"][:, cs], scalar=ALPHA, in1=sg[:, cs], op0=ALU.mult, op1=ALU.add))
            sg_ring.free[si] = [ptk]
            in_ring.free[ch["ii"]] = [ch["mtok"][1], ptk, ch["ttok"]]
            ch["hnd"] = ln.stats(pre, ptk)
            ch["pi"] = pi

        def stage_F(c, ch):
            sl = slice(c * 128, (c + 1) * 128)
            if last:
                ntok, toks = ln.finish(ch["hnd"], out_d.ap()[sl, :], None)
            else:
                ntok, toks = ln.finish(ch["hnd"], XA.ap()[sl, :], XT_D.ap()[:, :, sl].rearrange("k p t -> p k t"))
            pre_ring.free[ch["pi"]] = [ntok]
            for t_ in toks:
                out_toks[id(t_[0])] = t_

        def mk(c):
            ii, bufs, ltok, lptok = load_chunk(c)
            return dict(ii=ii, bufs=bufs, ltok=ltok, lptok=lptok)

        chs = {0: mk(0), 1: mk(1)}
        wd_tok, wg_tok = {}, {}
        for qi, q in enumerate(range(0, 22, 4)):
            n_ = min(4, 22 - q)
            tk = load_w_cast(Wd[:, q:q + n_, :], w_down.ap()[l, q * 128:(q + n_) * 128, :].rearrange("(k p) c -> p k c", p=128), kb.sem("p2b_wd%d" % qi))
            for k in range(q, q + n_):
                wd_tok[k] = tk
        for hf in range(2):
            tk = load_w_cast(Wg[:, hf * 4:(hf + 1) * 4, :], w_pg.ap()[l, hf * 512:(hf + 1) * 512, :].rearrange("(k p) c -> p k c", p=128), kb.sem("p2b_wg%d" % hf))
            for k in range(hf * 4, hf * 4 + 4):
                wg_tok[k] = tk
        wtok = load_w_cast(Wp[:, :, :], w_pp.ap()[l].rearrange("(k p) c -> p k c", p=128), s_w)
        stage_P(0, chs[0])
        for c in range(NCH):
            if c + 1 < NCH:
                stage_P(c + 1, chs[c + 1])
            stage_M(c, chs[c])
            if c >= 1:
                stage_F(c - 1, chs[c - 1])
                del chs[c - 1]
            stage_E(c, chs[c])
            if c + 2 < NCH:
                chs[c + 2] = mk(c + 2)
        stage_F(NCH - 1, chs[NCH - 1])
        return list(out_toks.values())

    with kb.scope():
        toks = phase_p0()
        barrier(toks)
    for l in range(n_layers):
        with kb.scope():
            toks = phase_p1a(l)
            barrier(toks)
        if stop == "p1a":
            break
        with kb.scope():
            toks = phase_na(l)
            barrier(toks)
        if stop in ("na", "na_ex"):
            break
        with kb.scope():
            w1d = p1d_weights(l)
            with kb.scope():
                toks = phase_ret(l)
                barrier(toks)
            if stop == "ret":
                break
            with kb.scope():
                toks = phase_p1d(l, w1d)
                barrier(toks)
        if stop == "p1d":
            break
        with kb.scope():
            toks = phase_p2a(l)
            barrier(toks)
        if stop == "p2a":
            break
        with kb.scope():
            toks = phase_p2b(l, last=(l == DEPTH - 1))
            barrier(toks)
        if stop == "p2b" and l == debug.get("stop_layer", 0):
            break

    s_fin = kb.sem("fin")
    fin = []
    for name in debug.get("dump", []):
        src = kb.dram[name]
        shp = list(src.shape)
        o = kb.dout("dbg_" + name, shp, src.dtype)
        fin.append(kb.dma("sp", o.ap(), src.ap(), s_fin))
    if not debug:
        pass
    for tk in fin[-1:]:
        kb.wait("sp", tk, force=True)
    return kb


def rope_tables(core):
    half = core % 2
    pos = (np.arange(NT, dtype=np.float64) + half * NT)
    inv = 10000.0 ** (-np.arange(64, dtype=np.float64) / 64.0)
    ang = (pos[:, None].astype(np.float32) * inv[None, :].astype(np.float32)).astype(np.float64)
    cos = np.cos(ang).T
    sin = np.sin(ang).T
    s = 128.0 ** -0.5
    cosf = np.concatenate([cos, cos], 0)
    sinsw = np.concatenate([sin, -sin], 0)
    return np.stack([cosf, sinsw, cosf * s, sinsw * s]).astype(np.float32)


def na_tables(rpb_l, parity):
    kc = np.arange(64)[:, None]
    c = np.arange(64)[None, :]
    cs = np.clip(c - 8, 0, 48)
    inwin = (kc >= cs) & (kc < cs + 16)
    off = np.clip(kc - c + 15, 0, 30)
    bi = np.full((8, 15, 64, 64), NEG, np.float32)
    for ro in range(15):
        g = rpb_l[:, ro][:, off]
        bi[:, 14 - ro] = np.where(inwin[None], g, np.float32(NEG))
    bb = np.full((8, NA_NSLOT, 64, 64), NEG, np.float32)
    for (kr, r), sl in NA_SLOT.items():
        if na_valid(parity, r, kr):
            bb[:, sl] = bi[:, 14 - (kr - r + 7)]
    def pack(a):
        n = a.shape[1]
        a = a.reshape(4, 2, n, 64, 64).transpose(0, 1, 3, 2, 4)
        return np.ascontiguousarray(a.reshape(4, 128, n * 64))
    return pack(bi), pack(bb)


def core_inputs(inputs, c, names):
    b, h = c // 2, c % 2
    sl = slice(h * NT, (h + 1) * NT)
    m = {}
    for n in names:
        if n == "x":
            m[n] = np.ascontiguousarray(inputs["x"][b, sl])
        elif n == "p":
            m[n] = np.ascontiguousarray(inputs["p"][:, b, sl])
        elif n == "rope":
            m[n] = rope_tables(c)
        elif n == "ident":
            m[n] = np.eye(128, dtype=np.float32)
        elif n == "rconst":
            i = np.arange(128)
            a1 = np.maximum(i[None, :] - i[:, None], 0)
            a2 = np.maximum(i[:, None] - i[None, :], 0)
            idx1 = np.broadcast_to(i[None, :] + 1, (128, 128))
            idx2 = np.broadcast_to(128 - i[None, :], (128, 128))
            pidx = np.stack([127 - i, i], 1)
            nidx = np.broadcast_to(128 * (31 - np.arange(32))[None, :], (128, 32))
            m[n] = np.ascontiguousarray(np.concatenate([a1, a2, idx1, idx2, pidx, nidx], 1).astype(np.float32))
        elif n == "convp":
            cw = inputs["conv_w"].reshape(DEPTH, 3, 44, 128)
            cb = inputs["conv_b"].reshape(DEPTH, 1, 44, 128)
            m[n] = np.ascontiguousarray(np.concatenate([cw, cb], 1).transpose(0, 3, 2, 1))
        elif n == "par":
            m[n] = np.ascontiguousarray(np.broadcast_to(np.array([h, 1 - h], np.float32)[None, :], (128, 2)))
        elif n == "na_bi":
            tabs = [na_tables(inputs["na_rpb"][l], h) for l in range(DEPTH)]
            m["na_bi"] = np.stack([t[0] for t in tabs])
            m["na_bb"] = np.stack([t[1] for t in tabs])
        elif n == "na_bb":
            pass
        else:
            m[n] = np.ascontiguousarray(inputs[n])
    return m


INPUT_NAMES = ["x", "p", "w_in", "rope", "ident", "na_bi", "na_bb", "ret_decay_f", "ret_decay_b", "rconst", "par",
               "w_branch_a", "w_branch_b", "w_out", "ln1_g", "ln1_b", "ln2_g", "ln2_b", "w_up", "convp", "w_down",
               "w_ple_gate", "w_ple_proj"]


_PROG = {}


def kernel(**inputs):
    inputs = {k: np.asarray(v) for k, v in inputs.items()}
    if "kb" not in _PROG:
        _PROG["kb"] = build_program(n_layers=DEPTH)
    kb = _PROG["kb"]
    names = [n for n in INPUT_NAMES if n in kb.dram]
    in_maps = [core_inputs(inputs, c, names) for c in range(8)]
    res = run_bass_kernel_spmd(kb.nc, in_maps, core_ids=list(range(8)))
    out = np.empty((4, 2 * NT, D), np.float32)
    for c in range(8):
        out[c // 2, (c % 2) * NT:(c % 2 + 1) * NT] = np.asarray(res.results[c]["out"], dtype=np.float32)
    return out
```

```python
import contextlib
import numpy as np
import ml_dtypes
import concourse.bass as bass
import concourse.mybir as mybir
from concourse.bass_utils import run_bass_kernel_spmd

F32 = mybir.dt.float32
BF16 = mybir.dt.bfloat16
AF = mybir.ActivationFunctionType
ALU = mybir.AluOpType

D = 1024
NT = 4096
NCH = 32
DEPTH = 4
DIN = 6656
DFF = 2816
PLE = 256
ALPHA = (2.0 * DEPTH) ** 0.25
LN_EPS = 1e-5
GN_EPS = 1e-6
NEG = -30000.0
PAIRS = [[0, 1], [2, 3], [4, 5], [6, 7]]


def na_superset(r):
    if r <= 3:
        return list(range(r - 4, 8))
    if r >= 61:
        return list(range(56, r + 4))
    return list(range(r - 4, r + 4))


def na_valid(parity, r, kr):
    R = r + 64 * parity
    rs = min(max(R - 4, 0), 120)
    KR = kr + 64 * parity
    return rs <= KR <= rs + 7


NA_BND = (0, 1, 2, 3, 61, 62, 63)
NA_QR = {kr: [r for r in range(64) if kr in na_superset(r)] for kr in range(-4, 67)}
NA_SLOT = {}
for _kr in range(-4, 67):
    for _r in NA_QR[_kr]:
        if _r in NA_BND:
            NA_SLOT[(_kr, _r)] = len(NA_SLOT)
NA_NSLOT = len(NA_SLOT)


class Sem:
    def __init__(self, kb, name):
        self.h = kb.es.enter_context(kb.nc.semaphore(name))
        self.n = 0


class KB:
    def __init__(self):
        self.nc = bass.Bass("TRN2", target_bir_lowering=False)
        self.es = contextlib.ExitStack()
        self.cur = self.es
        self.sems = {}
        nc = self.nc
        self.eng = {"pe": nc.tensor, "act": nc.scalar, "dve": nc.vector, "pool": nc.gpsimd, "sp": nc.sync}
        self.prog = {e: Sem(self, "prog_" + e) for e in self.eng}
        self.waited = {}
        self.nsem = len(self.eng)
        self.dram = {}

    def uniq(self, name):
        self.ucnt = getattr(self, "ucnt", 0) + 1
        return "%s_u%d" % (name, self.ucnt)

    def sb(self, name, shape, dt=F32):
        return self.cur.enter_context(self.nc.sbuf_tensor(self.uniq(name), list(shape), dt))

    def ps(self, name, shape, dt=F32):
        return self.cur.enter_context(self.nc.psum_tensor(self.uniq(name), list(shape), dt))

    def sem(self, name):
        if name not in self.sems:
            self.nsem += 1
            self.sems[name] = Sem(self, name)
        return self.sems[name]

    @contextlib.contextmanager
    def scope(self):
        old = self.cur
        with contextlib.ExitStack() as st:
            self.cur = st
            yield
        self.cur = old

    def din(self, name, shape, dt=F32):
        t = self.nc.dram_tensor(name, list(shape), dt, kind="ExternalInput")
        self.dram[name] = t
        return t

    def dout(self, name, shape, dt=F32):
        t = self.nc.dram_tensor(name, list(shape), dt, kind="ExternalOutput")
        self.dram[name] = t
        return t

    def dscr(self, name, shape, dt=BF16):
        t = self.nc.dram_tensor(name, list(shape), dt)
        self.dram[name] = t
        return t

    def mark(self, e, ins):
        s = self.prog[e]
        ins.then_inc(s.h, 1)
        s.n += 1
        return (s, s.n)

    def wait(self, e, tok, force=False):
        if tok is None:
            return
        s, v = tok
        if v <= 0:
            return
        if s is self.prog[e] and not force:
            return
        key = (e, id(s))
        if self.waited.get(key, 0) >= v:
            return
        self.waited[key] = v
        self.eng[e].wait_ge(s.h, v)

    def dma(self, e, out, in_, sem, **kw):
        ins = self.eng[e].dma_start(out=out, in_=in_, **kw)
        ins.then_inc(sem.h, 16)
        sem.n += 16
        return (sem, sem.n)


class Ring:
    def __init__(self, bufs):
        self.bufs = bufs
        self.n = len(bufs)
        self.k = 0
        self.free = [[] for _ in bufs]

    def next(self):
        i = self.k % self.n
        self.k += 1
        fr = self.free[i]
        self.free[i] = []
        return i, self.bufs[i], fr


def build_program(n_layers=DEPTH, debug=None, pairs=PAIRS):
    kb = KB()
    nc = kb.nc
    debug = debug or {}
    stop = debug.get("stop")

    x_in = kb.din("x", [NT, D])
    p_in = kb.din("p", [DEPTH, NT, PLE])
    w_in = kb.din("w_in", [DEPTH, D, DIN])
    rope = kb.din("rope", [4, 128, NT])
    ident_d = kb.din("ident", [128, 128])
    w_ba = kb.din("w_branch_a", [DEPTH, 512, D])
    w_bb = kb.din("w_branch_b", [DEPTH, 1024, D])
    w_out = kb.din("w_out", [DEPTH, D, D])
    ln1_g = kb.din("ln1_g", [DEPTH, D])
    ln1_b = kb.din("ln1_b", [DEPTH, D])
    ln2_g = kb.din("ln2_g", [DEPTH, D])
    ln2_b = kb.din("ln2_b", [DEPTH, D])
    w_up = kb.din("w_up", [DEPTH, D, 2 * DFF])
    convp = kb.din("convp", [DEPTH, 128, 44, 4])
    w_down = kb.din("w_down", [DEPTH, DFF, D])
    w_pg = kb.din("w_ple_gate", [DEPTH, D, D])
    w_pp = kb.din("w_ple_proj", [DEPTH, PLE, D])
    dec_f = kb.din("ret_decay_f", [DEPTH, 4])
    dec_b = kb.din("ret_decay_b", [DEPTH, 4])
    rconst = kb.din("rconst", [128, 4 * 128 + 2 + 32])
    par_d = kb.din("par", [128, 2])
    na_bi = kb.din("na_bi", [DEPTH, 4, 128, 15 * 64])
    na_bb = kb.din("na_bb", [DEPTH, 4, 128, NA_NSLOT * 64])
    out_d = kb.dout("out", [NT, D])

    XT_D = kb.dscr("XT_D", [8, 128, NT])
    XA = kb.dscr("XA", [NT, D], F32)
    QNA_T = kb.dscr("QNA_T", [4, 128, NT])
    KNA_T = kb.dscr("KNA_T", [4, 128, NT])
    VNA = kb.dscr("VNA", [NT, 512])
    QR_T = kb.dscr("QR_T", [4, 128, NT])
    KR_T = kb.dscr("KR_T", [4, 128, NT])
    VR = kb.dscr("VR", [NT, 1024])
    SG = kb.dscr("SG", [NT, 1024])
    GA_T = kb.dscr("GA_T", [8, 128, NT])
    GB_T = kb.dscr("GB_T", [8, 128, NT])
    NAB = kb.dscr("NAB", [4, 131072])
    NAG = kb.dscr("NAG", [8, 131072])
    YA_T = kb.dscr("YA_T", [4, 128, NT])
    YB_T = kb.dscr("YB_T", [8, 128, NT])
    X1 = kb.dscr("X1", [NT, D], F32)
    X1T_D = kb.dscr("X1T_D", [8, 128, NT])
    U_T = kb.dscr("U_T", [22, 128, NT])
    CXB = kb.dscr("CXB", [2, 1024])
    CXG = kb.dscr("CXG", [4, 1024])
    RSB = kb.dscr("RSB", [2, 131072], F32)
    RSG = kb.dscr("RSG", [4, 131072], F32)

    ident_f = kb.sb("ident_f", [128, 128], F32)
    ident_b = kb.sb("ident_b", [128, 128], BF16)
    psb = [kb.ps("psb%d" % i, [128, 512], F32) for i in range(6)]
    ptb = [kb.ps("ptb%d" % i, [128, 1024], BF16) for i in range(2)]

    s_misc = kb.sem("misc")
    t = kb.dma("sp", ident_f[:, :], ident_d.ap()[:, :], s_misc)
    kb.wait("dve", t)
    tok_ident = kb.mark("dve", nc.vector.tensor_copy(out=ident_b[:, :], in_=ident_f[:, :]))

    def barrier(toks):
        for e in kb.eng:
            for tk in toks:
                kb.wait(e, tk)

    pend = []

    def transpose_chunk(src_bf, src_tok, dst_ap, pbank, pbank_free, evac_e):
        kb.wait("pe", src_tok)
        for tk in pbank_free:
            kb.wait("pe", tk)
        kb.wait("pe", tok_ident)
        pv = pbank
        ins = None
        for kc in range(8):
            ins = nc.tensor.transpose(pv[:, kc * 128:(kc + 1) * 128], src_bf[:, kc * 128:(kc + 1) * 128], ident_b[:, :])
        pt = kb.mark("pe", ins)
        kb.wait(evac_e, pt)
        src = pv[:, 0:1024].rearrange("p (k t) -> p k t", k=8)
        if evac_e == "act":
            ins = nc.scalar.copy(out=dst_ap, in_=src)
        else:
            ins = nc.vector.tensor_copy(out=dst_ap, in_=src)
        et = kb.mark(evac_e, ins)
        return pt, et

    def phase_p0():
        xb_ring = Ring([kb.sb("p0_xb%d" % i, [128, D], BF16) for i in range(2)])
        xb_sem = [kb.sem("p0_xbs%d" % i) for i in range(2)]
        xt_ring = Ring([kb.sb("p0_xt%d" % i, [128, 8, 128], BF16) for i in range(2)])
        xt_sem = [kb.sem("p0_xts%d" % i) for i in range(2)]
        pfree = [[], []]
        toks = []
        s_xa = kb.sem("p0_xa")
        for q in range(4):
            toks.append(kb.dma("sp", XA.ap()[q * 1024:(q + 1) * 1024, :], x_in.ap()[q * 1024:(q + 1) * 1024, :], s_xa))
        toks = toks[-1:]
        for c in range(NCH):
            i, xb, fr = xb_ring.next()
            for tk in fr:
                kb.wait("pool", tk)
            lt = kb.dma("pool", xb[:, :], x_in.ap()[c * 128:(c + 1) * 128, :], xb_sem[i])
            j, xt, fr2 = xt_ring.next()
            e = "act" if c % 2 == 0 else "dve"
            for tk in fr2:
                kb.wait(e, tk)
            pt, et = transpose_chunk(xb, lt, xt[:, :, :], ptb[c % 2], pfree[c % 2], e)
            pfree[c % 2] = [et]
            xb_ring.free[i] = [pt]
            kb.wait("sp", et)
            st = kb.dma("sp", XT_D.ap()[:, :, c * 128:(c + 1) * 128].rearrange("k p t -> p k t"), xt[:, :, :], xt_sem[j])
            xt_ring.free[j] = [st]
            toks.append(st)
        return toks

    shared = {}

    def na_exchange(after):
        s_cc = kb.sem("cc")
        s_ex = kb.sem("na_ex")
        for tk in after:
            kb.wait("sp", tk)
        kb.dma("sp", NAB.ap()[0, :].rearrange("(a p t) -> a p t", a=4, p=128), KNA_T.ap()[:, :, 0:256], s_ex)
        kb.dma("sp", NAB.ap()[1, :].rearrange("(a p t) -> a p t", a=4, p=128), KNA_T.ap()[:, :, NT - 256:NT], s_ex)
        kb.dma("sp", NAB.ap()[2, :].rearrange("(t c) -> t c", c=512), VNA.ap()[0:256, :], s_ex)
        t = kb.dma("sp", NAB.ap()[3, :].rearrange("(t c) -> t c", c=512), VNA.ap()[NT - 256:NT, :], s_ex)
        kb.wait("pool", t)
        ins = nc.gpsimd.collective_compute("AllGather", ALU.bypass, replica_groups=pairs,
                                           ins=[NAB.ap().opt()], outs=[NAG.ap().opt()])
        ins.then_inc(s_cc.h)
        s_cc.n += 1
        return (s_cc, s_cc.n)

    def phase_p1a(l):
        XT = kb.sb("p1a_XT", [128, 8, NT], BF16)
        s_xt = kb.sem("p1a_xt")
        lt = None
        for kc in range(8):
            lt = kb.dma("sp", XT[:, kc, :], XT_D.ap()[kc, :, :], s_xt)
        xt_tok = lt
        wg_ring = Ring([kb.sb("p1a_wg%d" % i, [128, 8, 512], BF16) for i in range(2)])
        wg_sem = [kb.sem("p1a_wgs%d" % i) for i in range(2)]
        rp_ring = Ring([kb.sb("p1a_rp%d" % i, [128, 2, 512], F32) for i in range(2)])
        rp_sem = [kb.sem("p1a_rps%d" % i) for i in range(2)]
        st_ring = Ring([kb.sb("p1a_st%d" % i, [128, 512], BF16) for i in range(4)])
        st_sem = [kb.sem("p1a_sts%d" % i) for i in range(4)]
        tmp_ring = Ring([kb.sb("p1a_tmp%d" % i, [128, 2, 512], F32) for i in range(2)])
        ps_ring = Ring(psb[0:4])
        out_toks = []
        ev_alt = [0]

        def load_w(g):
            i, wg, fr = wg_ring.next()
            for tk in fr:
                kb.wait("pool", tk)
            src = w_in.ap()[l, :, g * 512:(g + 1) * 512].rearrange("(k p) c -> p k c", p=128)
            return wg, kb.dma("pool", wg[:, :, :], src, wg_sem[i]), i

        groups = list(range(13))
        nxt = load_w(groups[0])
        for gi, g in enumerate(groups):
            wg, wtok, wi = nxt
            if gi + 1 < len(groups):
                nxt = load_w(groups[gi + 1])
            fm = g in (0, 1, 3, 4, 9, 10, 11, 12)
            last_pe = None
            if g == 5 and stop != "p1a":
                shared["na_cc"] = na_exchange([tk for fr in st_ring.free for tk in fr])
            for tile in range(32):
                if fm:
                    if g in (3, 4):
                        tt, fb = tile // 4, tile % 4
                    else:
                        fb, tt = tile // 8, tile % 8
                if g in (3, 4) and fb == 0:
                    def ld_rope(tt_):
                        ri, rp, fr = rp_ring.next()
                        for tk in fr:
                            kb.wait("sp", tk)
                        base = 0 if g == 3 else 2
                        rtok = kb.dma("sp", rp[:, :, :], rope.ap()[base:base + 2, :, tt_ * 512:(tt_ + 1) * 512].rearrange("a p t -> p a t"), rp_sem[ri])
                        return (rp, rtok, ri)
                    if tt == 0:
                        rp_nxt = ld_rope(0)
                    rp_cur = rp_nxt
                    if tt + 1 < 8:
                        rp_nxt = ld_rope(tt + 1)
                pi, pb, pfr = ps_ring.next()
                for tk in pfr:
                    kb.wait("pe", tk)
                kb.wait("pe", wtok)
                kb.wait("pe", xt_tok)
                ins = None
                for kc in range(8):
                    if fm:
                        ins = nc.tensor.matmul(pb[:, :], lhsT=wg[:, kc, fb * 128:(fb + 1) * 128], rhs=XT[:, kc, tt * 512:(tt + 1) * 512],
                                               start=(kc == 0), stop=(kc == 7))
                    else:
                        ins = nc.tensor.matmul(pb[:, :], lhsT=XT[:, kc, tile * 128:(tile + 1) * 128], rhs=wg[:, kc, :],
                                               start=(kc == 0), stop=(kc == 7))
                ptok = kb.mark("pe", ins)
                last_pe = ptok
                si, stg, sfr = st_ring.next()
                if g in (3, 4):
                    rp, rtok, ri = rp_cur
                    ti, tmp, tfr = tmp_ring.next()
                    kb.wait("dve", ptok)
                    kb.wait("dve", rtok)
                    for tk in tfr:
                        kb.wait("dve", tk)
                    nc.vector.tensor_tensor(out=tmp[:, 0, :], in0=pb[:, :], in1=rp[:, 0, :], op=ALU.mult)
                    nc.vector.tensor_tensor(out=tmp[0:64, 1, :], in0=pb[64:128, :], in1=rp[64:128, 1, :], op=ALU.mult)
                    ins = nc.vector.tensor_tensor(out=tmp[64:128, 1, :], in0=pb[0:64, :], in1=rp[0:64, 1, :], op=ALU.mult)
                    dtok = kb.mark("dve", ins)
                    ps_ring.free[pi] = [dtok]
                    if fb == 3:
                        rp_ring.free[ri] = [dtok]
                    kb.wait("pool", dtok)
                    for tk in sfr:
                        kb.wait("pool", tk)
                    ins = nc.gpsimd.tensor_tensor(out=stg[:, :], in0=tmp[:, 0, :], in1=tmp[:, 1, :], op=ALU.add)
                    etok = kb.mark("pool", ins)
                    tmp_ring.free[ti] = [etok]
                else:
                    if g in (7, 8, 9, 10, 11, 12, 0):
                        e = "act"
                    elif g in (5, 6):
                        e = "act" if tile % 2 == 0 else "dve"
                    else:
                        e = "dve"
                    kb.wait(e, ptok)
                    for tk in sfr:
                        kb.wait(e, tk)
                    if e == "act":
                        if g == 0:
                            ins = nc.scalar.mul(out=stg[:, :], in_=pb[:, :], mul=0.125)
                        elif g in (7, 8):
                            ins = nc.scalar.activation(out=stg[:, :], in_=pb[:, :], func=AF.Silu)
                        elif g in (9, 10, 11, 12):
                            ins = nc.scalar.activation(out=stg[:, :], in_=pb[:, :], func=AF.Sigmoid)
                        else:
                            ins = nc.scalar.copy(out=stg[:, :], in_=pb[:, :])
                    else:
                        ins = nc.vector.tensor_copy(out=stg[:, :], in_=pb[:, :])
                    etok = kb.mark(e, ins)
                    ps_ring.free[pi] = [etok]
                if g == 0:
                    dst = QNA_T.ap()[fb, :, tt * 512:(tt + 1) * 512]
                elif g == 1:
                    dst = KNA_T.ap()[fb, :, tt * 512:(tt + 1) * 512]
                elif g == 2:
                    dst = VNA.ap()[tile * 128:(tile + 1) * 128, :]
                elif g == 3:
                    dst = QR_T.ap()[fb, :, tt * 512:(tt + 1) * 512]
                elif g == 4:
                    dst = KR_T.ap()[fb, :, tt * 512:(tt + 1) * 512]
                elif g in (5, 6):
                    dst = VR.ap()[tile * 128:(tile + 1) * 128, (g - 5) * 512:(g - 4) * 512]
                elif g in (7, 8):
                    dst = SG.ap()[tile * 128:(tile + 1) * 128, (g - 7) * 512:(g - 6) * 512]
                elif g in (9, 10):
                    dst = GA_T.ap()[(g - 9) * 4 + fb, :, tt * 512:(tt + 1) * 512]
                else:
                    dst = GB_T.ap()[(g - 11) * 4 + fb, :, tt * 512:(tt + 1) * 512]
                kb.wait("sp", etok)
                stok = kb.dma("sp", dst, stg[:, :], st_sem[si])
                st_ring.free[si] = [stok]
            wg_ring.free[wi] = [last_pe]
        for si in range(4):
            out_toks += st_ring.free[si]
        return out_toks


    def phase_na(l):
        if "na_cc" in shared:
            cc_tok = shared.pop("na_cc")
        else:
            cc_tok = na_exchange([])
        kb.wait("sp", cc_tok)
        if stop == "na_ex":
            return [cc_tok]

        NK = 71
        bi = kb.sb("na_bi", [128, 4, 15 * 64], F32)
        s_bi = kb.sem("na_bis")
        for hp in range(4):
            bi_tok = kb.dma("sp", bi[:, hp, :], na_bi.ap()[l, hp, :, :], s_bi)
        bb_ring = [kb.sb("na_bb%d" % i, [128, NA_NSLOT * 64], F32) for i in range(2)]
        kt_ring = [kb.sb("na_kt%d" % i, [128, 72, 128], BF16) for i in range(2)]
        vt_ring = [kb.sb("na_vt%d" % i, [128, 72, 130], BF16) for i in range(2)]
        qt_ring = [kb.sb("na_qt%d" % i, [128, NT], BF16) for i in range(2)]
        ld_sem = [kb.sem("na_lds%d" % i) for i in range(2)]
        ones_tok = [None, None]
        for i in range(2):
            ktk = kb.mark("dve", nc.vector.memset(kt_ring[i][:, :, :], 0.0))
            zt = kb.mark("pool", nc.gpsimd.memset(vt_ring[i][:, :, :], 0.0))
            kb.wait("pool", zt, force=True)
            nc.gpsimd.memset(vt_ring[i][0:64, :, 64:65], 1.0)
            ones_tok[i] = (ktk, kb.mark("pool", nc.gpsimd.memset(vt_ring[i][64:128, :, 129:130], 1.0)))
        z_ring = Ring([kb.sb("na_z%d" % i, [128, 768], F32) for i in range(2)])
        pt_ring = [kb.sb("na_pt%d" % i, [128, 768], BF16) for i in range(16)]
        pt_free = [[] for _ in range(16)]
        rs_ring = Ring([kb.sb("na_rs%d" % i, [64, 6], F32) for i in range(2)])
        ya_ring = Ring([kb.sb("na_ya%d" % i, [64, 6, 64], BF16) for i in range(2)])
        stg_ring = Ring([kb.sb("na_stg%d" % i, [128, 192], BF16) for i in range(3)])
        stg_sem = [kb.sem("na_stgs%d" % i) for i in range(3)]
        s_tiles = [kb.ps("na_s%d" % i, [128, 1024], F32) for i in range(2)] if False else None
        s_ring = Ring([(psb[0], psb[1]), (psb[2], psb[3])])
        acc_ring = Ring([psb[4], psb[5]])
        tp_ring = Ring(ptb)

        def load_hp(hp, slot, free_toks):
            for tk in free_toks:
                kb.wait("sp", tk)
            kt, vt, qt, bb, sm = kt_ring[slot], vt_ring[slot], qt_ring[slot], bb_ring[slot], ld_sem[slot]
            kb.wait("sp", ones_tok[slot][0])
            kb.wait("sp", ones_tok[slot][1])
            kb.dma("sp", qt[:, :], QNA_T.ap()[hp, :, :], sm)
            kb.dma("sp", bb[:, :], na_bb.ap()[l, hp, :, :], sm)
            ktop = NAG.ap()[1, :].rearrange("(a p t) -> a p t", a=4, p=128)[hp]
            kbot = NAG.ap()[4, :].rearrange("(a p t) -> a p t", a=4, p=128)[hp]
            vtop = NAG.ap()[3, :].rearrange("(t c) -> t c", c=512)
            vbot = NAG.ap()[6, :].rearrange("(t c) -> t c", c=512)
            for h in range(2):
                ps_ = slice(h * 64, (h + 1) * 64)
                kc_ = slice(h * 64, (h + 1) * 64)
                ksrc = KNA_T.ap()[hp, ps_, :].rearrange("p (r c) -> p r c", c=64)
                for r16 in range(4):
                    kb.dma("sp", kt[ps_, 4 + r16 * 16:20 + r16 * 16, kc_], ksrc[:, r16 * 16:(r16 + 1) * 16, :], sm)
                kb.dma("sp", kt[ps_, 0:4, kc_], ktop[ps_, :].rearrange("p (r c) -> p r c", c=64), sm)
                kb.dma("sp", kt[ps_, 68:72, kc_], kbot[ps_, :].rearrange("p (r c) -> p r c", c=64), sm)
                c0 = hp * 128 + h * 64
                vc_ = slice(h * 65, h * 65 + 64)
                vsrc = VNA.ap()[:, c0:c0 + 64].rearrange("(r c) d -> c r d", c=64)
                for r16 in range(4):
                    kb.dma("sp", vt[ps_, 4 + r16 * 16:20 + r16 * 16, vc_], vsrc[:, r16 * 16:(r16 + 1) * 16, :], sm)
                kb.dma("sp", vt[ps_, 0:4, vc_], vtop[:, c0:c0 + 64].rearrange("(r c) d -> c r d", c=64), sm)
                tk = kb.dma("sp", vt[ps_, 68:72, vc_], vbot[:, c0:c0 + 64].rearrange("(r c) d -> c r d", c=64), sm)
            return tk

        out_toks = []
        hp_free = [[], []]
        ld_tok = [None, None]
        ld_tok[0] = load_hp(0, 0, [])
        for hp in range(4):
            slot = hp % 2
            if hp + 1 < 4:
                ld_tok[1 - slot] = load_hp(hp + 1, 1 - slot, hp_free[1 - slot])
            kt, vt, qt, bb = kt_ring[slot], vt_ring[slot], qt_ring[slot], bb_ring[slot]
            ltok = ld_tok[slot]
            exp_tok = {}
            last_pv_pe = None
            last_dve_read = None

            def qk(kidx):
                kr = kidx - 4
                rows = NA_QR[kr]
                qlo, qhi = rows[0], rows[-1]
                nq = (qhi - qlo + 1) * 64
                si, (b0, b1), sfr = s_ring.next()
                for tk in sfr:
                    kb.wait("pe", tk)
                kb.wait("pe", ltok)
                ins = None
                for seg, bank in ((0, b0), (1, b1)):
                    c0 = seg * 512
                    if c0 >= nq:
                        continue
                    w = min(512, nq - c0)
                    ins = nc.tensor.matmul(bank[:, 0:w], lhsT=kt[:, kidx, :],
                                           rhs=qt[:, qlo * 64 + c0:qlo * 64 + c0 + w], start=True, stop=True)
                ptok = kb.mark("pe", ins)
                zi, z, zfr = z_ring.next()
                kb.wait("dve", ptok)
                kb.wait("dve", ltok)
                kb.wait("dve", bi_tok)
                for tk in zfr:
                    kb.wait("dve", tk)
                parts = []
                for r in rows:
                    kind = "b" if r in NA_BND else "i"
                    if parts and parts[-1][0] == kind:
                        parts[-1][2] = r
                    else:
                        parts.append([kind, r, r])
                ins = None
                for kind, r0, r1 in parts:
                    a0, a1 = (r0 - qlo) * 64, (r1 - qlo + 1) * 64
                    pieces = []
                    if a0 < 512 < a1:
                        pieces = [(a0, 512), (512, a1)]
                    else:
                        pieces = [(a0, a1)]
                    for (c0, c1) in pieces:
                        bank = b0 if c0 < 512 else b1
                        off = 0 if c0 < 512 else 512
                        rr = qlo + c0 // 64
                        if kind == "i":
                            t0 = (7 + rr - kr) * 64
                            tab = bi[:, hp, t0:t0 + (c1 - c0)]
                        else:
                            t0 = NA_SLOT[(kr, rr)] * 64
                            tab = bb[:, t0:t0 + (c1 - c0)]
                        ins = nc.vector.tensor_tensor(out=z[:, c0:c1], in0=bank[:, c0 - off:c1 - off], in1=tab, op=ALU.add)
                dtok = kb.mark("dve", ins)
                s_ring.free[si] = [dtok]
                pslot = kidx % 16
                kb.wait("act", dtok)
                for tk in pt_free[pslot]:
                    kb.wait("act", tk)
                pt_free[pslot] = []
                ins = nc.scalar.activation(out=pt_ring[pslot][:, 0:nq], in_=z[:, 0:nq], func=AF.Exp)
                etok = kb.mark("act", ins)
                z_ring.free[zi] = [etok]
                exp_tok[kidx] = etok
                return dtok

            def pv_group(rows3):
                ai, acc, afr = acc_ring.next()
                for tk in afr:
                    kb.wait("pe", tk)
                ins = None
                for j, r in enumerate(rows3):
                    ks = na_superset(r)
                    kb.wait("pe", exp_tok[ks[-1] + 4])
                    for n_, kr in enumerate(ks):
                        kidx = kr + 4
                        qlo = NA_QR[kr][0]
                        c0 = (r - qlo) * 64
                        ins = nc.tensor.matmul(acc[0:64, j * 130:(j + 1) * 130],
                                               lhsT=pt_ring[kidx % 16][:, c0:c0 + 64], rhs=vt[:, kidx, :],
                                               start=(n_ == 0 and j == 0), stop=(n_ == len(ks) - 1), skip_group_check=True)
                ptok = kb.mark("pe", ins)
                for r in rows3:
                    for kr in na_superset(r):
                        pt_free[(kr + 4) % 16] = [ptok]
                ng = len(rows3)

                def fin():
                    return pv_fin(rows3, ai, acc, ptok, ng)
                return ptok, fin

            def pv_fin(rows3, ai, acc, ptok, ng):
                ri, rs, rfr = rs_ring.next()
                yi, ya, yfr = ya_ring.next()
                kb.wait("dve", ptok)
                for tk in rfr + yfr:
                    kb.wait("dve", tk)
                accv = acc[0:64, 0:ng * 130].rearrange("p (g e) -> p g e", e=65)
                rtk = kb.mark("dve", nc.vector.reciprocal(out=rs[:, 0:2 * ng], in_=accv[:, :, 64]))
                kb.wait("dve", rtk, force=True)
                ins = nc.vector.tensor_tensor(out=ya[:, 0:2 * ng, :], in0=accv[:, :, 0:64],
                                              in1=rs[:, 0:2 * ng].rearrange("p (g o) -> p g o", o=1).broadcast_to([64, 2 * ng, 64]), op=ALU.mult)
                ntok = kb.mark("dve", ins)
                acc_ring.free[ai] = [ntok]
                ti, tp, tfr = tp_ring.next()
                kb.wait("pe", ntok)
                for tk in tfr:
                    kb.wait("pe", tk)
                yav = ya[:, :, :].rearrange("p g d -> p (g d)")
                for j in range(ng):
                    ins = nc.tensor.transpose(tp[:, j * 64:(j + 1) * 64], yav[0:64, j * 128:(j + 1) * 128], ident_b[0:64, 0:64])
                ttok = kb.mark("pe", ins)
                rs_ring.free[ri] = [ntok]
                ya_ring.free[yi] = [ttok]
                gi, stg, gfr = stg_ring.next()
                kb.wait("act", ttok)
                for tk in gfr:
                    kb.wait("act", tk)
                ins = nc.scalar.copy(out=stg[:, 0:ng * 64], in_=tp[:, 0:ng * 64])
                ctok = kb.mark("act", ins)
                tp_ring.free[ti] = [ctok]
                kb.wait("sp", ctok)
                r0 = rows3[0]
                stok = kb.dma("sp", YA_T.ap()[hp, :, r0 * 64:(r0 + ng) * 64], stg[:, 0:ng * 64], stg_sem[gi])
                stg_ring.free[gi] = [stok]

            groups = [list(range(r0, min(r0 + 3, 64))) for r0 in range(0, 64, 3)]
            gnext = 0
            LAG = 2
            pending = []
            for kidx in range(NK + LAG):
                if kidx < NK:
                    last_dve_read = qk(kidx)
                for f in pending:
                    f()
                pending = []
                done_kr = kidx - LAG - 4
                while gnext < len(groups) and max(na_superset(groups[gnext][-1])) <= done_kr:
                    for f in pending:
                        f()
                    pending = []
                    last_pv_pe, f = pv_group(groups[gnext])
                    pending.append(f)
                    gnext += 1
            for f in pending:
                f()
            assert gnext == len(groups)
            hp_free[slot] = [last_pv_pe, last_dve_read]
        for gi in range(3):
            out_toks += stg_ring.free[gi]
        return out_toks


    def phase_ret(l):
        s_cc = kb.sem("cc")
        s_ld = kb.sem("ret_c")
        dve, act, pool, pe = nc.vector, nc.scalar, nc.gpsimd, nc.tensor

        def bc(ap2, n, g=4):
            return ap2.rearrange("p (g o) -> p g o", o=1).broadcast_to([128, g, n])

        rc = kb.sb("ret_rc", [128, 4 * 128 + 2 + 32], F32)
        dfb = kb.sb("ret_dfb", [128, 8], F32)
        par = kb.sb("ret_par", [128, 2], F32)
        kb.dma("sp", rc[:, :], rconst.ap()[:, :], s_ld)
        kb.dma("sp", dfb[:, 0:4], dec_f.ap()[l, :].partition_broadcast(128), s_ld)
        kb.dma("sp", dfb[:, 4:8], dec_b.ap()[l, :].partition_broadcast(128), s_ld)
        t0 = kb.dma("sp", par[:, :], par_d.ap()[:, :], s_ld)
        A1, A2, IDX1, IDX2 = rc[:, 0:128], rc[:, 128:256], rc[:, 256:384], rc[:, 384:512]
        PIDX, NIDX = rc[:, 512:514], rc[:, 514:546]
        lg = kb.sb("ret_lg", [128, 8], F32)
        sg_ = kb.sb("ret_sg", [128, 8], F32)
        kb.wait("act", t0)
        tk = kb.mark("act", act.activation(out=sg_[:, :], in_=dfb[:, :], func=AF.Sigmoid))
        kb.wait("act", tk, force=True)
        tk = kb.mark("act", act.activation(out=lg[:, :], in_=sg_[:, :], func=AF.Ln))
        kb.wait("act", tk, force=True)
        kb.wait("dve", tk)
        DT = kb.sb("ret_DT", [128, 4, 128], F32)
        XF = kb.sb("ret_XF", [128, 4, 128], F32)
        XB = kb.sb("ret_XB", [128, 4, 128], F32)
        ZF = kb.sb("ret_ZF", [128, 4], F32)
        ZB = kb.sb("ret_ZB", [128, 4], F32)
        GC = kb.sb("ret_GC", [128, 8], F32)
        CDF = kb.sb("ret_CDF", [128, 4, 32], F32)
        CDB = kb.sb("ret_CDB", [128, 4, 32], F32)
        ZFn = kb.sb("ret_ZFn", [128, 4, 32], F32)
        tmpa = kb.sb("ret_tmpa", [128, 4, 128], F32)
        tmpz = kb.sb("ret_tmpz", [128, 8], F32)
        for h in range(4):
            dve.tensor_scalar(out=tmpa[:, h, :], in0=A1, scalar1=lg[:, h:h + 1], scalar2=None, op0=ALU.mult)
        tk = kb.mark("dve", dve.tensor_copy(out=tmpz[:, 0:1], in_=lg[:, 0:1]))
        kb.wait("dve", tk, force=True)
        for h in range(4):
            dve.scalar_tensor_tensor(out=tmpa[:, h, :], in0=A2, scalar=lg[:, 4 + h:5 + h], in1=tmpa[:, h, :], op0=ALU.mult, op1=ALU.add)
            dve.tensor_scalar(out=tmpz[:, h:h + 1], in0=PIDX[:, 0:1], scalar1=lg[:, h:h + 1], scalar2=None, op0=ALU.mult)
            dve.tensor_scalar(out=tmpz[:, 4 + h:5 + h], in0=PIDX[:, 1:2], scalar1=lg[:, 4 + h:5 + h], scalar2=None, op0=ALU.mult)
            dve.tensor_scalar(out=CDF[:, h, :], in0=NIDX, scalar1=lg[:, h:h + 1], scalar2=None, op0=ALU.mult)
            dve.tensor_scalar(out=ZFn[:, h, :], in0=NIDX, scalar1=PIDX[:, 0:1], scalar2=lg[:, h:h + 1], op0=ALU.add, op1=ALU.mult)
            tk = kb.mark("dve", dve.tensor_scalar(out=CDB[:, h, :], in0=NIDX, scalar1=lg[:, 4 + h:5 + h], scalar2=None, op0=ALU.mult))
        kb.wait("act", tk)
        act.activation(out=DT[:, :, :], in_=tmpa[:, :, :], func=AF.Exp)
        act.activation(out=ZF[:, :], in_=tmpz[:, 0:4], func=AF.Exp)
        act.activation(out=ZB[:, :], in_=tmpz[:, 4:8], func=AF.Exp)
        act.activation(out=CDF[:, :, :], in_=CDF[:, :, :], func=AF.Exp)
        act.activation(out=CDB[:, :, :], in_=CDB[:, :, :], func=AF.Exp)
        act.activation(out=ZFn[:, :, :], in_=ZFn[:, :, :], func=AF.Exp)
        act.activation(out=GC[:, :], in_=lg[:, :], func=AF.Exp, scale=128.0)
        for h in range(4):
            act.activation(out=XF[:, h, :], in_=IDX1, func=AF.Exp, scale=lg[:, h:h + 1])
            tk = kb.mark("act", act.activation(out=XB[:, h, :], in_=IDX2, func=AF.Exp, scale=lg[:, 4 + h:5 + h]))
        tab_tok = tk

        SB_all = kb.sb("ret_SBall", [128, 32, 4, 256], BF16)
        S_run = kb.sb("ret_Srun", [128, 4, 256], F32)
        E_f = kb.sb("ret_Ef", [128, 4, 256], F32)
        Sf_bf = [kb.sb("ret_Sfbf%d" % i, [128, 4, 256], BF16) for i in range(2)]
        kt_ring = Ring([kb.sb("ret_kt%d" % i, [128, 4, 128], BF16) for i in range(3)])
        qt_ring = Ring([kb.sb("ret_qt%d" % i, [128, 4, 128], BF16) for i in range(3)])
        v_ring = Ring([kb.sb("ret_v%d" % i, [128, 4, 256], BF16) for i in range(3)])
        g_ring = Ring([kb.sb("ret_g%d" % i, [128, 4, 256], BF16) for i in range(3)])
        ld_sem = [kb.sem("ret_lds%d" % i) for i in range(3)]
        kzf_ring = Ring([kb.sb("ret_kzf%d" % i, [128, 4, 128], BF16) for i in range(2)])
        kzb_ring = Ring([kb.sb("ret_kzb%d" % i, [128, 4, 128], BF16) for i in range(2)])
        pt_ring = Ring([kb.sb("ret_pt%d" % i, [128, 4, 128], BF16) for i in range(2)])
        qf_ring = Ring([kb.sb("ret_qf%d" % i, [128, 4, 128], BF16) for i in range(2)])
        qb_ring = Ring([kb.sb("ret_qb%d" % i, [128, 4, 128], BF16) for i in range(2)])
        on_ring = Ring([kb.sb("ret_on%d" % i, [128, 4, 256], F32) for i in range(2)])
        yb_ring = Ring([kb.sb("ret_yb%d" % i, [128, 1024], BF16) for i in range(2)])
        ybt_ring = Ring([kb.sb("ret_ybt%d" % i, [128, 8, 128], BF16) for i in range(2)])
        ybt_sem = [kb.sem("ret_ybts%d" % i) for i in range(2)]
        st_ring = Ring([kb.sb("ret_st%d" % i, [128, 4, 6], F32) for i in range(2)])
        mv_ring = Ring([kb.sb("ret_mv%d" % i, [128, 4, 2], F32) for i in range(2)])
        r1_ring = Ring([kb.sb("ret_r1%d" % i, [128, 3, 4], F32) for i in range(2)])
        nm_ring = Ring([kb.sb("ret_nm%d" % i, [128, 4], F32) for i in range(2)])
        tmps = kb.sb("ret_tmps", [128, 4, 256], F32)
        ST_r = Ring([psb[0], psb[1]])
        OTa, OTb = psb[2], psb[3]
        KVa, KVb = psb[4], psb[5]
        KTR, YTR = ptb[0], ptb[1]

        def load_chunk(n, want_q):
            i = n % 2 if not want_q else n % 2
            _, kt, fk = kt_ring.next()
            _, v, fv = v_ring.next()
            for tk in fk + fv:
                kb.wait("sp", tk)
            sl = slice(n * 128, (n + 1) * 128)
            sm = ld_sem[(kt_ring.k - 1) % 3]
            kb.dma("sp", kt[:, :, :], KR_T.ap()[:, :, sl].rearrange("h p t -> p h t"), sm)
            tk = kb.dma("sp", v[:, :, :], VR.ap()[sl, :].rearrange("t (h e) -> t h e", h=4), sm)
            qt = g = None
            if want_q:
                _, qt, fq = qt_ring.next()
                _, g, fg = g_ring.next()
                for t_ in fq + fg:
                    kb.wait("sp", t_)
                kb.dma("sp", qt[:, :, :], QR_T.ap()[:, :, sl].rearrange("h p t -> p h t"), sm)
                tk = kb.dma("sp", g[:, :, :], SG.ap()[sl, :].rearrange("t (h e) -> t h e", h=4), sm)
            return dict(kt=kt, v=v, qt=qt, g=g, tok=tk, ki=(kt_ring.k - 1) % 3, vi=(v_ring.k - 1) % 3,
                        qi=(qt_ring.k - 1) % 3, gi=(g_ring.k - 1) % 3)

        def mm4(dst_a, dst_b, lhs_fn, rhs_fn, groups_extra=None):
            ins = None
            for h in range(4):
                bank = dst_a if h < 2 else dst_b
                terms = [(lhs_fn(h), rhs_fn(h))] + ([(a(h), b(h)) for a, b in groups_extra] if groups_extra else [])
                for ti, (lh, rh) in enumerate(terms):
                    ins = pe.matmul(bank[:, (h % 2) * 256:(h % 2) * 256 + 256], lhsT=lh, rhs=rh,
                                    start=(ti == 0 and h % 2 == 0), stop=(ti == len(terms) - 1), skip_group_check=True)
            return ins

        def bank4(a, b):
            return a[:, :].rearrange("p (g e) -> p g e", e=256), b[:, :].rearrange("p (g e) -> p g e", e=256)

        kb.wait("pool", tab_tok)
        pool.memset(S_run[:, :, :], 0.0)
        pool.memset(E_f[:, :, :], 0.0)
        z_tok = kb.mark("pool", pool.memset(SB_all[:, 31, :, :], 0.0))
        ktr_free = []
        kv_free = []
        upd_tok = z_tok
        pend_copy = None
        atok = None
        nxt = load_chunk(31, False)
        for n in range(31, -1, -1):
            cur = nxt
            if n > 0:
                nxt = load_chunk(n - 1, False)
            kt, v = cur["kt"], cur["v"]
            kb.wait("pe", cur["tok"])
            kb.wait("pe", tok_ident)
            for tk in ktr_free:
                kb.wait("pe", tk)
            for h in range(4):
                ins = pe.transpose(KTR[:, h * 128:(h + 1) * 128], kt[:, h, :], ident_b[:, :])
            ttok = kb.mark("pe", ins)
            kt_ring.free[cur["ki"]] = [ttok]
            _, kzf, f1 = kzf_ring.next()
            _, kzb, f2 = kzb_ring.next()
            kb.wait("act", ttok)
            kb.wait("act", tab_tok)
            for tk in f1 + f2:
                kb.wait("act", tk)
            for h in range(4):
                act.activation(out=kzf[:, h, :], in_=KTR[:, h * 128:(h + 1) * 128], func=AF.Identity, scale=ZF[:, h:h + 1])
                ins = act.activation(out=kzb[:, h, :], in_=KTR[:, h * 128:(h + 1) * 128], func=AF.Identity, scale=ZB[:, h:h + 1])
            ztok = kb.mark("act", ins)
            ktr_free = [ztok]
            if pend_copy is not None:
                kb.wait("act", upd_tok)
                atok = kb.mark("act", act.copy(out=SB_all[:, pend_copy, :, :], in_=S_run[:, :, :]))
                pend_copy = None
            kb.wait("pe", ztok)
            for tk in kv_free:
                kb.wait("pe", tk)
            mm4(psb[2], psb[3], lambda h: kzf[:, h, :], lambda h: v[:, h, :])
            ins = mm4(psb[4], psb[5], lambda h: kzb[:, h, :], lambda h: v[:, h, :])
            mtok = kb.mark("pe", ins)
            kzf_ring.free[(kzf_ring.k - 1) % 2] = [mtok]
            kzb_ring.free[(kzb_ring.k - 1) % 2] = [mtok]
            v_ring.free[cur["vi"]] = [mtok]
            kb.wait("dve", mtok)
            kb.wait("dve", upd_tok, force=True)
            kb.wait("dve", atok)
            fa, fb = bank4(psb[2], psb[3])
            ba, bb_ = bank4(psb[4], psb[5])
            for h in range(4):
                fsrc = (fa if h < 2 else fb)[:, h % 2, :]
                dve.scalar_tensor_tensor(out=E_f[:, h, :], in0=fsrc, scalar=CDF[:, h, n:n + 1], in1=E_f[:, h, :], op0=ALU.mult, op1=ALU.add)
            for h in range(4):
                bsrc = (ba if h < 2 else bb_)[:, h % 2, :]
                tk = kb.mark("dve", dve.scalar_tensor_tensor(out=S_run[:, h, :], in0=S_run[:, h, :], scalar=GC[:, 4 + h:5 + h], in1=bsrc, op0=ALU.mult, op1=ALU.add))
            upd_tok = tk
            kv_free = [tk]
            if n > 0:
                pend_copy = n - 1
        ef_tok = upd_tok
        s_ex = kb.sem("ret_ex")
        kb.wait("sp", ef_tok)
        kb.dma("sp", RSB.ap()[0, :].rearrange("(p f) -> p f", p=128), E_f[:, :, :].rearrange("p h e -> p (h e)"), s_ex)
        t = kb.dma("sp", RSB.ap()[1, :].rearrange("(p f) -> p f", p=128), S_run[:, :, :].rearrange("p h e -> p (h e)"), s_ex)
        kb.wait("pool", t)
        ins = pool.collective_compute("AllGather", ALU.bypass, replica_groups=pairs, ins=[RSB.ap().opt()], outs=[RSG.ap().opt()])
        ins.then_inc(s_cc.h)
        s_cc.n += 1
        cc_tok = (s_cc, s_cc.n)
        kb.wait("sp", cc_tok)
        Sb_in = kb.sb("ret_Sbin", [128, 4, 256], F32)
        kb.wait("sp", t)
        kb.dma("sp", S_run[:, :, :].rearrange("p h e -> p (h e)"), RSG.ap()[0, :].rearrange("(p f) -> p f", p=128), s_ex)
        t = kb.dma("sp", Sb_in[:, :, :].rearrange("p h e -> p (h e)"), RSG.ap()[3, :].rearrange("(p f) -> p f", p=128), s_ex)
        kb.wait("dve", t)
        dve.tensor_scalar(out=S_run[:, :, :], in0=S_run[:, :, :], scalar1=par[:, 0:1], scalar2=None, op0=ALU.mult)
        tk = kb.mark("dve", dve.tensor_scalar(out=Sb_in[:, :, :], in0=Sb_in[:, :, :], scalar1=par[:, 1:2], scalar2=None, op0=ALU.mult))
        kb.wait("dve", tk, force=True)
        kb.wait("act", tk)
        sf_tok = [kb.mark("act", act.copy(out=Sf_bf[0][:, :, :], in_=S_run[:, :, :])), None]
        sfc_tok = sf_tok[0]
        for n in range(32):
            for h in range(4):
                ins = dve.scalar_tensor_tensor(out=SB_all[:, n, h, :], in0=Sb_in[:, h, :], scalar=CDB[:, h, n:n + 1],
                                               in1=SB_all[:, n, h, :], op0=ALU.mult, op1=ALU.add)
        fix_tok = kb.mark("dve", ins)

        ot_free = [ef_tok]
        kv_free = [upd_tok]
        ytr_free = []
        upd_tok = fix_tok
        nm_ring2 = Ring([kb.sb("ret_nmr%d" % i, [128, 4], F32) for i in range(2)])
        pend = None

        def gn_norm(pd):
            nonlocal ot_free
            mv, r1, nm, oa, ob = pd["mv"], pd["r1"], pd["nm"], pd["oa"], pd["ob"]
            _, on, f5 = on_ring.next()
            kb.wait("act", pd["rtok"])
            for tk in f5:
                kb.wait("act", tk)
            for h in range(4):
                src = (oa if h < 2 else ob)[:, h % 2, :]
                ins = act.activation(out=on[:, h, :], in_=src, func=AF.Identity, bias=nm[:, h:h + 1], scale=r1[:, 2, h:h + 1])
            ntok = kb.mark("act", ins)
            ot_free = [ntok]
            st_ring.free[pd["sti"]] = [ntok]
            mv_ring.free[pd["mvi"]] = [ntok]
            r1_ring.free[pd["r1i"]] = [ntok]
            nm_ring2.free[pd["nmi"]] = [ntok]
            pd["on"] = on
            pd["oni"] = (on_ring.k - 1) % 2
            pd["ntok"] = ntok

        def gn_out(pd):
            nonlocal ytr_free
            n_, g, gi, on, ntok = pd["n"], pd["g"], pd["gi"], pd["on"], pd["ntok"]
            yi, yb, fy = yb_ring.next()
            kb.wait("pool", ntok)
            for tk in fy:
                kb.wait("pool", tk)
            ytok = kb.mark("pool", pool.tensor_tensor(out=yb[:, :].rearrange("p (h e) -> p h e", h=4), in0=on[:, :, :], in1=g[:, :, :], op=ALU.mult))
            on_ring.free[pd["oni"]] = [ytok]
            g_ring.free[gi] = [ytok]
            bi_, ybt, fb_ = ybt_ring.next()
            for tk in fb_:
                kb.wait("act", tk)
            pt_, et_ = transpose_chunk(yb, ytok, ybt[:, :, :], YTR, ytr_free, "act")
            ytr_free = [et_]
            yb_ring.free[yi] = [pt_]
            kb.wait("sp", et_)
            stt_ = kb.dma("sp", YB_T.ap()[:, :, n_ * 128:(n_ + 1) * 128].rearrange("k p t -> p k t"), ybt[:, :, :], ybt_sem[bi_])
            ybt_ring.free[bi_] = [stt_]

        nxt = load_chunk(0, True)
        for n in range(32):
            cur = nxt
            if n < 31:
                nxt = load_chunk(n + 1, True)
            kt, v, qt, g = cur["kt"], cur["v"], cur["qt"], cur["g"]
            sfb = Sf_bf[n % 2]
            si, STb, sfr = ST_r.next()
            kb.wait("pe", cur["tok"])
            for tk in sfr + ktr_free:
                kb.wait("pe", tk)
            for h in range(4):
                pe.matmul(STb[:, h * 128:(h + 1) * 128], lhsT=kt[:, h, :], rhs=qt[:, h, :], start=(h == 0), stop=True, skip_group_check=True)
            for h in range(4):
                ins = pe.transpose(KTR[:, h * 128:(h + 1) * 128], kt[:, h, :], ident_b[:, :])
            stok = kb.mark("pe", ins)
            kt_ring.free[cur["ki"]] = [stok]
            _, pt, f1 = pt_ring.next()
            kb.wait("dve", stok)
            for tk in f1:
                kb.wait("dve", tk)
            ptok = kb.mark("dve", dve.tensor_tensor(out=pt[:, :, :], in0=STb[:, :].rearrange("p (h t) -> p h t", h=4), in1=DT[:, :, :], op=ALU.mult))
            ST_r.free[si] = [ptok]
            _, kzf, f2 = kzf_ring.next()
            kb.wait("act", stok)
            for tk in f2:
                kb.wait("act", tk)
            for h in range(4):
                ins = act.activation(out=kzf[:, h, :], in_=KTR[:, h * 128:(h + 1) * 128], func=AF.Identity, scale=ZF[:, h:h + 1])
            ktok2 = kb.mark("act", ins)
            ktr_free = [ktok2]
            _, qf, f1 = qf_ring.next()
            _, qb, f2 = qb_ring.next()
            kb.wait("pool", cur["tok"])
            kb.wait("pool", tab_tok)
            for tk in f1 + f2:
                kb.wait("pool", tk)
            pool.tensor_tensor(out=qf[:, :, :], in0=qt[:, :, :], in1=XF[:, :, :], op=ALU.mult)
            qtok = kb.mark("pool", pool.tensor_tensor(out=qb[:, :, :], in0=qt[:, :, :], in1=XB[:, :, :], op=ALU.mult))
            qt_ring.free[cur["qi"]] = [qtok, stok]
            kb.wait("pe", ktok2)
            for tk in kv_free:
                kb.wait("pe", tk)
            ins = mm4(KVa, KVb, lambda h: kzf[:, h, :], lambda h: v[:, h, :])
            ktok = kb.mark("pe", ins)
            kzf_ring.free[(kzf_ring.k - 1) % 2] = [ktok]
            if pend is not None:
                gn_norm(pend)
            kb.wait("pe", ptok)
            kb.wait("pe", qtok)
            kb.wait("pe", sf_tok[n % 2])
            kb.wait("pe", fix_tok)
            for tk in ot_free:
                kb.wait("pe", tk)
            ins = mm4(OTa, OTb, lambda h: pt[:, h, :], lambda h: v[:, h, :],
                      [(lambda h: qf[:, h, :], lambda h: sfb[:, h, :]), (lambda h: qb[:, h, :], lambda h: SB_all[:, n, h, :])])
            otok = kb.mark("pe", ins)
            pt_ring.free[(pt_ring.k - 1) % 2] = [otok]
            qf_ring.free[(qf_ring.k - 1) % 2] = [otok]
            qb_ring.free[(qb_ring.k - 1) % 2] = [otok]
            v_ring.free[cur["vi"]] = [otok]
            kb.wait("dve", ktok)
            kb.wait("dve", upd_tok, force=True)
            kb.wait("dve", sfc_tok)
            ka, kb_ = bank4(KVa, KVb)
            for h in range(4):
                src = (ka if h < 2 else kb_)[:, h % 2, :]
                ins = dve.scalar_tensor_tensor(out=S_run[:, h, :], in0=S_run[:, h, :], scalar=GC[:, h:h + 1], in1=src, op0=ALU.mult, op1=ALU.add)
            upd_tok = kb.mark("dve", ins)
            kv_free = [upd_tok]
            kb.wait("act", upd_tok)
            kb.wait("act", otok if n > 0 else None)
            sfc_tok = kb.mark("act", act.copy(out=Sf_bf[(n + 1) % 2][:, :, :], in_=S_run[:, :, :]))
            sf_tok[(n + 1) % 2] = sfc_tok
            if pend is not None:
                gn_out(pend)
            sti, stt, f1 = st_ring.next()
            mvi, mv, f2 = mv_ring.next()
            r1i, r1, f3 = r1_ring.next()
            nmi, nm, f4 = nm_ring2.next()
            kb.wait("dve", otok)
            for tk in f1 + f2 + f3 + f4:
                kb.wait("dve", tk)
            oa, ob = bank4(OTa, OTb)
            for h in range(4):
                src = (oa if h < 2 else ob)[:, h % 2, :]
                tk = kb.mark("dve", dve.bn_stats(out=stt[:, h, :], in_=src))
            kb.wait("dve", tk, force=True)
            for h in range(4):
                tk = kb.mark("dve", dve.bn_aggr(out=mv[:, h, :], in_=stt[:, h:h + 1, :]))
            kb.wait("dve", tk, force=True)
            tk = kb.mark("dve", dve.tensor_scalar(out=r1[:, 0, :], in0=mv[:, :, 1], scalar1=GN_EPS, scalar2=None, op0=ALU.add))
            kb.wait("act", tk)
            tk = kb.mark("act", act.activation(out=r1[:, 1, :], in_=r1[:, 0, :], func=AF.Sqrt))
            kb.wait("dve", tk)
            tk = kb.mark("dve", dve.reciprocal(out=r1[:, 2, :], in_=r1[:, 1, :]))
            kb.wait("dve", tk, force=True)
            rtok = kb.mark("dve", dve.scalar_tensor_tensor(out=nm[:, :], in0=mv[:, :, 0], scalar=-1.0, in1=r1[:, 2, :], op0=ALU.mult, op1=ALU.mult))
            pend = dict(n=n, mv=mv, r1=r1, nm=nm, g=g, gi=cur["gi"], oa=oa, ob=ob, rtok=rtok, sti=sti, mvi=mvi, r1i=r1i, nmi=nmi)
        gn_norm(pend)
        gn_out(pend)
        return ybt_ring.free[0] + ybt_ring.free[1]

    class LN:
        def __init__(self, tag, g_d, b_d, l, banks):
            self.banks = banks
            self.G = kb.sb(tag + "_G", [128, D], F32)
            self.B = kb.sb(tag + "_B", [128, D], F32)
            sm = kb.sem(tag + "_gb")
            kb.dma("sp", self.G[:, :], g_d.ap()[l, :].partition_broadcast(128), sm)
            self.gb_tok = kb.dma("sp", self.B[:, :], b_d.ap()[l, :].partition_broadcast(128), sm)
            self.stt = Ring([kb.sb(tag + "_stt%d" % i, [128, 2, 6], F32) for i in range(3)])
            self.mv = Ring([kb.sb(tag + "_mv%d" % i, [128, 2], F32) for i in range(3)])
            self.r = Ring([kb.sb(tag + "_r%d" % i, [128, 4], F32) for i in range(3)])
            self.xn = Ring([kb.sb(tag + "_xn%d" % i, [128, D], F32) for i in range(2)])
            self.xo = Ring([kb.sb(tag + "_xo%d" % i, [128, D], F32) for i in range(2)])
            self.xo_sem = [kb.sem(tag + "_xos%d" % i) for i in range(2)]
            self.xb = Ring([kb.sb(tag + "_xb%d" % i, [128, D], BF16) for i in range(2)])
            self.xt = Ring([kb.sb(tag + "_xt%d" % i, [128, 8, 128], BF16) for i in range(2)])
            self.xt_sem = [kb.sem(tag + "_xts%d" % i) for i in range(2)]
            self.tp_free = [[] for _ in banks]
            self.k = 0

        def stats(self, pre, pre_tok):
            dve, act = nc.vector, nc.scalar
            si, stt, f1 = self.stt.next()
            mi, mv, f2 = self.mv.next()
            ri, r, f3 = self.r.next()
            kb.wait("dve", pre_tok, force=True)
            for tk in f1 + f2 + f3:
                kb.wait("dve", tk)
            dve.bn_stats(out=stt[:, 0, :], in_=pre[:, 0:512])
            tk = kb.mark("dve", dve.bn_stats(out=stt[:, 1, :], in_=pre[:, 512:1024]))
            kb.wait("dve", tk, force=True)
            tk = kb.mark("dve", dve.bn_aggr(out=mv[:, :], in_=stt[:, :, :]))
            kb.wait("dve", tk, force=True)
            tk = kb.mark("dve", dve.tensor_scalar(out=r[:, 0:1], in0=mv[:, 1:2], scalar1=LN_EPS, scalar2=None, op0=ALU.add))
            kb.wait("act", tk)
            stok = kb.mark("act", act.activation(out=r[:, 1:2], in_=r[:, 0:1], func=AF.Sqrt))
            return dict(pre=pre, mv=mv, r=r, si=si, mi=mi, ri=ri, stok=stok)

        def finish_a(self, h, dst_x, want_xt):
            dve, act = nc.vector, nc.scalar
            pre, mv, r = h["pre"], h["mv"], h["r"]
            kb.wait("dve", h["stok"])
            tk = kb.mark("dve", dve.reciprocal(out=r[:, 2:3], in_=r[:, 1:2]))
            kb.wait("dve", tk, force=True)
            tk = kb.mark("dve", dve.tensor_scalar(out=r[:, 3:4], in0=mv[:, 0:1], scalar1=r[:, 2:3], scalar2=-1.0, op0=ALU.mult, op1=ALU.mult))
            _, xn, f4 = self.xn.next()
            kb.wait("act", tk)
            for t_ in f4:
                kb.wait("act", t_)
            ntok = kb.mark("act", act.activation(out=xn[:, :], in_=pre[:, :], func=AF.Identity, bias=r[:, 3:4], scale=r[:, 2:3]))
            self.stt.free[h["si"]] = [ntok]
            self.mv.free[h["mi"]] = [ntok]
            self.r.free[h["ri"]] = [ntok]
            oi, xo, f5 = self.xo.next()
            kb.wait("dve", ntok)
            kb.wait("dve", self.gb_tok)
            for t_ in f5:
                kb.wait("dve", t_)
            dve.tensor_tensor(out=xo[:, :], in0=xn[:, :], in1=self.G[:, :], op=ALU.mult)
            tk = kb.mark("dve", dve.tensor_tensor(out=xo[:, :], in0=xo[:, :], in1=self.B[:, :], op=ALU.add))
            self.xn.free[(self.xn.k - 1) % 2] = [tk]
            toks = []
            btok = xb = bi_ = None
            if want_xt:
                bi_, xb, f6 = self.xb.next()
                kb.wait("act", tk)
                for t_ in f6:
                    kb.wait("act", t_)
                btok = kb.mark("act", act.copy(out=xb[:, :], in_=xo[:, :]))
            kb.wait("sp", tk)
            st = kb.dma("sp", dst_x, xo[:, :], self.xo_sem[oi])
            self.xo.free[oi] = [st] + ([btok] if btok else [])
            toks.append(st)
            h.update(ntok=ntok, toks=toks, btok=btok, xb=xb, bi=bi_)
            return h

        def finish_b(self, h, dst_xt):
            toks = h["toks"]
            if dst_xt is not None:
                ti, xt, f7 = self.xt.next()
                e = "act"
                for t_ in f7:
                    kb.wait(e, t_)
                j = self.k % len(self.banks)
                self.k += 1
                pt_, et_ = transpose_chunk(h["xb"], h["btok"], xt[:, :, :], self.banks[j], self.tp_free[j], e)
                self.tp_free[j] = [et_]
                self.xb.free[h["bi"]] = [pt_]
                kb.wait("sp", et_)
                st2 = kb.dma("sp", dst_xt, xt[:, :, :], self.xt_sem[ti])
                self.xt.free[ti] = [st2]
                toks.append(st2)
            return h["ntok"], toks

        def finish(self, h, dst_x, dst_xt):
            self.finish_a(h, dst_x, dst_xt is not None)
            return self.finish_b(h, dst_xt)

    def load_w_cast(dst, src, sem):
        return kb.dma("pool", dst, src, sem)

    def p1d_weights(l):
        s_w = kb.sem("p1d_w")
        Wa = kb.sb("p1d_Wa", [128, 4, D], BF16)
        Wb = kb.sb("p1d_Wb", [128, 8, D], BF16)
        Wo = kb.sb("p1d_Wo", [128, 8, D], BF16)
        load_w_cast(Wa[:, :, :], w_ba.ap()[l].rearrange("(k p) c -> p k c", p=128), s_w)
        for hf in range(2):
            load_w_cast(Wb[:, hf * 4:(hf + 1) * 4, :], w_bb.ap()[l, hf * 512:(hf + 1) * 512, :].rearrange("(k p) c -> p k c", p=128), s_w)
            wtok = load_w_cast(Wo[:, hf * 4:(hf + 1) * 4, :], w_out.ap()[l, hf * 512:(hf + 1) * 512, :].rearrange("(k p) c -> p k c", p=128), s_w)
        return Wa, Wb, Wo, wtok

    def phase_p1d(l, w1d):
        dve, act, pool, pe = nc.vector, nc.scalar, nc.gpsimd, nc.tensor
        Wa, Wb, Wo, wtok = w1d
        ln = LN("ln1", ln1_g, ln1_b, l, [ptb[0], ptb[1]])
        in_ring = Ring([dict(ya=kb.sb("p1d_ya%d" % i, [128, 4, 512], BF16), yb=kb.sb("p1d_yb%d" % i, [128, 8, 512], BF16),
                             ga=kb.sb("p1d_ga%d" % i, [128, 8, 512], BF16), gb=kb.sb("p1d_gb%d" % i, [128, 8, 512], BF16)) for i in range(2)])
        in_sem = [kb.sem("p1d_ins%d" % i) for i in range(2)]
        mg_ring = Ring([kb.sb("p1d_mg%d" % i, [128, 8, 512], BF16) for i in range(2)])
        t1_ring = Ring([kb.sb("p1d_t1%d" % i, [128, 512], F32) for i in range(2)])
        t2_ring = Ring([kb.sb("p1d_t2%d" % i, [128, 512], F32) for i in range(2)])
        x_ring = Ring([kb.sb("p1d_x%d" % i, [128, D], F32) for i in range(2)])
        x_sem = [kb.sem("p1d_xs%d" % i) for i in range(2)]
        pre_ring = Ring([kb.sb("p1d_pre%d" % i, [128, D], F32) for i in range(3)])
        za_ring = Ring([psb[0], psb[1]])
        zb_ring = Ring([psb[2], psb[3]])
        y_free = []
        out_toks = {}

        def load_tile(tt):
            i, bufs, fr = in_ring.next()
            for tk in fr:
                kb.wait("sp", tk)
            sl = slice(tt * 512, (tt + 1) * 512)
            kb.dma("sp", bufs["ya"][:, :, :], YA_T.ap()[:, :, sl].rearrange("k p t -> p k t"), in_sem[i])
            kb.dma("sp", bufs["yb"][:, :, :], YB_T.ap()[:, :, sl].rearrange("k p t -> p k t"), in_sem[i])
            kb.dma("sp", bufs["ga"][:, :, :], GA_T.ap()[:, :, sl].rearrange("k p t -> p k t"), in_sem[i])
            tk = kb.dma("sp", bufs["gb"][:, :, :], GB_T.ap()[:, :, sl].rearrange("k p t -> p k t"), in_sem[i])
            return i, bufs, tk

        pending = None

        pending_b = None

        def ln_fin_a(hnd, c, pi):
            nonlocal pending_b
            ln.finish_a(hnd, X1.ap()[c * 128:(c + 1) * 128, :], True)
            pre_ring.free[pi] = [hnd["ntok"]]
            pending_b = (hnd, c)

        def ln_fin_b(hnd, c):
            ntok, toks = ln.finish_b(hnd, X1T_D.ap()[:, :, c * 128:(c + 1) * 128].rearrange("k p t -> p k t"))
            for t_ in toks:
                out_toks[id(t_[0])] = t_

        tiles = {}

        def tile_begin(tt):
            ii, bufs, ltok = tiles[tt]["load"]
            mi, mg, mfr = mg_ring.next()
            tiles[tt].update(ii=ii, bufs=bufs, ltok=ltok, mi=mi, mg=mg, mfr=mfr)

        def fb_step(tt, fb):
            T = tiles[tt]
            bufs, ltok, mg = T["bufs"], T["ltok"], T["mg"]
            _, za, fa = za_ring.next()
            _, zb, fb_ = zb_ring.next()
            kb.wait("pe", ltok)
            kb.wait("pe", wtok)
            for tk in fa + fb_:
                kb.wait("pe", tk)
            for kc in range(4):
                pe.matmul(za[:, :], lhsT=Wa[:, kc, fb * 128:(fb + 1) * 128], rhs=bufs["ya"][:, kc, :], start=(kc == 0), stop=(kc == 3))
            for kc in range(8):
                ins = pe.matmul(zb[:, :], lhsT=Wb[:, kc, fb * 128:(fb + 1) * 128], rhs=bufs["yb"][:, kc, :], start=(kc == 0), stop=(kc == 7))
            ptok = kb.mark("pe", ins)
            _, t1, f1 = t1_ring.next()
            _, t2, f2 = t2_ring.next()
            kb.wait("dve", ptok)
            kb.wait("dve", ltok)
            for tk in f1 + f2:
                kb.wait("dve", tk)
            if fb == 0:
                for tk in T["mfr"]:
                    kb.wait("dve", tk)
            dve.tensor_tensor(out=t1[:, :], in0=za[:, :], in1=bufs["ga"][:, fb, :], op=ALU.mult)
            dtok = kb.mark("dve", dve.tensor_tensor(out=t2[:, :], in0=zb[:, :], in1=bufs["gb"][:, fb, :], op=ALU.mult))
            za_ring.free[(za_ring.k - 1) % 2] = [dtok]
            zb_ring.free[(zb_ring.k - 1) % 2] = [dtok]
            T["mtok"] = kb.mark("dve", dve.tensor_tensor(out=mg[:, fb, :], in0=t1[:, :], in1=t2[:, :], op=ALU.add))
            if fb == 7:
                in_ring.free[T["ii"]] = [ptok, dtok]

        def sub_chunk(tt, sc):
            nonlocal y_free, pending, pending_b
            T = tiles[tt]
            mg = T["mg"]
            c = tt * 4 + sc
            xi, xt_, fx = x_ring.next()
            for tk in fx:
                kb.wait("sp", tk)
            xtok = kb.dma("sp", xt_[:, :], XA.ap()[c * 128:(c + 1) * 128, :], x_sem[xi])
            kb.wait("pe", T["mtok"])
            for tk in y_free:
                kb.wait("pe", tk)
            for half in range(2):
                bank = psb[4 + half]
                for fb in range(8):
                    ins = pe.matmul(bank[:, :], lhsT=mg[:, fb, sc * 128:(sc + 1) * 128], rhs=Wo[:, fb, half * 512:(half + 1) * 512],
                                    start=(fb == 0), stop=(fb == 7))
            ytok = kb.mark("pe", ins)
            if pending_b is not None:
                ln_fin_b(*pending_b)
                pending_b = None
            pi, pre, fp = pre_ring.next()
            kb.wait("dve", ytok)
            kb.wait("dve", xtok)
            for tk in fp:
                kb.wait("dve", tk)
            dve.scalar_tensor_tensor(out=pre[:, 0:512], in0=xt_[:, 0:512], scalar=ALPHA, in1=psb[4][:, :], op0=ALU.mult, op1=ALU.add)
            ptk = kb.mark("dve", dve.scalar_tensor_tensor(out=pre[:, 512:1024], in0=xt_[:, 512:1024], scalar=ALPHA, in1=psb[5][:, :], op0=ALU.mult, op1=ALU.add))
            y_free = [ptk]
            x_ring.free[xi] = [ptk]
            hnd = ln.stats(pre, ptk)
            pending = (hnd, c, pi)
            if sc == 3:
                mg_ring.free[T["mi"]] = [ytok]

        tiles[0] = dict(load=load_tile(0))
        for tt in range(9):
            if tt < 8:
                if tt + 1 < 8:
                    tiles[tt + 1] = dict(load=load_tile(tt + 1))
                tile_begin(tt)
            for fb in range(8):
                if tt < 8:
                    fb_step(tt, fb)
                if fb % 2 == 0 and pending is not None:
                    ln_fin_a(*pending)
                    pending = None
                if tt >= 1 and fb % 2 == 1:
                    sub_chunk(tt - 1, fb // 2)
        if pending is not None:
            ln_fin_a(*pending)
        if pending_b is not None:
            ln_fin_b(*pending_b)
        return list(out_toks.values())

    def phase_p2a(l):
        dve, act, pool, pe = nc.vector, nc.scalar, nc.gpsimd, nc.tensor
        s_cc = kb.sem("cc")
        s_ex = kb.sem("p2a_ex")
        par = kb.sb("p2a_par", [128, 2], F32)
        kb.dma("sp", par[:, :], par_d.ap()[:, :], s_ex)
        kb.dma("sp", CXB.ap()[0, :].rearrange("(k p o) -> k p o", p=128, o=1), X1T_D.ap()[:, :, 0:1], s_ex, allow_slow_non_contiguous=True)
        t = kb.dma("sp", CXB.ap()[1, :].rearrange("(k p o) -> k p o", p=128, o=1), X1T_D.ap()[:, :, NT - 1:NT], s_ex, allow_slow_non_contiguous=True)
        ex_t = t
        XT = kb.sb("p2a_XT", [128, 8, NT + 2], BF16)
        s_xt = kb.sem("p2a_xt")
        for kc in range(8):
            xtm_tok = kb.dma("sp", XT[:, kc, 1:NT + 1], X1T_D.ap()[kc, :, :], s_xt)
        cp = kb.sb("p2a_cp", [128, 44, 4], F32)
        cp_tok = kb.dma("sp", cp[:, :, :], convp.ap()[l], kb.sem("p2a_cp"))
        w_ring = Ring([kb.sb("p2a_w%d" % i, [128, 2, 8, 128], BF16) for i in range(2)])
        w_sem = [kb.sem("p2a_ws%d" % i) for i in range(2)]
        HW = 2048
        H_ring = Ring([kb.sb("p2a_H%d" % i, [128, 2, HW + 2], F32) for i in range(2)])
        C_ring = Ring([kb.sb("p2a_C%d" % i, [128, 2, HW], F32) for i in range(2)])
        U_ring = Ring([kb.sb("p2a_U%d" % i, [128, HW], BF16) for i in range(2)])
        U_sem = [kb.sem("p2a_us%d" % i) for i in range(2)]
        ps_ring = Ring(psb[0:4])
        hp_ring = Ring([psb[4], psb[5]])

        def load_w(c):
            i, w, fr = w_ring.next()
            for tk in fr:
                kb.wait("pool", tk)
            for part in range(2):
                c0 = part * DFF + c * 128
                tk = load_w_cast(w[:, part, :, :], w_up.ap()[l, :, c0:c0 + 128].rearrange("(k p) c -> p k c", p=128), w_sem[i])
            return i, w, tk

        nxt = load_w(0)
        kb.wait("pool", ex_t)
        ins = pool.collective_compute("AllGather", ALU.bypass, replica_groups=pairs, ins=[CXB.ap().opt()], outs=[CXG.ap().opt()])
        ins.then_inc(s_cc.h)
        s_cc.n += 1
        cc_tok = (s_cc, s_cc.n)
        kb.wait("sp", cc_tok)
        hal = kb.sb("p2a_hal", [128, 8, 2], BF16)
        s_hal = kb.sem("p2a_hal")
        kb.dma("sp", hal[:, :, 0:1], CXG.ap()[1, :].rearrange("(k p o) -> p k o", p=128, o=1), s_hal, allow_slow_non_contiguous=True)
        t = kb.dma("sp", hal[:, :, 1:2], CXG.ap()[2, :].rearrange("(k p o) -> p k o", p=128, o=1), s_hal, allow_slow_non_contiguous=True)
        kb.wait("dve", t)
        kb.wait("dve", xtm_tok)
        dve.tensor_scalar(out=XT[:, :, 0], in0=hal[:, :, 0], scalar1=par[:, 0:1], scalar2=None, op0=ALU.mult)
        xt_tok = kb.mark("dve", dve.tensor_scalar(out=XT[:, :, NT + 1], in0=hal[:, :, 1], scalar1=par[:, 1:2], scalar2=None, op0=ALU.mult))
        for c in range(22):
            wi, w, wtok = nxt
            if c + 1 < 22:
                nxt = load_w(c + 1)
            for hf in range(2):
                T0 = hf * HW
                hi, H, fh = H_ring.next()
                ev_toks = []
                for part in range(2):
                    kb.wait("pe", wtok)
                    kb.wait("pe", xtm_tok)
                    for t4 in range(4):
                        pi, pb, pfr = ps_ring.next()
                        for tk in pfr:
                            kb.wait("pe", tk)
                        for kc in range(8):
                            ins = pe.matmul(pb[:, :], lhsT=w[:, part, kc, :], rhs=XT[:, kc, 1 + T0 + t4 * 512:1 + T0 + (t4 + 1) * 512],
                                            start=(kc == 0), stop=(kc == 7))
                        mtok = kb.mark("pe", ins)
                        kb.wait("act", mtok)
                        if part == 0 and t4 == 0:
                            for tk in fh:
                                kb.wait("act", tk)
                        etk = kb.mark("act", act.copy(out=H[:, part, 1 + t4 * 512:1 + (t4 + 1) * 512], in_=pb[:, :]))
                        ps_ring.free[pi] = [etk]
                    _, hb, fhp = hp_ring.next()
                    kb.wait("pe", xt_tok)
                    for tk in fhp:
                        kb.wait("pe", tk)
                    for kc in range(8):
                        ins = pe.matmul(hb[:, 0:2], lhsT=w[:, part, kc, :], rhs=XT[:, kc, T0:T0 + HW + 2:HW + 1], start=(kc == 0), stop=(kc == 7))
                    htok = kb.mark("pe", ins)
                    kb.wait("act", htok)
                    etk = kb.mark("act", act.copy(out=H[:, part, 0:HW + 2:HW + 1], in_=hb[:, 0:2]))
                    hp_ring.free[(hp_ring.k - 1) % 2] = [etk]
                    ev_toks.append(etk)
                last_mm = htok
                ci, C, fc = C_ring.next()
                ia, ib = c, 22 + c
                kb.wait("act", cp_tok)
                for tk in fc:
                    kb.wait("act", tk)
                a1 = kb.mark("act", act.activation(out=C[:, 0, :], in_=H[:, 0, 1:HW + 1], func=AF.Identity, bias=cp[:, ia, 3:4], scale=cp[:, ia, 1:2]))
                kb.wait("pool", ev_toks[1])
                kb.wait("pool", cp_tok)
                for tk in fc:
                    kb.wait("pool", tk)
                p1 = kb.mark("pool", pool.tensor_scalar(out=C[:, 1, :], in0=H[:, 1, 1:HW + 1], scalar1=cp[:, ib, 1:2], scalar2=cp[:, ib, 3:4], op0=ALU.mult, op1=ALU.add))
                kb.wait("dve", a1)
                kb.wait("dve", cp_tok)
                dve.scalar_tensor_tensor(out=C[:, 0, :], in0=H[:, 0, 0:HW], scalar=cp[:, ia, 0:1], in1=C[:, 0, :], op0=ALU.mult, op1=ALU.add)
                d1 = kb.mark("dve", dve.scalar_tensor_tensor(out=C[:, 0, :], in0=H[:, 0, 2:HW + 2], scalar=cp[:, ia, 2:3], in1=C[:, 0, :], op0=ALU.mult, op1=ALU.add))
                kb.wait("dve", p1)
                dve.scalar_tensor_tensor(out=C[:, 1, :], in0=H[:, 1, 0:HW], scalar=cp[:, ib, 0:1], in1=C[:, 1, :], op0=ALU.mult, op1=ALU.add)
                d2 = kb.mark("dve", dve.scalar_tensor_tensor(out=C[:, 1, :], in0=H[:, 1, 2:HW + 2], scalar=cp[:, ib, 2:3], in1=C[:, 1, :], op0=ALU.mult, op1=ALU.add))
                H_ring.free[hi] = [d1, d2]
                kb.wait("act", d1)
                g1 = kb.mark("act", act.activation(out=C[:, 0, :], in_=C[:, 0, :], func=AF.Gelu))
                ui, U, fu = U_ring.next()
                kb.wait("pool", g1)
                kb.wait("pool", d2)
                for tk in fu:
                    kb.wait("pool", tk)
                utok = kb.mark("pool", pool.tensor_tensor(out=U[:, :], in0=C[:, 0, :], in1=C[:, 1, :], op=ALU.mult))
                C_ring.free[ci] = [utok]
                kb.wait("sp", utok)
                st = kb.dma("sp", U_T.ap()[c, :, T0:T0 + HW], U[:, :], U_sem[ui])
                U_ring.free[ui] = [st]
            w_ring.free[wi] = [last_mm]
        return U_ring.free[0] + U_ring.free[1]

    def phase_p2b(l, last):
        dve, act, pool, pe = nc.vector, nc.scalar, nc.gpsimd, nc.tensor
        s_w = kb.sem("p2b_w")
        Wd = kb.sb("p2b_Wd", [128, 22, D], BF16)
        Wg = kb.sb("p2b_Wg", [128, 8, D], BF16)
        Wp = kb.sb("p2b_Wp", [128, 2, D], BF16)
        ln = LN("ln2", ln2_g, ln2_b, l, [ptb[1]])
        in_ring = Ring([dict(u=kb.sb("p2b_u%d" % i, [128, 22, 128], BF16), xt=kb.sb("p2b_xt%d" % i, [128, 8, 128], BF16),
                             x1=kb.sb("p2b_x1%d" % i, [128, D], F32), pb=kb.sb("p2b_pb%d" % i, [128, PLE], BF16)) for i in range(3)])
        in_sem = [kb.sem("p2b_ins%d" % i) for i in range(3)]
        pin_sem = [kb.sem("p2b_pins%d" % i) for i in range(3)]
        pT_ring = Ring([kb.sb("p2b_pT%d" % i, [128, 2, 128], BF16) for i in range(2)])
        sg_ring = Ring([kb.sb("p2b_sg%d" % i, [128, D], F32) for i in range(2)])
        pre_ring = Ring([kb.sb("p2b_pre%d" % i, [128, D], F32) for i in range(3)])
        acc_free = [[], []]
        tp_free = []
        out_toks = {}

        def load_chunk(c):
            i, bufs, fr = in_ring.next()
            for tk in fr:
                kb.wait("sp", tk)
                kb.wait("pool", tk)
            sl = slice(c * 128, (c + 1) * 128)
            kb.dma("sp", bufs["u"][:, :, :], U_T.ap()[:, :, sl].rearrange("k p t -> p k t"), in_sem[i])
            kb.dma("sp", bufs["xt"][:, :, :], X1T_D.ap()[:, :, sl].rearrange("k p t -> p k t"), in_sem[i])
            tk = kb.dma("sp", bufs["x1"][:, :], X1.ap()[sl, :], in_sem[i])
            ptk = kb.dma("pool", bufs["pb"][:, :], p_in.ap()[l, sl, :], pin_sem[i])
            return i, bufs, tk, ptk

        def stage_P(c, ch):
            nonlocal tp_free
            pti, pT, fpt = pT_ring.next()
            kb.wait("pe", ch["lptok"])
            kb.wait("pe", tok_ident)
            for tk in tp_free:
                kb.wait("pe", tk)
            TPB = ptb[0]
            for k2 in range(2):
                ins = pe.transpose(TPB[:, k2 * 128:(k2 + 1) * 128], ch["bufs"]["pb"][:, k2 * 128:(k2 + 1) * 128], ident_b[:, :])
            ttok = kb.mark("pe", ins)
            kb.wait("act", ttok)
            for tk in fpt:
                kb.wait("act", tk)
            ctok = kb.mark("act", act.copy(out=pT[:, :, :], in_=TPB[:, 0:256].rearrange("p (k t) -> p k t", k=2)))
            tp_free = [ctok]
            ch.update(pT=pT, pti=pti, ctok=ctok, ttok=ttok)

        def stage_M(c, ch):
            bufs, pT = ch["bufs"], ch["pT"]
            kb.wait("pe", ch["ltok"])
            kb.wait("pe", ch["ctok"])
            ch["mtok"] = []
            for half in range(2):
                cs = slice(half * 512, (half + 1) * 512)
                by, bg, bp = psb[3 * half], psb[3 * half + 1], psb[3 * half + 2]
                for tk in acc_free[half]:
                    kb.wait("pe", tk)
                for k in range(22):
                    kb.wait("pe", wd_tok[k])
                    pe.matmul(by[:, :], lhsT=bufs["u"][:, k, :], rhs=Wd[:, k, cs], start=(k == 0), stop=(k == 21))
                for k in range(8):
                    kb.wait("pe", wg_tok[k])
                    pe.matmul(bg[:, :], lhsT=bufs["xt"][:, k, :], rhs=Wg[:, k, cs], start=(k == 0), stop=(k == 7))
                kb.wait("pe", wtok)
                for k in range(2):
                    ins = pe.matmul(bp[:, :], lhsT=pT[:, k, :], rhs=Wp[:, k, cs], start=(k == 0), stop=(k == 1))
                ch["mtok"].append(kb.mark("pe", ins))
            pT_ring.free[ch["pti"]] = [ch["mtok"][1]]

        def stage_E(c, ch):
            bufs = ch["bufs"]
            si, sg, fs = sg_ring.next()
            pi, pre, fp = pre_ring.next()
            for half in range(2):
                cs = slice(half * 512, (half + 1) * 512)
                by, bg, bp = psb[3 * half], psb[3 * half + 1], psb[3 * half + 2]
                kb.wait("act", ch["mtok"][half])
                if half == 0:
                    for tk in fs:
                        kb.wait("act", tk)
                stok = kb.mark("act", act.activation(out=sg[:, cs], in_=bg[:, :], func=AF.Sigmoid))
                kb.wait("dve", stok)
                kb.wait("dve", ch["ltok"])
                if half == 0:
                    for tk in fp:
                        kb.wait("dve", tk)
                dve.tensor_tensor(out=sg[:, cs], in0=sg[:, cs], in1=bp[:, :], op=ALU.mult)
                atk = kb.mark("dve", dve.tensor_tensor(out=sg[:, cs], in0=sg[:, cs], in1=by[:, :], op=ALU.add))
                acc_free[half] = [atk]
                ptk = kb.mark("dve", dve.scalar_tensor_tensor(out=pre[:, cs], in0=bufs["x1"][:, cs], scalar=ALPHA, in1=sg[:, cs], op0=ALU.mult, op1=ALU.add))
            sg_ring.free[si] = [ptk]
            in_ring.free[ch["ii"]] = [ch["mtok"][1], ptk, ch["ttok"]]
            ch["hnd"] = ln.stats(pre, ptk)
            ch["pi"] = pi

        def stage_F(c, ch):
            sl = slice(c * 128, (c + 1) * 128)
            if last:
                ntok, toks = ln.finish(ch["hnd"], out_d.ap()[sl, :], None)
            else:
                ntok, toks = ln.finish(ch["hnd"], XA.ap()[sl, :], XT_D.ap()[:, :, sl].rearrange("k p t -> p k t"))
            pre_ring.free[ch["pi"]] = [ntok]
            for t_ in toks:
                out_toks[id(t_[0])] = t_

        def mk(c):
            ii, bufs, ltok, lptok = load_chunk(c)
            return dict(ii=ii, bufs=bufs, ltok=ltok, lptok=lptok)

        chs = {0: mk(0), 1: mk(1)}
        wd_tok, wg_tok = {}, {}
        for qi, q in enumerate(range(0, 22, 4)):
            n_ = min(4, 22 - q)
            tk = load_w_cast(Wd[:, q:q + n_, :], w_down.ap()[l, q * 128:(q + n_) * 128, :].rearrange("(k p) c -> p k c", p=128), kb.sem("p2b_wd%d" % qi))
            for k in range(q, q + n_):
                wd_tok[k] = tk
        for hf in range(2):
            tk = load_w_cast(Wg[:, hf * 4:(hf + 1) * 4, :], w_pg.ap()[l, hf * 512:(hf + 1) * 512, :].rearrange("(k p) c -> p k c", p=128), kb.sem("p2b_wg%d" % hf))
            for k in range(hf * 4, hf * 4 + 4):
                wg_tok[k] = tk
        wtok = load_w_cast(Wp[:, :, :], w_pp.ap()[l].rearrange("(k p) c -> p k c", p=128), s_w)
        stage_P(0, chs[0])
        for c in range(NCH):
            if c + 1 < NCH:
                stage_P(c + 1, chs[c + 1])
            stage_M(c, chs[c])
            if c >= 1:
                stage_F(c - 1, chs[c - 1])
                del chs[c - 1]
            stage_E(c, chs[c])
            if c + 2 < NCH:
                chs[c + 2] = mk(c + 2)
        stage_F(NCH - 1, chs[NCH - 1])
        return list(out_toks.values())

    with kb.scope():
        toks = phase_p0()
        barrier(toks)
    for l in range(n_layers):
        with kb.scope():
            toks = phase_p1a(l)
            barrier(toks)
        if stop == "p1a":
            break
        with kb.scope():
            toks = phase_na(l)
            barrier(toks)
        if stop in ("na", "na_ex"):
            break
        with kb.scope():
            w1d = p1d_weights(l)
            with kb.scope():
                toks = phase_ret(l)
                barrier(toks)
            if stop == "ret":
                break
            with kb.scope():
                toks = phase_p1d(l, w1d)
                barrier(toks)
        if stop == "p1d":
            break
        with kb.scope():
            toks = phase_p2a(l)
            barrier(toks)
        if stop == "p2a":
            break
        with kb.scope():
            toks = phase_p2b(l, last=(l == DEPTH - 1))
            barrier(toks)
        if stop == "p2b" and l == debug.get("stop_layer", 0):
            break

    s_fin = kb.sem("fin")
    fin = []
    for name in debug.get("dump", []):
        src = kb.dram[name]
        shp = list(src.shape)
        o = kb.dout("dbg_" + name, shp, src.dtype)
        fin.append(kb.dma("sp", o.ap(), src.ap(), s_fin))
    if not debug:
        pass
    for tk in fin[-1:]:
        kb.wait("sp", tk, force=True)
    return kb


def rope_tables(core):
    half = core % 2
    pos = (np.arange(NT, dtype=np.float64) + half * NT)
    inv = 10000.0 ** (-np.arange(64, dtype=np.float64) / 64.0)
    ang = (pos[:, None].astype(np.float32) * inv[None, :].astype(np.float32)).astype(np.float64)
    cos = np.cos(ang).T
    sin = np.sin(ang).T
    s = 128.0 ** -0.5
    cosf = np.concatenate([cos, cos], 0)
    sinsw = np.concatenate([sin, -sin], 0)
    return np.stack([cosf, sinsw, cosf * s, sinsw * s]).astype(np.float32)


def na_tables(rpb_l, parity):
    kc = np.arange(64)[:, None]
    c = np.arange(64)[None, :]
    cs = np.clip(c - 8, 0, 48)
    inwin = (kc >= cs) & (kc < cs + 16)
    off = np.clip(kc - c + 15, 0, 30)
    bi = np.full((8, 15, 64, 64), NEG, np.float32)
    for ro in range(15):
        g = rpb_l[:, ro][:, off]
        bi[:, 14 - ro] = np.where(inwin[None], g, np.float32(NEG))
    bb = np.full((8, NA_NSLOT, 64, 64), NEG, np.float32)
    for (kr, r), sl in NA_SLOT.items():
        if na_valid(parity, r, kr):
            bb[:, sl] = bi[:, 14 - (kr - r + 7)]
    def pack(a):
        n = a.shape[1]
        a = a.reshape(4, 2, n, 64, 64).transpose(0, 1, 3, 2, 4)
        return np.ascontiguousarray(a.reshape(4, 128, n * 64))
    return pack(bi), pack(bb)


def core_inputs(inputs, c, names):
    b, h = c // 2, c % 2
    sl = slice(h * NT, (h + 1) * NT)
    m = {}
    for n in names:
        if n == "x":
            m[n] = np.ascontiguousarray(inputs["x"][b, sl])
        elif n == "p":
            m[n] = np.ascontiguousarray(inputs["p"][:, b, sl])
        elif n == "rope":
            m[n] = rope_tables(c)
        elif n == "ident":
            m[n] = np.eye(128, dtype=np.float32)
        elif n == "rconst":
            i = np.arange(128)
            a1 = np.maximum(i[None, :] - i[:, None], 0)
            a2 = np.maximum(i[:, None] - i[None, :], 0)
            idx1 = np.broadcast_to(i[None, :] + 1, (128, 128))
            idx2 = np.broadcast_to(128 - i[None, :], (128, 128))
            pidx = np.stack([127 - i, i], 1)
            nidx = np.broadcast_to(128 * (31 - np.arange(32))[None, :], (128, 32))
            m[n] = np.ascontiguousarray(np.concatenate([a1, a2, idx1, idx2, pidx, nidx], 1).astype(np.float32))
        elif n == "convp":
            cw = inputs["conv_w"].reshape(DEPTH, 3, 44, 128)
            cb = inputs["conv_b"].reshape(DEPTH, 1, 44, 128)
            m[n] = np.ascontiguousarray(np.concatenate([cw, cb], 1).transpose(0, 3, 2, 1))
        elif n == "par":
            m[n] = np.ascontiguousarray(np.broadcast_to(np.array([h, 1 - h], np.float32)[None, :], (128, 2)))
        elif n == "na_bi":
            tabs = [na_tables(inputs["na_rpb"][l], h) for l in range(DEPTH)]
            m["na_bi"] = np.stack([t[0] for t in tabs])
            m["na_bb"] = np.stack([t[1] for t in tabs])
        elif n == "na_bb":
            pass
        else:
            m[n] = np.ascontiguousarray(inputs[n])
    return m


INPUT_NAMES = ["x", "p", "w_in", "rope", "ident", "na_bi", "na_bb", "ret_decay_f", "ret_decay_b", "rconst", "par",
               "w_branch_a", "w_branch_b", "w_out", "ln1_g", "ln1_b", "ln2_g", "ln2_b", "w_up", "convp", "w_down",
               "w_ple_gate", "w_ple_proj"]


_PROG = {}


def kernel(**inputs):
    inputs = {k: np.asarray(v) for k, v in inputs.items()}
    if "kb" not in _PROG:
        _PROG["kb"] = build_program(n_layers=DEPTH)
    kb = _PROG["kb"]
    names = [n for n in INPUT_NAMES if n in kb.dram]
    in_maps = [core_inputs(inputs, c, names) for c in range(8)]
    res = run_bass_kernel_spmd(kb.nc, in_maps, core_ids=list(range(8)))
    out = np.empty((4, 2 * NT, D), np.float32)
    for c in range(8):
        out[c // 2, (c % 2) * NT:(c % 2 + 1) * NT] = np.asarray(res.results[c]["out"], dtype=np.float32)
    return out
```
